# Optimizing a Trainium2 kernel written in Bass

```python
import math
import jax, jax.numpy as jnp
from jax import lax
import numpy as np

D_MODEL = 2048
BATCH = 4
SEQ = 8192
DEPTH = 1
DEC_BATCH = 32
DEC_SEQ = 64
PAST_LEN = 2048

CHUNK = 64
ATTN_WIDTH = D_MODEL // 2
SSM_WIDTH = D_MODEL - ATTN_WIDTH
N_HEADS = 8
HEAD_DIM = ATTN_WIDTH // (2 * N_HEADS)
V_DIM = 2 * HEAD_DIM
ROT_DIM = HEAD_DIM // 4
ROPE_THETA = 500000.0
SSM_GROUP = 16
N_SSM_GROUPS = SSM_WIDTH // SSM_GROUP
SSM_STATE = 64
IN_WIDTH = 3 * ATTN_WIDTH + SSM_WIDTH
D_FF = 4 * D_MODEL
Q_BLOCK = 128
EPS = 1e-6
NEG_INF = -1e30

kernel_name = "hymba_diffattn_s5_streaming_step"


def rms_norm(x, g):
    xf = x.astype(jnp.float32)
    y = xf * lax.rsqrt(jnp.mean(xf * xf, axis=-1, keepdims=True) + EPS)
    return (y * g.astype(jnp.float32)).astype(x.dtype)


def ada_modulation(c, w_ada, b_ada):
    m = jax.nn.silu(c) @ w_ada + b_ada
    return [t[:, None, :] for t in jnp.split(m, 6, axis=-1)]


def rope_tables(positions):
    inv = jnp.power(jnp.float32(ROPE_THETA), -jnp.arange(0, ROT_DIM, 2, dtype=jnp.float32) / ROT_DIM)
    ang = positions.astype(jnp.float32)[:, None] * inv[None, :]
    return jnp.cos(ang), jnp.sin(ang)


def apply_partial_rope(t, cos, sin):
    half = ROT_DIM // 2
    c = cos[None, :, None, None, :]
    s = sin[None, :, None, None, :]
    t1 = t[..., :half].astype(jnp.float32)
    t2 = t[..., half:ROT_DIM].astype(jnp.float32)
    rot = jnp.concatenate([t1 * c - t2 * s, t2 * c + t1 * s], axis=-1).astype(t.dtype)
    return jnp.concatenate([rot, t[..., ROT_DIM:]], axis=-1)


def diff_attend(q, k, v, lam, mask=None):
    s = jnp.einsum('bqhcd,bkhcd->bhcqk', q.astype(jnp.float32), k.astype(jnp.float32)) * (HEAD_DIM ** -0.5)
    if mask is not None:
        s = jnp.where(mask, s, NEG_INF)
    p = jax.nn.softmax(s, axis=-1)
    w = p[:, :, 0] - lam * p[:, :, 1]
    return jnp.einsum('bhqk,bkhd->bqhd', w, v.astype(jnp.float32))


def prompt_diff_attention(q, k, v, lam):
    bsz, seq = q.shape[:2]
    n_blk = seq // Q_BLOCK
    q_blocks = q.reshape(bsz, n_blk, Q_BLOCK, N_HEADS, 2, HEAD_DIM).swapaxes(0, 1)
    key_chunk = jnp.arange(seq, dtype=jnp.int32) // CHUNK

    def one_block(args):
        blk, q_blk = args
        q_chunk = (blk * Q_BLOCK + jnp.arange(Q_BLOCK, dtype=jnp.int32)) // CHUNK
        mask = key_chunk[None, :] <= q_chunk[:, None]
        return diff_attend(q_blk, k, v, lam, mask)

    o = lax.map(one_block, (jnp.arange(n_blk, dtype=jnp.int32), q_blocks))
    return o.swapaxes(0, 1).reshape(bsz, seq, N_HEADS, V_DIM)


def s5_discretize(lam_re, lam_im, log_dt, b_re, b_im):
    lam_re = lam_re.astype(jnp.float32)
    lam_im = lam_im.astype(jnp.float32)
    dt = jnp.exp(log_dt.astype(jnp.float32))[:, None]
    mag = jnp.exp(lam_re * dt)
    ar = mag * jnp.cos(lam_im * dt)
    ai = mag * jnp.sin(lam_im * dt)
    den = lam_re * lam_re + lam_im * lam_im
    f_re = ((ar - 1.0) * lam_re + ai * lam_im) / den
    f_im = (ai * lam_re - (ar - 1.0) * lam_im) / den
    b_re = b_re.astype(jnp.float32)
    b_im = b_im.astype(jnp.float32)
    bb_re = f_re[..., None] * b_re - f_im[..., None] * b_im
    bb_im = f_re[..., None] * b_im + f_im[..., None] * b_re
    return ar, ai, bb_re, bb_im


def complex_affine_combine(e1, e2):
    a1r, a1i, b1r, b1i = e1
    a2r, a2i, b2r, b2i = e2
    return (a2r * a1r - a2i * a1i,
            a2r * a1i + a2i * a1r,
            a2r * b1r - a2i * b1i + b2r,
            a2r * b1i + a2i * b1r + b2i)


def s5_block(h_re, h_im, u_blk, ar, ai, bb_re, bb_im, c_re, c_im):
    bu_re = jnp.einsum('blgh,gph->blgp', u_blk, bb_re)
    bu_im = jnp.einsum('blgh,gph->blgp', u_blk, bb_im)
    bu_re = bu_re.at[:, 0].add(ar * h_re - ai * h_im)
    bu_im = bu_im.at[:, 0].add(ar * h_im + ai * h_re)
    a_re = jnp.broadcast_to(ar, bu_re.shape)
    a_im = jnp.broadcast_to(ai, bu_im.shape)
    _, _, s_re, s_im = lax.associative_scan(complex_affine_combine, (a_re, a_im, bu_re, bu_im), axis=1)
    y = (jnp.einsum('blgp,ghp->blgh', s_re, c_re.astype(jnp.float32))
         - jnp.einsum('blgp,ghp->blgh', s_im, c_im.astype(jnp.float32)))
    return s_re[:, -1], s_im[:, -1], y


def s5_prompt_scan(u_g, h0_re, h0_im, disc, c_re, c_im):
    bsz, seq = u_g.shape[:2]
    n_chunks = seq // CHUNK
    u_c = u_g.reshape(bsz, n_chunks, CHUNK, N_SSM_GROUPS, SSM_GROUP).swapaxes(0, 1)

    def step(carry, u_blk):
        h_re, h_im, y = s5_block(carry[0], carry[1], u_blk, *disc, c_re, c_im)
        return (h_re, h_im), y

    (h_re, h_im), ys = lax.scan(step, (h0_re, h0_im), u_c)
    return h_re, h_im, ys.swapaxes(0, 1).reshape(bsz, seq, N_SSM_GROUPS, SSM_GROUP)


def hybrid_layer(x, c, positions, past_k, past_v, h0_re, h0_im, lam_init,
                 w_ada, b_ada, g_mix, w_in, lam_q1, lam_k1, lam_q2, lam_k2, g_subln,
                 ssm_lam_re, ssm_lam_im, ssm_log_dt, ssm_b_re, ssm_b_im, ssm_c_re, ssm_c_im,
                 ssm_d, w_glu, b_glu, w_out, g_ffn, w_ff1, w_ff2):
    first_chunk = past_k is None
    bsz, seq, _ = x.shape
    shift1, scale1, gate1, shift2, scale2, gate2 = ada_modulation(c, w_ada, b_ada)

    h = rms_norm(x, g_mix) * (1.0 + scale1) + shift1
    proj = h @ w_in
    q, k, v, u = jnp.split(proj, [ATTN_WIDTH, 2 * ATTN_WIDTH, 3 * ATTN_WIDTH], axis=-1)

    cos, sin = rope_tables(positions)
    q = apply_partial_rope(q.reshape(bsz, seq, N_HEADS, 2, HEAD_DIM), cos, sin)
    k = apply_partial_rope(k.reshape(bsz, seq, N_HEADS, 2, HEAD_DIM), cos, sin)
    v = v.reshape(bsz, seq, N_HEADS, V_DIM)
    lam = (jnp.exp(jnp.sum(lam_q1.astype(jnp.float32) * lam_k1.astype(jnp.float32)))
           - jnp.exp(jnp.sum(lam_q2.astype(jnp.float32) * lam_k2.astype(jnp.float32)))
           + lam_init)
    if first_chunk:
        o = prompt_diff_attention(q, k, v, lam)
    else:
        o = diff_attend(q, jnp.concatenate([past_k, k.astype(past_k.dtype)], axis=1),
                        jnp.concatenate([past_v, v.astype(past_v.dtype)], axis=1), lam)
    o = rms_norm(o, g_subln) * (1.0 - lam_init)
    o = o.reshape(bsz, seq, ATTN_WIDTH).astype(x.dtype)

    disc = s5_discretize(ssm_lam_re, ssm_lam_im, ssm_log_dt, ssm_b_re, ssm_b_im)
    u_g = u.astype(jnp.float32).reshape(bsz, seq, N_SSM_GROUPS, SSM_GROUP)
    if first_chunk:
        zeros = jnp.zeros((bsz, N_SSM_GROUPS, SSM_STATE), jnp.float32)
        h_re, h_im, y_ssm = s5_prompt_scan(u_g, zeros, zeros, disc, ssm_c_re, ssm_c_im)
    else:
        h_re, h_im, y_ssm = s5_block(h0_re.astype(jnp.float32), h0_im.astype(jnp.float32),
                                     u_g, *disc, ssm_c_re, ssm_c_im)
    y_ssm = y_ssm + ssm_d.astype(jnp.float32).reshape(N_SSM_GROUPS, SSM_GROUP) * u_g
    z = jax.nn.gelu(y_ssm.reshape(bsz, seq, SSM_WIDTH)).astype(x.dtype)
    g = z @ w_glu + b_glu
    y_glu = g[..., :SSM_WIDTH] * jax.nn.sigmoid(g[..., SSM_WIDTH:])

    mix = jnp.concatenate([o, y_glu.astype(x.dtype)], axis=-1) @ w_out
    x = x + gate1 * mix

    h2 = rms_norm(x, g_ffn) * (1.0 + scale2) + shift2
    ff = jnp.square(jax.nn.relu(h2 @ w_ff1)) @ w_ff2
    x = x + gate2 * ff
    return x, k, v, h_re, h_im


def setup_inputs(seed: int = 0) -> dict:
    key = jax.random.key(seed)
    ks = iter(jax.random.split(key, 40))

    def nrm(shape, scale):
        return scale * jax.random.normal(next(ks), shape, jnp.float32)

    G, P, Hg = N_SSM_GROUPS, SSM_STATE, SSM_GROUP
    n_idx = jnp.arange(P, dtype=jnp.float32)
    return {
        "x_prompt": nrm((BATCH, SEQ, D_MODEL), 1.0),
        "x_sample": nrm((DEC_BATCH, DEC_SEQ, D_MODEL), 1.0),
        "c_prompt": nrm((BATCH, D_MODEL), 1.0),
        "c_sample": nrm((DEC_BATCH, D_MODEL), 1.0),
        "cache_k": nrm((DEPTH, DEC_BATCH, PAST_LEN, N_HEADS, 2, HEAD_DIM), 1.0),
        "cache_v": nrm((DEPTH, DEC_BATCH, PAST_LEN, N_HEADS, V_DIM), 1.0),
        "state_ssm_re": nrm((DEPTH, DEC_BATCH, G, P), 0.1),
        "state_ssm_im": nrm((DEPTH, DEC_BATCH, G, P), 0.1),
        "w_ada": nrm((DEPTH, D_MODEL, 6 * D_MODEL), 0.2 * D_MODEL ** -0.5),
        "b_ada": nrm((DEPTH, 6 * D_MODEL), 0.02),
        "g_mix": 1.0 + nrm((DEPTH, D_MODEL), 0.02),
        "w_in": nrm((DEPTH, D_MODEL, IN_WIDTH), D_MODEL ** -0.5),
        "lam_q1": nrm((DEPTH, HEAD_DIM), 0.1),
        "lam_k1": nrm((DEPTH, HEAD_DIM), 0.1),
        "lam_q2": nrm((DEPTH, HEAD_DIM), 0.1),
        "lam_k2": nrm((DEPTH, HEAD_DIM), 0.1),
        "g_subln": 1.0 + nrm((DEPTH, V_DIM), 0.02),
        "ssm_lam_re": -0.5 + nrm((DEPTH, G, P), 0.01),
        "ssm_lam_im": math.pi * n_idx + nrm((DEPTH, G, P), 0.01),
        "ssm_log_dt": jax.random.uniform(next(ks), (DEPTH, G), jnp.float32, math.log(1e-3), math.log(1e-1)),
        "ssm_b_re": nrm((DEPTH, G, P, Hg), (2 * Hg) ** -0.5),
        "ssm_b_im": nrm((DEPTH, G, P, Hg), (2 * Hg) ** -0.5),
        "ssm_c_re": nrm((DEPTH, G, Hg, P), (2 * P) ** -0.5),
        "ssm_c_im": nrm((DEPTH, G, Hg, P), (2 * P) ** -0.5),
        "ssm_d": nrm((DEPTH, SSM_WIDTH), 1.0),
        "w_glu": nrm((DEPTH, SSM_WIDTH, 2 * SSM_WIDTH), SSM_WIDTH ** -0.5),
        "b_glu": nrm((DEPTH, 2 * SSM_WIDTH), 0.02),
        "w_out": nrm((DEPTH, D_MODEL, D_MODEL), D_MODEL ** -0.5),
        "g_ffn": 1.0 + nrm((DEPTH, D_MODEL), 0.02),
        "w_ff1": nrm((DEPTH, D_MODEL, D_FF), D_MODEL ** -0.5),
        "w_ff2": nrm((DEPTH, D_FF, D_MODEL), D_FF ** -0.5),
        "g_final": 1.0 + nrm((D_MODEL,), 0.02),
    }


def reference(x_prompt, x_sample, c_prompt, c_sample, cache_k, cache_v, state_ssm_re, state_ssm_im,
              w_ada, b_ada, g_mix, w_in, lam_q1, lam_k1, lam_q2, lam_k2, g_subln,
              ssm_lam_re, ssm_lam_im, ssm_log_dt, ssm_b_re, ssm_b_im, ssm_c_re, ssm_c_im,
              ssm_d, w_glu, b_glu, w_out, g_ffn, w_ff1, w_ff2, g_final):
    xp, xs = x_prompt, x_sample
    pos_p = jnp.arange(xp.shape[1], dtype=jnp.int32)
    pos_s = cache_k.shape[2] + jnp.arange(xs.shape[1], dtype=jnp.int32)
    kp, vp, srp, sip, ksn, vsn, srs, sis = [], [], [], [], [], [], [], []
    for l in range(DEPTH):
        lam_init = 0.8 - 0.6 * math.exp(-0.3 * l)
        params = (w_ada[l], b_ada[l], g_mix[l], w_in[l], lam_q1[l], lam_k1[l], lam_q2[l], lam_k2[l],
                  g_subln[l], ssm_lam_re[l], ssm_lam_im[l], ssm_log_dt[l], ssm_b_re[l], ssm_b_im[l],
                  ssm_c_re[l], ssm_c_im[l], ssm_d[l], w_glu[l], b_glu[l], w_out[l], g_ffn[l],
                  w_ff1[l], w_ff2[l])
        xp, k_l, v_l, hr_l, hi_l = hybrid_layer(xp, c_prompt, pos_p, None, None, None, None,
                                                lam_init, *params)
        kp.append(k_l); vp.append(v_l); srp.append(hr_l); sip.append(hi_l)
        xs, k_l, v_l, hr_l, hi_l = hybrid_layer(xs, c_sample, pos_s, cache_k[l], cache_v[l],
                                                state_ssm_re[l], state_ssm_im[l], lam_init, *params)
        ksn.append(k_l); vsn.append(v_l); srs.append(hr_l); sis.append(hi_l)
    y_prompt = rms_norm(xp, g_final)
    y_sample = rms_norm(xs, g_final)
    return (y_prompt, y_sample,
            jnp.stack(kp), jnp.stack(vp), jnp.stack(srp), jnp.stack(sip),
            jnp.stack(ksn), jnp.stack(vsn), jnp.stack(srs), jnp.stack(sis))
```

```python
import math
import bisect
import contextlib
import numpy as np
import concourse.bass as bass
import concourse.mybir as mybir
from concourse.bass_utils import run_bass_kernel_spmd

F32 = mybir.dt.float32
BF16 = mybir.dt.bfloat16
I32 = mybir.dt.int32
AF = mybir.ActivationFunctionType
ALU = mybir.AluOpType
AX = mybir.AxisListType

D = 2048
SEQ = 8192
NT = 16
TT = 512
EPS = 1e-6
LAM_INIT = 0.2
TWO_PI = 2.0 * math.pi


class Prog:
    def __init__(self, nc):
        self.nc = nc
        self.streams = {e: [] for e in ("pe", "act", "dve", "pool", "sp")}
        self.buf = {}
        self.vidx = {}
        self.dval = {}
        self.waited = {e: {} for e in self.streams}

    def _need(self, eng, deps, sk, val):
        if eng == "pe" and sk == "Epe":
            return
        if sk[0] == "D":
            val = self.dval[sk]
        if self.waited[eng].get(sk, 0) >= val:
            return
        deps[sk] = max(deps.get(sk, 0), val)

    def _collect(self, eng, reads, writes):
        deps = {}
        for k in reads:
            st = self.buf.get(k)
            if st and st[0]:
                self._need(eng, deps, *st[0])
            if st and k.startswith("ps") and k[2:].isdigit():
                for r in st[1]:
                    if r[0] != "E" + eng:
                        self._need(eng, deps, *r)
        for k in writes:
            st = self.buf.get(k)
            if st:
                if st[0]:
                    self._need(eng, deps, *st[0])
                for r in st[1]:
                    self._need(eng, deps, *r)
        for sk, v in deps.items():
            self.waited[eng][sk] = v
        return list(deps.items())

    def _commit(self, reads, writes, tok):
        for k in reads:
            st = self.buf.setdefault(k, [None, []])
            st[1].append(tok)
            if len(st[1]) > 10:
                best = {}
                for sk, v in st[1]:
                    best[sk] = max(best.get(sk, 0), v)
                st[1] = list(best.items())
        for k in writes:
            self.buf[k] = [tok, []]

    def op(self, eng, fn, reads=(), writes=()):
        fns = fn if isinstance(fn, (list, tuple)) else [fn]
        waits = self._collect(eng, reads, writes)
        sk = "E" + eng
        self.vidx[sk] = self.vidx.get(sk, 0) + 1
        tok = (sk, self.vidx[sk])
        for i, f in enumerate(fns):
            self.streams[eng].append((f, waits if i == 0 else [], tok if i == len(fns) - 1 else None))
        self._commit(reads, writes, tok)

    def dma(self, q, slot, fn, reads=(), writes=()):
        if slot == "misc":
            slot = "m_" + writes[0]
        waits = self._collect(q, reads, writes)
        sk = "D" + slot
        self.dval[sk] = self.dval.get(sk, 0) + 16
        tok = (sk, self.dval[sk])
        self.streams[q].append((fn, waits, tok))
        self._commit(reads, writes, tok)

    def emit(self):
        nc = self.nc
        fin = [(sk, v) for sk, v in self.vidx.items()] + [(sk, v) for sk, v in self.dval.items()]
        self.streams["sp"].append((None, fin, None))
        needed = {}
        for st in self.streams.values():
            for fn, waits, tok in st:
                for sk, v in waits:
                    if sk[0] == "E":
                        needed.setdefault(sk, set()).add(v)
        needed = {sk: sorted(s) for sk, s in needed.items()}

        def real(sk, v):
            if sk[0] == "D":
                return v
            return bisect.bisect_right(needed[sk], v)

        with contextlib.ExitStack() as es:
            sems = {}
            for sk in list(self.vidx) + list(self.dval):
                sems[sk] = es.enter_context(nc.semaphore(sk))
            block = es.enter_context(nc.Block())
            streams = self.streams

            def run(engine, name):
                for fn, waits, tok in streams[name]:
                    for sk, v in waits:
                        engine.wait_ge(sems[sk], real(sk, v))
                    if fn is None:
                        continue
                    ins = fn(engine)
                    if tok is not None:
                        sk, v = tok
                        if sk[0] == "D":
                            ins.then_inc(sems[sk], 16)
                        else:
                            lst = needed.get(sk, [])
                            i = bisect.bisect_left(lst, v)
                            if i < len(lst) and lst[i] == v:
                                ins.then_inc(sems[sk], 1)

            @block.tensor
            def _(e):
                run(e, "pe")

            @block.scalar
            def _(e):
                run(e, "act")

            @block.vector
            def _(e):
                run(e, "dve")

            @block.gpsimd
            def _(e):
                run(e, "pool")

            @block.sync
            def _(e):
                run(e, "sp")


class Ctx:
    pass


def build_nc(stop_after="all", debug=()):
    nc = bass.Bass("TRN2", target_bir_lowering=False)
    g = Ctx()

    def din(name, shape, dt=F32):
        return nc.dram_tensor(name, list(shape), dt, kind="ExternalInput").ap()

    def dout(name, shape, dt=F32):
        return nc.dram_tensor(name, list(shape), dt, kind="ExternalOutput").ap()

    def dscr(name, shape, dt):
        return nc.dram_tensor(name, list(shape), dt, kind="ExternalOutput" if name in debug else "Internal").ap()

    xa = din("xa", [SEQ, D]); xo = din("xo", [4096, D]); xs = din("xs", [256, D])
    cvec = din("cvec", [5, D])
    ck = din("ck", [4, 2048, 1024]); cv = din("cv", [4, 2048, 1024])
    sre0 = din("sre0", [4, 64, 64]); sim0 = din("sim0", [4, 64, 64])
    w_ada = din("w_ada", [D, 6 * D]); b_ada = din("b_ada", [1, 6 * D])
    g_mix = din("g_mix", [1, D]); w_in = din("w_in", [D, 4096])
    lq1 = din("lq1", [1, 64]); lk1 = din("lk1", [1, 64]); lq2 = din("lq2", [1, 64]); lk2 = din("lk2", [1, 64])
    g_subln = din("g_subln", [1, 128])
    lam_re = din("lam_re", [64, 64]); lam_im = din("lam_im", [64, 64]); log_dt = din("log_dt", [1, 64])
    b_re = din("b_re", [64, 64, 16]); b_im = din("b_im", [64, 64, 16])
    c_re = din("c_re", [64, 16, 64]); c_im = din("c_im", [64, 16, 64])
    ssm_d = din("ssm_d", [1, 1024])
    w_glu = din("w_glu", [1024, 2048]); b_glu = din("b_glu", [1, 2048])
    w_out = din("w_out", [D, D]); g_ffn = din("g_ffn", [1, D])
    w_ff1 = din("w_ff1", [D, 4 * D]); w_ff2 = din("w_ff2", [4 * D, D]); g_final = din("g_final", [1, D])
    flags = din("flags", [128, 4]); bmask = din("bmask", [128, 128])

    yo = dout("yo", [4096, D]); ys = dout("ys", [256, D])
    nk = dout("nk", [SEQ, 1024]); nv = dout("nv", [SEQ, 1024])
    hre = dout("hre", [64, 64]); him = dout("him", [64, 64])
    nks = dout("nks", [256, 1024]); nvs = dout("nvs", [256, 1024])
    sre = dout("sre", [4, 64, 64]); sim = dout("sim", [4, 64, 64])

    wb_in = dscr("wb_in", [D, 4096], BF16); wb_glu = dscr("wb_glu", [1024, 2048], BF16)
    wb_out = dscr("wb_out", [D, D], BF16); wb_ff1 = dscr("wb_ff1", [D, 4 * D], BF16)
    wb_ff2 = dscr("wb_ff2", [4 * D, D], BF16)
    mrow = dscr("mrow", [5, 6 * D], F32)
    KT = dscr("KT", [8, 128, SEQ], BF16); VS = dscr("VS", [SEQ, 1024], BF16)
    HSr = dscr("HSr", [NT, 128, 32, 8], F32); HSi = dscr("HSi", [NT, 128, 32, 8], F32)
    KTS = dscr("KTS", [4, 8, 128, 2048], BF16); VSS = dscr("VSS", [4, 2048, 1024], BF16)
    g.x1s = dscr("x1s", [4096 + 256, D], F32)

    outer = contextlib.ExitStack()
    with outer:
        def sbp(name, shape, dt=F32):
            return outer.enter_context(nc.sbuf_tensor(name, list(shape), dt))

        ps = [outer.enter_context(nc.psum_tensor("ps%d" % i, [128, 512], F32)) for i in range(8)]
        psk = ["ps%d" % i for i in range(8)]
        ident = sbp("ident", [128, 128], BF16); identf = sbp("identf", [128, 128], F32)
        ones_b = sbp("ones_b", [128, 128], BF16)
        cosT = sbp("cosT", [128, 32, 64]); sinT = sbp("sinT", [128, 32, 64])
        R0 = sbp("R0", [128, 32, 64]); rpow = sbp("rpow", [128, 32, 64])
        BTr = sbp("BTr", [128, 8, 128], BF16); BTi = sbp("BTi", [128, 8, 128], BF16)
        CTr = sbp("CTr", [128, 8, 128], BF16); CTi = sbp("CTi", [128, 8, 128], BF16)
        A64r = sbp("A64r", [128, 32]); A64i = sbp("A64i", [128, 32])
        E64r = sbp("E64r", [128, 32]); E64i = sbp("E64i", [128, 32])
        rcol = sbp("rcol", [128, 32])
        cosA = sbp("cosA", [128, 64, 8]); sinA = sbp("sinA", [128, 64, 8])
        cosO = sbp("cosO", [128, 32, 8]); sinO = sbp("sinO", [128, 32, 8])
        cosS = sbp("cosS", [128, 8]); sinS = sbp("sinS", [128, 8])
        flg = sbp("flg", [128, 4])
        G1 = sbp("G1", [128, 16, 5]); S1 = sbp("S1", [128, 16, 5])
        G2 = sbp("G2", [128, 16, 5]); S2 = sbp("S2", [128, 16, 5])
        lamc = sbp("lamc", [128, 2])
        dcol = sbp("dcol", [128, 8]); gsub = sbp("gsub", [128, 1])
        bglu = sbp("bglu", [128, 16])

        with contextlib.ExitStack() as sc:
            def sb(name, shape, dt=F32):
                return sc.enter_context(nc.sbuf_tensor(name, list(shape), dt))
            P = Prog(nc)

            def cast(src, dst, rows, piece, slot, key):
                for r0 in range(0, rows, piece):
                    P.dma("pool", slot, lambda e, r0=r0: e.dma_start(out=dst[r0:r0 + piece, :], in_=src[r0:r0 + piece, :]),
                          writes=[key])
            cast(w_in, wb_in, D, 256, "c_in", "wb_in")

            P.op("pool", lambda e: e.memset(identf[:], 0.0), writes=["identf"])
            P.op("pool", lambda e: e.affine_select(out=identf[:], in_=identf[:], pattern=[[-1, 128]], compare_op=ALU.not_equal,
                                                   fill=1.0, base=0, channel_multiplier=1), reads=["identf"], writes=["identf"])
            P.op("dve", lambda e: e.tensor_copy(out=ident[:], in_=identf[:]), reads=["identf"], writes=["ident"])
            P.op("dve", lambda e: e.memset(ones_b[:], 1.0), writes=["ones_b"])
            P.dma("sp", "misc", lambda e: e.dma_start(out=flg[:], in_=flags), writes=["flg"])

            ti = sb("ti", [128, 2048], I32); tf = sb("tf", [128, 2048]); ta = sb("ta", [128, 2048])

            def sin_of(out2d, ang2d, n, extra, key_out):
                a = ta[:, 0:n]; f = tf[:, 0:n]; i = ti[:, 0:n]
                P.op("dve", lambda e: e.tensor_scalar(out=a, in0=ang2d, scalar1=extra + 8 * math.pi, scalar2=None, op0=ALU.add),
                     reads=[key_out + "_ang"], writes=["ta"])
                P.op("dve", lambda e: e.tensor_scalar(out=f, in0=a, scalar1=1.0 / TWO_PI, scalar2=None, op0=ALU.mult), reads=["ta"], writes=["tf"])
                P.op("dve", lambda e: e.tensor_copy(out=i, in_=f), reads=["tf"], writes=["ti"])
                P.op("dve", lambda e: e.tensor_copy(out=f, in_=i), reads=["ti"], writes=["tf"])
                P.op("dve", lambda e: e.scalar_tensor_tensor(out=a, in0=f, scalar=-TWO_PI, in1=a, op0=ALU.mult, op1=ALU.add), reads=["tf", "ta"], writes=["ta"])
                P.op("dve", lambda e: e.tensor_scalar(out=f, in0=a, scalar1=math.pi, scalar2=-TWO_PI, op0=ALU.is_gt, op1=ALU.mult), reads=["ta"], writes=["tf"])
                P.op("dve", lambda e: e.tensor_tensor(out=a, in0=a, in1=f, op=ALU.add), reads=["ta", "tf"], writes=["ta"])
                P.op("dve", lambda e: e.tensor_scalar(out=f, in0=a, scalar1=-math.pi, scalar2=TWO_PI, op0=ALU.is_lt, op1=ALU.mult), reads=["ta"], writes=["tf"])
                P.op("dve", lambda e: e.tensor_tensor(out=a, in0=a, in1=f, op=ALU.add), reads=["ta", "tf"], writes=["ta"])
                P.op("act", lambda e: e.activation(out=out2d, in_=a, func=AF.Sin), reads=["ta"], writes=[key_out])

            inv = sb("inv", [128, 8]); posi = sb("posi", [128, 64], I32); posf = sb("posf", [128, 64])
            angR = sb("angR", [128, 512])
            for i in range(8):
                P.op("pool", lambda e, i=i: e.memset(inv[:, i:i + 1], float(np.float32(500000.0) ** np.float32(-i / 8.0))), writes=["inv"])
            P.op("pool", lambda e: e.iota(out=posi[:], pattern=[[128, 64]], base=0, channel_multiplier=1), writes=["posi"])
            P.op("dve", lambda e: e.tensor_copy(out=posf[:], in_=posi[:]), reads=["posi"], writes=["posf"])
            P.op("dve", lambda e: e.tensor_tensor(out=angR[:].rearrange("p (n i) -> p n i", i=8), in0=posf[:].unsqueeze(2).to_broadcast([128, 64, 8]),
                                                  in1=inv[:].unsqueeze(1).to_broadcast([128, 64, 8]), op=ALU.mult), reads=["posf", "inv"], writes=["cosA_ang", "sinA_ang"])
            sin_of(sinA[:].rearrange("p n i -> p (n i)"), angR[:], 512, 0.0, "sinA")
            sin_of(cosA[:].rearrange("p n i -> p (n i)"), angR[:], 512, math.pi / 2, "cosA")
            P.op("pool", lambda e: e.iota(out=posi[:, 0:32].rearrange("p (s u) -> p s u", u=4), pattern=[[1024, 8], [128, 4]], base=0, channel_multiplier=1),
                 reads=["posf"], writes=["posi"])
            P.op("dve", lambda e: e.tensor_copy(out=posf[:, 0:32], in_=posi[:, 0:32]), reads=["posi", "cosA_ang", "sinA_ang"], writes=["posf"])
            P.op("dve", lambda e: e.scalar_tensor_tensor(out=posf[:, 0:32], in0=flg[:, 0:1].to_broadcast([128, 32]), scalar=512.0, in1=posf[:, 0:32],
                                                         op0=ALU.mult, op1=ALU.add), reads=["flg", "posf"], writes=["posf"])
            P.op("dve", lambda e: e.tensor_tensor(out=angR[:, 0:256].rearrange("p (n i) -> p n i", i=8), in0=posf[:, 0:32].unsqueeze(2).to_broadcast([128, 32, 8]),
                                                  in1=inv[:].unsqueeze(1).to_broadcast([128, 32, 8]), op=ALU.mult), reads=["posf", "inv", "sinA", "cosA"], writes=["cosO_ang", "sinO_ang"])
            sin_of(sinO[:].rearrange("p n i -> p (n i)"), angR[:, 0:256], 256, 0.0, "sinO")
            sin_of(cosO[:].rearrange("p n i -> p (n i)"), angR[:, 0:256], 256, math.pi / 2, "cosO")
            P.op("pool", lambda e: e.iota(out=posi[:, 0:1], pattern=[[0, 1]], base=2048, channel_multiplier=1), reads=["posf"], writes=["posi"])
            P.op("dve", lambda e: e.tensor_copy(out=posf[:, 0:1], in_=posi[:, 0:1]), reads=["posi", "cosO_ang", "sinO_ang"], writes=["posf"])
            P.op("dve", lambda e: e.tensor_tensor(out=angR[:, 0:8], in0=posf[:, 0:1].to_broadcast([128, 8]), in1=inv[:], op=ALU.mult),
                 reads=["posf", "inv", "sinO", "cosO"], writes=["cosS_ang", "sinS_ang"])
            sin_of(sinS[:], angR[:, 0:8], 8, 0.0, "sinS")
            sin_of(cosS[:], angR[:, 0:8], 8, math.pi / 2, "cosS")

            lre = sb("lre", [128, 32]); lim = sb("lim", [128, 32]); dtb = sb("dtb", [128, 32])
            th = sb("th", [128, 32]); lr = sb("lr", [128, 32])
            T1i = sb("T1i", [128, 64], I32); T1 = sb("T1", [128, 64])
            ang3 = sb("ang3", [128, 2048])
            for q_ in range(4):
                P.dma("sp", "misc", lambda e, q_=q_: e.dma_start(out=lre[:, 8 * q_:8 * q_ + 8], in_=lam_re.rearrange("(j gl) p -> (gl p) j", gl=2)[:, 8 * q_:8 * q_ + 8], allow_slow_non_contiguous=True), writes=["lre"])
                P.dma("sp", "misc", lambda e, q_=q_: e.dma_start(out=lim[:, 8 * q_:8 * q_ + 8], in_=lam_im.rearrange("(j gl) p -> (gl p) j", gl=2)[:, 8 * q_:8 * q_ + 8], allow_slow_non_contiguous=True), writes=["lim"])
            ldt = log_dt.rearrange("o (j gl) -> o gl j", gl=2)
            for gl in range(2):
                P.dma("sp", "misc", lambda e, gl=gl: e.dma_start(out=dtb[64 * gl:64 * gl + 64, :], in_=ldt[:, gl, :].partition_broadcast(64), allow_slow_non_contiguous=True), writes=["dtb"])
            P.op("act", lambda e: e.activation(out=dtb[:], in_=dtb[:], func=AF.Exp), reads=["dtb"], writes=["dtb"])
            P.op("dve", lambda e: e.tensor_tensor(out=th[:], in0=lim[:], in1=dtb[:], op=ALU.mult), reads=["lim", "dtb"], writes=["th"])
            P.op("dve", lambda e: e.tensor_tensor(out=lr[:], in0=lre[:], in1=dtb[:], op=ALU.mult), reads=["lre", "dtb"], writes=["lr"])
            P.op("pool", lambda e: e.iota(out=T1i[:], pattern=[[1, 64]], base=1, channel_multiplier=0), writes=["T1i"])
            P.op("dve", lambda e: e.tensor_copy(out=T1[:], in_=T1i[:]), reads=["T1i"], writes=["T1"])
            a3 = ang3[:].rearrange("p (j t) -> p j t", t=64)
            P.op("dve", lambda e: e.tensor_tensor(out=a3, in0=th[:].unsqueeze(2).to_broadcast([128, 32, 64]), in1=T1[:].unsqueeze(1).to_broadcast([128, 32, 64]), op=ALU.mult),
                 reads=["th", "T1", "sinS", "cosS"], writes=["sinT_ang", "cosT_ang"])
            sin_of(sinT[:].rearrange("p j t -> p (j t)"), ang3[:], 2048, 0.0, "sinT")
            sin_of(cosT[:].rearrange("p j t -> p (j t)"), ang3[:], 2048, math.pi / 2, "cosT")
            P.op("dve", lambda e: e.tensor_tensor(out=a3, in0=lr[:].unsqueeze(2).to_broadcast([128, 32, 64]), in1=T1[:].unsqueeze(1).to_broadcast([128, 32, 64]), op=ALU.mult),
                 reads=["lr", "T1", "sinT", "cosT"], writes=["ang3"])
            P.op("act", lambda e: e.activation(out=rpow[:].rearrange("p j t -> p (j t)"), in_=ang3[:], func=AF.Exp), reads=["ang3"], writes=["rpow"])
            P.op("dve", lambda e: e.tensor_copy(out=rcol[:], in_=rpow[:, :, 0]), reads=["rpow"], writes=["rcol"])
            P.op("dve", lambda e: e.tensor_copy(out=R0[:], in_=rcol[:].unsqueeze(2).to_broadcast([128, 32, 64])), reads=["rcol"], writes=["R0"])
            P.op("dve", lambda e: e.memset(R0[:, :, 0:1], 0.0), reads=["R0"], writes=["R0"])
            P.op("dve", lambda e: e.tensor_copy(out=E64r[:], in_=cosT[:, :, 63]), reads=["cosT"], writes=["E64r"])
            P.op("dve", lambda e: e.tensor_copy(out=E64i[:], in_=sinT[:, :, 63]), reads=["sinT"], writes=["E64i"])
            P.op("dve", lambda e: e.tensor_tensor(out=A64r[:], in0=E64r[:], in1=rpow[:, :, 63], op=ALU.mult), reads=["E64r", "rpow"], writes=["A64r"])
            P.op("dve", lambda e: e.tensor_tensor(out=A64i[:], in0=E64i[:], in1=rpow[:, :, 63], op=ALU.mult), reads=["E64i", "rpow"], writes=["A64i"])
            ar = sb("ar", [128, 32]); ai = sb("ai", [128, 32]); den = sb("den", [128, 32]); t0 = sb("t0", [128, 32]); t1 = sb("t1", [128, 32])
            fr = sb("fr", [128, 32]); fi = sb("fi", [128, 32])
            P.op("dve", lambda e: e.tensor_tensor(out=ar[:], in0=cosT[:, :, 0], in1=rcol[:], op=ALU.mult), reads=["cosT", "rcol"], writes=["ar"])
            P.op("dve", lambda e: e.tensor_scalar(out=ar[:], in0=ar[:], scalar1=-1.0, scalar2=None, op0=ALU.add), reads=["ar"], writes=["ar"])
            P.op("dve", lambda e: e.tensor_tensor(out=ai[:], in0=sinT[:, :, 0], in1=rcol[:], op=ALU.mult), reads=["sinT", "rcol"], writes=["ai"])
            P.op("dve", lambda e: e.tensor_tensor(out=den[:], in0=lre[:], in1=lre[:], op=ALU.mult), reads=["lre"], writes=["den"])
            P.op("dve", lambda e: e.tensor_tensor(out=t0[:], in0=lim[:], in1=lim[:], op=ALU.mult), reads=["lim"], writes=["t0"])
            P.op("dve", lambda e: e.tensor_tensor(out=den[:], in0=den[:], in1=t0[:], op=ALU.add), reads=["den", "t0"], writes=["den"])
            P.op("dve", lambda e: e.reciprocal(out=den[:], in_=den[:]), reads=["den"], writes=["den"])
            P.op("dve", lambda e: e.tensor_tensor(out=t0[:], in0=ar[:], in1=lre[:], op=ALU.mult), reads=["ar", "lre", "den"], writes=["t0"])
            P.op("dve", lambda e: e.tensor_tensor(out=t1[:], in0=ai[:], in1=lim[:], op=ALU.mult), reads=["ai", "lim"], writes=["t1"])
            P.op("dve", lambda e: e.tensor_tensor(out=t0[:], in0=t0[:], in1=t1[:], op=ALU.add), reads=["t0", "t1"], writes=["t0"])
            P.op("dve", lambda e: e.tensor_tensor(out=fr[:], in0=t0[:], in1=den[:], op=ALU.mult), reads=["t0", "den"], writes=["fr"])
            P.op("dve", lambda e: e.tensor_tensor(out=t0[:], in0=ai[:], in1=lre[:], op=ALU.mult), reads=["ai", "lre", "fr"], writes=["t0"])
            P.op("dve", lambda e: e.tensor_tensor(out=t1[:], in0=ar[:], in1=lim[:], op=ALU.mult), reads=["ar", "lim", "t0"], writes=["t1"])
            P.op("dve", lambda e: e.tensor_tensor(out=t0[:], in0=t0[:], in1=t1[:], op=ALU.subtract), reads=["t0", "t1"], writes=["t0"])
            P.op("dve", lambda e: e.tensor_tensor(out=fi[:], in0=t0[:], in1=den[:], op=ALU.mult), reads=["t0", "den"], writes=["fi"])
            Br = sb("Br", [128, 32, 16]); Bi = sb("Bi", [128, 32, 16]); bbr = sb("bbr", [128, 32, 16]); bbi = sb("bbi", [128, 32, 16]); tb = sb("tb", [128, 32, 16])
            Xr = sb("Xr", [128, 32, 2, 16], BF16); Xi = sb("Xi", [128, 32, 2, 16], BF16)
            m01 = sb("m01", [128, 2])
            for q_ in range(4):
                P.dma("sp", "misc", lambda e, q_=q_: e.dma_start(out=Br[:, 8 * q_:8 * q_ + 8, :], in_=b_re.rearrange("(j gl) p h -> (gl p) j h", gl=2)[:, 8 * q_:8 * q_ + 8, :], allow_slow_non_contiguous=True), writes=["Br"])
                P.dma("sp", "misc", lambda e, q_=q_: e.dma_start(out=Bi[:, 8 * q_:8 * q_ + 8, :], in_=b_im.rearrange("(j gl) p h -> (gl p) j h", gl=2)[:, 8 * q_:8 * q_ + 8, :], allow_slow_non_contiguous=True), writes=["Bi"])
            P.op("pool", lambda e: e.memset(m01[:], 0.0), writes=["m01"])
            P.op("pool", lambda e: e.memset(m01[0:64, 0:1], 1.0), reads=["m01"], writes=["m01"])
            P.op("pool", lambda e: e.memset(m01[64:128, 1:2], 1.0), reads=["m01"], writes=["m01"])
            frb = fr[:].unsqueeze(2).to_broadcast([128, 32, 16]); fib = fi[:].unsqueeze(2).to_broadcast([128, 32, 16])
            P.op("dve", lambda e: e.tensor_tensor(out=bbr[:], in0=Br[:], in1=frb, op=ALU.mult), reads=["Br", "fr"], writes=["bbr"])
            P.op("dve", lambda e: e.tensor_tensor(out=tb[:], in0=Bi[:], in1=fib, op=ALU.mult), reads=["Bi", "fi"], writes=["tb"])
            P.op("dve", lambda e: e.tensor_tensor(out=bbr[:], in0=bbr[:], in1=tb[:], op=ALU.subtract), reads=["bbr", "tb"], writes=["bbr"])
            P.op("dve", lambda e: e.tensor_tensor(out=bbi[:], in0=Bi[:], in1=frb, op=ALU.mult), reads=["Bi", "fr"], writes=["bbi"])
            P.op("dve", lambda e: e.tensor_tensor(out=tb[:], in0=Br[:], in1=fib, op=ALU.mult), reads=["Br", "fi", "bbr"], writes=["tb"])
            P.op("dve", lambda e: e.tensor_tensor(out=bbi[:], in0=bbi[:], in1=tb[:], op=ALU.add), reads=["bbi", "tb"], writes=["bbi"])
            for gl in range(2):
                P.op("dve", lambda e, gl=gl: e.tensor_scalar(out=Xr[:, :, gl, :], in0=bbr[:], scalar1=m01[:, gl:gl + 1], scalar2=None, op0=ALU.mult), reads=["bbr", "m01"], writes=["Xr"])
                P.op("dve", lambda e, gl=gl: e.tensor_scalar(out=Xi[:, :, gl, :], in0=bbi[:], scalar1=m01[:, gl:gl + 1], scalar2=None, op0=ALU.mult), reads=["bbi", "m01"], writes=["Xi"])
            pb = [p_[:].bitcast(BF16) for p_ in ps]
            for (X, BT, kx, kb, pi) in ((Xr, BTr, "Xr", "BTr", 0), (Xi, BTi, "Xi", "BTi", 1)):
                Xf = X[:].rearrange("p j g h -> p (j g h)")
                P.op("pe", [lambda e, i=i, Xf=Xf, pi=pi: e.transpose(out=pb[pi][:, i * 128:(i + 1) * 128], in_=Xf[:, i * 128:(i + 1) * 128], identity=ident[:]) for i in range(8)],
                     reads=[kx, "ident"], writes=[psk[pi]])
                P.op("dve", lambda e, BT=BT, pi=pi: e.tensor_copy(out=BT[:].rearrange("p i m -> p (i m)"), in_=pb[pi][:, 0:1024]), reads=[psk[pi]], writes=[kb])
            Cr = sb("Cr", [128, 8, 64]); Ci = sb("Ci", [128, 8, 64]); bm = sb("bm", [128, 128])
            Yr = sb("Yr", [128, 8, 2, 64], BF16); Yi = sb("Yi", [128, 8, 2, 64], BF16)
            P.dma("sp", "misc", lambda e: e.dma_start(out=Cr[:], in_=c_re.rearrange("(i r) h p -> (r h) i p", r=8), allow_slow_non_contiguous=True), writes=["Cr"])
            P.dma("sp", "misc", lambda e: e.dma_start(out=Ci[:], in_=c_im.rearrange("(i r) h p -> (r h) i p", r=8), allow_slow_non_contiguous=True), writes=["Ci"])
            P.dma("sp", "misc", lambda e: e.dma_start(out=bm[:], in_=bmask), writes=["bm"])
            bmb = bm[:].rearrange("p (g q) -> p g q", g=2).unsqueeze(1).to_broadcast([128, 8, 2, 64])
            P.op("dve", lambda e: e.tensor_tensor(out=Yr[:], in0=Cr[:].unsqueeze(2).to_broadcast([128, 8, 2, 64]), in1=bmb, op=ALU.mult), reads=["Cr", "bm"], writes=["Yr"])
            P.op("dve", lambda e: e.tensor_tensor(out=Yi[:], in0=Ci[:].unsqueeze(2).to_broadcast([128, 8, 2, 64]), in1=bmb, op=ALU.mult), reads=["Ci", "bm"], writes=["Yi"])
            for (Y, CT, ky, kc, pi, sgn) in ((Yr, CTr, "Yr", "CTr", 2, 1.0), (Yi, CTi, "Yi", "CTi", 3, -1.0)):
                Yf = Y[:].rearrange("p i g q -> p (i g q)")
                P.op("pe", [lambda e, i=i, Yf=Yf, pi=pi: e.transpose(out=pb[pi][:, i * 128:(i + 1) * 128], in_=Yf[:, i * 128:(i + 1) * 128], identity=ident[:]) for i in range(8)],
                     reads=[ky, "ident"], writes=[psk[pi]])
                P.op("dve", lambda e, CT=CT, pi=pi, sgn=sgn: e.tensor_scalar(out=CT[:].rearrange("p i m -> p (i m)"), in0=pb[pi][:, 0:1024], scalar1=sgn, scalar2=None, op0=ALU.mult),
                     reads=[psk[pi]], writes=[kc])

            P.dma("sp", "misc", lambda e: e.dma_start(out=dcol[:], in_=ssm_d.rearrange("o (i p) -> p (o i)", p=128), allow_slow_non_contiguous=True), writes=["dcol"])
            P.dma("sp", "misc", lambda e: e.dma_start(out=gsub[:], in_=g_subln.rearrange("o p -> p o"), allow_slow_non_contiguous=True), writes=["gsub"])
            P.dma("sp", "misc", lambda e: e.dma_start(out=bglu[:], in_=b_glu.rearrange("o (i p) -> p (o i)", p=128), allow_slow_non_contiguous=True), writes=["bglu"])
            lt = sb("lt", [128, 4, 64]); ls = sb("ls", [128, 2])
            for n, src in enumerate((lq1, lk1, lq2, lk2)):
                P.dma("sp", "misc", lambda e, n=n, src=src: e.dma_start(out=lt[:, n, :], in_=src.partition_broadcast(128)), writes=["lt"])
            P.op("dve", lambda e: e.tensor_tensor(out=lt[:, 0, :], in0=lt[:, 0, :], in1=lt[:, 1, :], op=ALU.mult), reads=["lt"], writes=["lt"])
            P.op("dve", lambda e: e.tensor_tensor(out=lt[:, 2, :], in0=lt[:, 2, :], in1=lt[:, 3, :], op=ALU.mult), reads=["lt"], writes=["lt"])
            P.op("dve", lambda e: e.reduce_sum(out=ls[:, 0:1], in_=lt[:, 0, :], axis=AX.X), reads=["lt"], writes=["ls"])
            P.op("dve", lambda e: e.reduce_sum(out=ls[:, 1:2], in_=lt[:, 2, :], axis=AX.X), reads=["lt", "ls"], writes=["ls"])
            P.op("act", lambda e: e.activation(out=ls[:], in_=ls[:], func=AF.Exp), reads=["ls"], writes=["ls"])
            P.op("dve", lambda e: e.tensor_tensor(out=lamc[:, 0:1], in0=ls[:, 1:2], in1=ls[:, 0:1], op=ALU.subtract), reads=["ls"], writes=["lamc"])
            P.op("dve", lambda e: e.tensor_scalar(out=lamc[:, 0:1], in0=lamc[:, 0:1], scalar1=-LAM_INIT, scalar2=None, op0=ALU.add), reads=["lamc"], writes=["lamc"])
            P.op("dve", lambda e: e.memset(lamc[:, 1:2], EPS), reads=["lamc"], writes=["lamc"])

            cs = sb("cs", [5, D]); csT = sb("csT", [128, 16, 5])
            wa = [sb("wa%d" % i, [128, 16, 256]) for i in range(2)]
            ba = [sb("ba%d" % i, [5, 256]) for i in range(2)]
            mst = [sb("mst%d" % i, [5, 256]) for i in range(2)]
            P.dma("sp", "misc", lambda e: e.dma_start(out=cs[:], in_=cvec), writes=["cs"])
            sg = sb("sg", [5, D])
            P.op("act", lambda e: e.activation(out=sg[:], in_=cs[:], func=AF.Sigmoid), reads=["cs"], writes=["sg"])
            P.op("dve", lambda e: e.tensor_tensor(out=cs[:], in0=cs[:], in1=sg[:], op=ALU.mult), reads=["cs", "sg"], writes=["cs"])
            P.op("pe", [lambda e, c=c: e.transpose(out=ps[4][:, c * 5:(c + 1) * 5], in_=cs[:, c * 128:(c + 1) * 128], identity=identf[0:5, 0:5]) for c in range(16)],
                 reads=["cs", "identf"], writes=[psk[4]])
            P.op("dve", lambda e: e.tensor_copy(out=csT[:].rearrange("p c r -> p (c r)"), in_=ps[4][:, 0:80]), reads=[psk[4]], writes=["csT"])
            wav = w_ada.rearrange("(c p) n -> p c n", p=128)
            for cb in range(48):
                bi = cb % 2
                P.dma("sp", "wa%d" % bi, lambda e, cb=cb, bi=bi: e.dma_start(out=wa[bi][:], in_=wav[:, :, cb * 256:(cb + 1) * 256]), writes=["wa%d" % bi])
                P.dma("sp", "ba%d" % bi, lambda e, cb=cb, bi=bi: e.dma_start(out=ba[bi][:], in_=b_ada[:, cb * 256:(cb + 1) * 256].partition_broadcast(5)), writes=["ba%d" % bi])
                pi = 5 + bi
                P.op("pe", [lambda e, c=c, bi=bi, pi=pi: e.matmul(out=ps[pi][0:5, 0:256], lhsT=csT[:, c, :], rhs=wa[bi][:, c, :], start=(c == 0), stop=(c == 15)) for c in range(16)],
                     reads=["csT", "wa%d" % bi], writes=[psk[pi]])
                P.op("dve", lambda e, bi=bi, pi=pi: e.tensor_tensor(out=mst[bi][:], in0=ps[pi][0:5, 0:256], in1=ba[bi][:], op=ALU.add), reads=[psk[pi], "ba%d" % bi], writes=["mst%d" % bi])
                P.dma("sp", "mo%d" % bi, lambda e, cb=cb, bi=bi: e.dma_start(out=mrow[:, cb * 256:(cb + 1) * 256], in_=mst[bi][:]), reads=["mst%d" % bi], writes=["mrow"])
            gm = sb("gm", [128, 16]); gf = sb("gf", [128, 16])
            P.dma("sp", "misc", lambda e: e.dma_start(out=gm[:], in_=g_mix.rearrange("o (c p) -> p (o c)", p=128), allow_slow_non_contiguous=True), writes=["gm"])
            P.dma("sp", "misc", lambda e: e.dma_start(out=gf[:], in_=g_ffn.rearrange("o (c p) -> p (o c)", p=128), allow_slow_non_contiguous=True), writes=["gf"])
            for (Gt, St, gvec, o_shift, o_scale, kg, ks, kv) in ((G1, S1, gm, 0, D, "G1", "S1", "gm"), (G2, S2, gf, 3 * D, 4 * D, "G2", "S2", "gf")):
                for r in range(5):
                    for q_ in range(2):
                        P.dma("sp", "misc", lambda e, St=St, o=o_shift, r=r, q_=q_: e.dma_start(out=St[:, 8 * q_:8 * q_ + 8, r], in_=mrow[r:r + 1, o + 1024 * q_:o + 1024 * q_ + 1024].rearrange("r (c p) -> p (r c)", p=128), allow_slow_non_contiguous=True), reads=["mrow"], writes=[ks])
                        P.dma("sp", "misc", lambda e, Gt=Gt, o=o_scale, r=r, q_=q_: e.dma_start(out=Gt[:, 8 * q_:8 * q_ + 8, r], in_=mrow[r:r + 1, o + 1024 * q_:o + 1024 * q_ + 1024].rearrange("r (c p) -> p (r c)", p=128), allow_slow_non_contiguous=True), reads=["mrow"], writes=[kg])
                P.op("dve", lambda e, Gt=Gt: e.tensor_scalar(out=Gt[:], in0=Gt[:], scalar1=1.0, scalar2=None, op0=ALU.add), reads=[kg], writes=[kg])
                P.op("dve", lambda e, Gt=Gt, gvec=gvec: e.tensor_tensor(out=Gt[:], in0=Gt[:], in1=gvec[:].unsqueeze(2).to_broadcast([128, 16, 5]), op=ALU.mult), reads=[kg, kv], writes=[kg])
            P.emit()
        g.stop = stop_after
        if stop_after == "b0":
            return nc

        def norm_to_hT(P, sb_x, kx, nsub, psub, Gt, St, rows, hT, khT, xb_t, kxb, st_t, kst, pbank, sub0=0):
            for sub in range(nsub):
                xv = sb_x[0:psub, sub, :]
                xbv = xb_t[0:psub, :]
                P.op("dve", lambda e: e.memset(st_t[0:psub, 0:1], 0.0), writes=[kst])
                P.op("act", lambda e, xv=xv, xbv=xbv: e.activation(out=xbv, in_=xv, func=AF.Square, accum_out=st_t[0:psub, 0:1]), reads=[kx], writes=[kxb, kst])
                P.op("act", lambda e: e.activation(out=st_t[0:psub, 1:2], in_=st_t[0:psub, 0:1], func=AF.Sqrt, bias=lamc[0:psub, 1:2], scale=1.0 / D), reads=[kst, "lamc"], writes=[kst])
                P.op("dve", lambda e: e.reciprocal(out=st_t[0:psub, 2:3], in_=st_t[0:psub, 1:2]), reads=[kst], writes=[kst])
                P.op("act", lambda e, xv=xv, xbv=xbv: e.activation(out=xbv, in_=xv, func=AF.Copy, scale=st_t[0:psub, 2:3]), reads=[kx, kst], writes=[kxb])
                for half in range(2):
                    pk = pbank[half]
                    pv = ps[pk][:].bitcast(BF16)
                    P.op("pe", [lambda e, c=c, half=half, pv=pv, xbv=xbv: e.transpose(out=pv[:, (c % 8) * psub:(c % 8 + 1) * psub], in_=xbv[:, c * 128:(c + 1) * 128], identity=ident[0:psub, 0:psub])
                                for c in range(half * 8, half * 8 + 8)], reads=[kxb, "ident"], writes=[psk[pk]])
                    r = rows[sub]
                    for c in range(half * 8, half * 8 + 8):
                        P.op("dve", lambda e, c=c, pv=pv, sub=sub, r=r: e.tensor_scalar(out=hT[:, c, (sub0 + sub) * psub:(sub0 + sub + 1) * psub], in0=pv[:, (c % 8) * psub:(c % 8 + 1) * psub],
                                                                                       scalar1=Gt[:, c, r:r + 1], scalar2=St[:, c, r:r + 1], op0=ALU.mult, op1=ALU.add),
                             reads=[psk[pk], "G1", "S1", "G2", "S2"], writes=[khT])

        def rope(P, kt, kk, nh2, psub, cs_ap, sn_ap, tmp, ktmp, dst=None, kdst=None):
            t1 = kt[:, :, 0:8]; t2 = kt[:, :, 8:16]
            o1, o2, ko = (t1, t2, kk) if dst is None else (dst[:, :, 0:8], dst[:, :, 8:16], kdst)
            cb = cs_ap.unsqueeze(1).to_broadcast([psub, nh2, 8]); sbb = sn_ap.unsqueeze(1).to_broadcast([psub, nh2, 8])
            a = tmp[0:psub, 0, 0:nh2, :]; b = tmp[0:psub, 1, 0:nh2, :]; c = tmp[0:psub, 2, 0:nh2, :]; d = tmp[0:psub, 3, 0:nh2, :]
            P.op("dve", lambda e: e.tensor_tensor(out=a, in0=t1, in1=cb, op=ALU.mult), reads=[kk, "ropetab"], writes=[ktmp])
            P.op("dve", lambda e: e.tensor_tensor(out=b, in0=t2, in1=sbb, op=ALU.mult), reads=[kk], writes=[ktmp])
            P.op("dve", lambda e: e.tensor_tensor(out=c, in0=t2, in1=cb, op=ALU.mult), reads=[kk], writes=[ktmp])
            P.op("dve", lambda e: e.tensor_tensor(out=d, in0=t1, in1=sbb, op=ALU.mult), reads=[kk], writes=[ktmp])
            P.op("dve", lambda e: e.tensor_tensor(out=o1, in0=a, in1=b, op=ALU.subtract), reads=[ktmp], writes=[ko])
            P.op("dve", lambda e: e.tensor_tensor(out=o2, in0=c, in1=d, op=ALU.add), reads=[ktmp], writes=[ko])

        with contextlib.ExitStack() as sc:
            def sb(name, shape, dt=F32):
                return sc.enter_context(nc.sbuf_tensor(name, list(shape), dt))
            P = Prog(nc)
            P.buf["wb_in"] = [None, []]
            def cast(src, dst, rows, piece, slot, key):
                for r0 in range(0, rows, piece):
                    P.dma("pool", slot, lambda e, r0=r0: e.dma_start(out=dst[r0:r0 + piece, :], in_=src[r0:r0 + piece, :]), writes=[key])
            cast(w_glu, wb_glu, 1024, 256, "c_glu", "wb_glu")
            cast(w_out, wb_out, D, 256, "c_out", "wb_out")
            cast(w_ff1, wb_ff1, D, 128, "c_ff1", "wb_ff1")
            cast(w_ff2, wb_ff2, 4 * D, 512, "c_ff2", "wb_ff2")
            for b_ in range(4):
                for r0 in range(0, 2048, 1024):
                    P.dma("pool", "c_cv", lambda e, b_=b_, r0=r0: e.dma_start(out=VSS[b_, r0:r0 + 1024, :], in_=cv[b_, r0:r0 + 1024, :]), writes=["VSS"])

            NXB = 2
            xt = [sb("xt%d" % i, [128, 1, D]) for i in range(NXB)]
            xb_t = sb("xb_t", [128, D], BF16); st_t = sb("st_t", [128, 4])
            hT = sb("hT", [128, 16, TT], BF16)
            wblk = [sb("wblk%d" % i, [128, 16, 512], BF16) for i in range(2)]
            kst = sb("kst", [128, 4, 1024]); vst = sb("vst", [128, 4, 1024])
            kbf = sb("kbf", [128, 1024], BF16); vbf = sb("vbf", [128, 4, 1024], BF16)
            KTt = sb("KTt", [128, 8, TT], BF16)
            uT = sb("uT", [128, 8, TT], BF16)
            rtmp = sb("rtmp", [128, 4, 16, 8])
            wre = [sb("wre%d" % i, [128, TT]) for i in range(2)]; wim = [sb("wim%d" % i, [128, TT]) for i in range(2)]
            tA = [sb("tA0", [128, TT])] * 2; tB = [sb("tB0", [128, TT])] * 2
            R0f = [sb("R0f%d" % i, [128, TT]) for i in range(2)]
            TK = ["tA0", "tA0"]; TKB = ["tB0", "tB0"]
            gre = [sb("gre%d" % i, [128, TT]) for i in range(2)]; gim = [sb("gim%d" % i, [128, TT]) for i in range(2)]
            Lr = sb("Lr", [128, 32, 8]); Li = sb("Li", [128, 32, 8]); Lt0 = sb("Lt0", [128, 32, 8]); Lt1 = sb("Lt1", [128, 32, 8])
            Hr = sb("Hr", [128, 32, 9]); Hi = sb("Hi", [128, 32, 9]); h0 = sb("h0", [128, 32]); h1 = sb("h1", [128, 32])
            P.op("dve", lambda e: e.memset(Hr[:], 0.0), writes=["Hr"])
            P.op("dve", lambda e: e.memset(Hi[:], 0.0), writes=["Hi"])
            wv = wb_in.rearrange("(c p) n -> p c n", p=128)
            wcount = [0]

            def load_w(colblk):
                bi = wcount[0] % 2
                wcount[0] += 1
                P.dma("sp", "wblk%d" % bi, lambda e, bi=bi, colblk=colblk: e.dma_start(out=wblk[bi][:], in_=wv[:, :, colblk * 512:(colblk + 1) * 512]),
                      reads=["wb_in"], writes=["wblk%d" % bi])
                return bi

            def s5_scan_pair(P, j, T, buf, inject=None):
                i, j4 = j // 4, j % 4
                nch = T // 64
                P.op("pe", lambda e: e.matmul(out=ps[0 + 2 * buf][:, 0:T], lhsT=BTr[32 * j4:32 * j4 + 32, i, :], rhs=uT[32 * j4:32 * j4 + 32, i, 0:T], start=True, stop=True, tile_position=(32 * j4, 0)),
                     reads=["uT", "BTr"], writes=[psk[0 + 2 * buf]])
                P.op("pe", lambda e: e.matmul(out=ps[1 + 2 * buf][:, 0:T], lhsT=BTi[32 * j4:32 * j4 + 32, i, :], rhs=uT[32 * j4:32 * j4 + 32, i, 0:T], start=True, stop=True, tile_position=(32 * j4, 0)),
                     reads=["uT", "BTi"], writes=[psk[1 + 2 * buf]])
                cb = cosT[:, j, :].unsqueeze(1).to_broadcast([128, nch, 64]); sbb = sinT[:, j, :].unsqueeze(1).to_broadcast([128, nch, 64])
                v3 = lambda t: t[:, 0:T].rearrange("p (c t) -> p c t", t=64)
                bre = ps[0 + 2 * buf][:, 0:T].rearrange("p (c t) -> p c t", t=64); bim = ps[1 + 2 * buf][:, 0:T].rearrange("p (c t) -> p c t", t=64)
                kb = str(buf)
                P.op("pool", lambda e: e.tensor_copy(out=v3(R0f[buf]), in_=R0[:, j, :].unsqueeze(1).to_broadcast([128, nch, 64])), reads=["R0"], writes=["R0f" + kb])
                xf = xb_t[:].bitcast(F32)
                tCf = xf[:, 0:512]; tDf = xf[:, 512:1024]
                tC = tCf[:, 0:T].rearrange("p (c t) -> p c t", t=64); tD = tDf[:, 0:T].rearrange("p (c t) -> p c t", t=64)
                P.op("dve", lambda e: e.tensor_tensor(out=v3(tA[buf]), in0=bre, in1=cb, op=ALU.mult), reads=[psk[0 + 2 * buf], "cosT"], writes=[TK[buf]])
                P.op("dve", lambda e: e.tensor_tensor(out=tC, in0=bim, in1=sbb, op=ALU.mult), reads=[psk[1 + 2 * buf], "sinT"], writes=["xb_t"])
                P.op("dve", lambda e: e.tensor_tensor(out=v3(tB[buf]), in0=bim, in1=cb, op=ALU.mult), reads=[psk[1 + 2 * buf], "cosT"], writes=[TKB[buf]])
                P.op("dve", lambda e: e.tensor_tensor(out=tD, in0=bre, in1=sbb, op=ALU.mult), reads=[psk[0 + 2 * buf], "sinT"], writes=["xb_t"])
                P.op("dve", lambda e: e.tensor_tensor(out=wre[buf][:, 0:T], in0=tA[buf][:, 0:T], in1=tCf[:, 0:T], op=ALU.add), reads=[TK[buf], "xb_t"], writes=["wre" + kb])
                P.op("dve", lambda e: e.tensor_tensor(out=wim[buf][:, 0:T], in0=tB[buf][:, 0:T], in1=tDf[:, 0:T], op=ALU.subtract), reads=[TKB[buf], "xb_t"], writes=["wim" + kb])
                if inject is not None:
                    inject(j, buf, v3)
                P.op("dve", lambda e: e.tensor_tensor_scan(out=gre[buf][:, 0:T], data0=R0f[buf][:, 0:T], data1=wre[buf][:, 0:T], initial=0.0, op0=ALU.mult, op1=ALU.add),
                     reads=["R0f" + kb, "wre" + kb], writes=["gre" + kb])
                P.op("dve", lambda e: e.tensor_tensor_scan(out=gim[buf][:, 0:T], data0=R0f[buf][:, 0:T], data1=wim[buf][:, 0:T], initial=0.0, op0=ALU.mult, op1=ALU.add),
                     reads=["R0f" + kb, "wim" + kb], writes=["gim" + kb])
            g.s5_scan_pair = s5_scan_pair

            def cmul(P, outr, outi, ar_, ai_, br_, bi_, tmp0, tmp1, keys_in, kor, koi, kt0, kt1, eng="dve"):
                P.op(eng, lambda e: e.tensor_tensor(out=tmp0, in0=ar_, in1=br_, op=ALU.mult), reads=keys_in, writes=[kt0])
                P.op(eng, lambda e: e.tensor_tensor(out=tmp1, in0=ai_, in1=bi_, op=ALU.mult), reads=keys_in, writes=[kt1])
                P.op(eng, lambda e: e.tensor_tensor(out=outr, in0=tmp0, in1=tmp1, op=ALU.subtract), reads=[kt0, kt1], writes=[kor])
                P.op(eng, lambda e: e.tensor_tensor(out=tmp0, in0=ar_, in1=bi_, op=ALU.mult), reads=keys_in + [kor], writes=[kt0])
                P.op(eng, lambda e: e.tensor_tensor(out=tmp1, in0=ai_, in1=br_, op=ALU.mult), reads=keys_in + [kor], writes=[kt1])
                P.op(eng, lambda e: e.tensor_tensor(out=outi, in0=tmp0, in1=tmp1, op=ALU.add), reads=[kt0, kt1], writes=[koi])

            g.cmul = cmul
            for gt in range(NT):
                for sub in range(4):
                    xi = (gt * 4 + sub) % NXB
                    kx = "xt%d" % xi
                    P.dma("sp", kx, lambda e, gt=gt, xi=xi, sub=sub: e.dma_start(out=xt[xi][:, 0, :], in_=xa[gt * TT + sub * 128:gt * TT + (sub + 1) * 128, :]), writes=[kx])
                    norm_to_hT(P, xt[xi], kx, 1, 128, G1, S1, [0], hT, "hT", xb_t, "xb_t", st_t, "st_t", (4, 5), sub0=sub)
                for cbk, (dst, kd) in enumerate(((kst, "kst"), (kst, "kst"), (vst, "vst"), (vst, "vst"))):
                    bi = load_w(2 + cbk)
                    for sub in range(4):
                        pk = 4 + (cbk * 4 + sub) % 4
                        P.op("pe", [lambda e, c=c, sub=sub, bi=bi, pk=pk: e.matmul(out=ps[pk][:], lhsT=hT[:, c, sub * 128:(sub + 1) * 128], rhs=wblk[bi][:, c, :], start=(c == 0), stop=(c == 15)) for c in range(16)],
                             reads=["hT", "wblk%d" % bi], writes=[psk[pk]])
                        P.op("act", lambda e, sub=sub, pk=pk, dst=dst, cbk=cbk: e.copy(out=dst[:, sub, (cbk % 2) * 512:(cbk % 2 + 1) * 512], in_=ps[pk][:]), reads=[psk[pk]], writes=[kd])
                for ub in range(2):
                    bi = load_w(6 + ub)
                    for ii in range(4):
                        i = ub * 4 + ii
                        pk = 4 + ii
                        P.op("pe", [lambda e, c=c, ii=ii, bi=bi, pk=pk: e.matmul(out=ps[pk][:], lhsT=wblk[bi][:, c, ii * 128:(ii + 1) * 128], rhs=hT[:, c, :], start=(c == 0), stop=(c == 15)) for c in range(16)],
                             reads=["hT", "wblk%d" % bi], writes=[psk[pk]])
                        P.op("act", lambda e, i=i, pk=pk: e.copy(out=uT[:, i, :], in_=ps[pk][:]), reads=[psk[pk]], writes=["uT"])
                P.buf.setdefault("ropetab", [None, []])
                for sub in range(4):
                    n = gt * 4 + sub
                    rope(P, kst[:, sub, :].rearrange("p (h d) -> p h d", d=64), "kst", 16, 128, cosA[:, n, :], sinA[:, n, :], rtmp, "rtmp")
                P.dma("sp", "nk", lambda e, gt=gt: e.dma_start(out=nk[gt * TT:(gt + 1) * TT, :].rearrange("(s p) d -> p s d", p=128), in_=kst[:]), reads=["kst"], writes=["nk"])
                P.dma("sp", "nv", lambda e, gt=gt: e.dma_start(out=nv[gt * TT:(gt + 1) * TT, :].rearrange("(s p) d -> p s d", p=128), in_=vst[:]), reads=["vst"], writes=["nv"])
                P.op("pool", lambda e: e.tensor_copy(out=vbf[:], in_=vst[:]), reads=["vst"], writes=["vbf"])
                P.dma("sp", "VS", lambda e, gt=gt: e.dma_start(out=VS[gt * TT:(gt + 1) * TT, :].rearrange("(s p) d -> p s d", p=128), in_=vbf[:]), reads=["vbf"], writes=["VS"])
                for sub in range(4):
                    P.op("act", lambda e, sub=sub: e.copy(out=kbf[:], in_=kst[:, sub, :]), reads=["kst"], writes=["kbf"])
                    pk = 4 + sub % 2
                    pv = ps[pk][:].bitcast(BF16)
                    P.op("pe", [lambda e, h=h, pv=pv: e.transpose(out=pv[:, h * 128:(h + 1) * 128], in_=kbf[:, h * 128:(h + 1) * 128], identity=ident[:]) for h in range(8)],
                         reads=["kbf", "ident"], writes=[psk[pk]])
                    P.op("dve", lambda e, sub=sub, pv=pv: e.tensor_copy(out=KTt[:, :, sub * 128:(sub + 1) * 128], in_=pv[:, 0:1024].rearrange("p (h t) -> p h t", t=128)), reads=[psk[pk]], writes=["KTt"])
                P.dma("sp", "KT", lambda e, gt=gt: e.dma_start(out=KT[:, :, gt * TT:(gt + 1) * TT].rearrange("h p t -> p h t"), in_=KTt[:]), reads=["KTt"], writes=["KT"])
                for j in range(32):
                    buf = j % 2
                    s5_scan_pair(P, j, TT, buf)
                    kb = str(buf)
                    P.op("pool", lambda e, j=j, buf=buf: e.tensor_copy(out=Lt0[:, j, :], in_=gre[buf][:].rearrange("p (c t) -> p c t", t=64)[:, :, 63]), reads=["gre" + kb], writes=["Lt0"])
                    P.op("pool", lambda e, j=j, buf=buf: e.tensor_copy(out=Lt1[:, j, :], in_=gim[buf][:].rearrange("p (c t) -> p c t", t=64)[:, :, 63]), reads=["gim" + kb], writes=["Lt1"])
                e64r = E64r[:].unsqueeze(2).to_broadcast([128, 32, 8]); e64i = E64i[:].unsqueeze(2).to_broadcast([128, 32, 8])
                t_a = wre[0][:, 0:256].rearrange("p (j c) -> p j c", c=8); t_b = wim[0][:, 0:256].rearrange("p (j c) -> p j c", c=8)
                cmul(P, Lr[:], Li[:], Lt0[:], Lt1[:], e64r, e64i, t_a, t_b, ["Lt0", "Lt1", "E64r", "E64i"], "Lr", "Li", "wre0", "wim0")
                for c in range(8):
                    cmul(P, Hr[:, :, c + 1], Hi[:, :, c + 1], Hr[:, :, c], Hi[:, :, c], A64r[:], A64i[:], h0[:], h1[:], ["Hr", "Hi", "A64r", "A64i"], "Hr", "Hi", "h0", "h1")
                    P.op("dve", lambda e, c=c: e.tensor_tensor(out=Hr[:, :, c + 1], in0=Hr[:, :, c + 1], in1=Lr[:, :, c], op=ALU.add), reads=["Hr", "Lr"], writes=["Hr"])
                    P.op("dve", lambda e, c=c: e.tensor_tensor(out=Hi[:, :, c + 1], in0=Hi[:, :, c + 1], in1=Li[:, :, c], op=ALU.add), reads=["Hi", "Li"], writes=["Hi"])
                P.dma("sp", "HSr", lambda e, gt=gt: e.dma_start(out=HSr[gt], in_=Hr[:, :, 0:8]), reads=["Hr"], writes=["HSr"])
                P.dma("sp", "HSi", lambda e, gt=gt: e.dma_start(out=HSi[gt], in_=Hi[:, :, 0:8]), reads=["Hi"], writes=["HSi"])
                if gt < NT - 1:
                    P.op("dve", lambda e: e.tensor_copy(out=Hr[:, :, 0], in_=Hr[:, :, 8]), reads=["Hr"], writes=["Hr"])
                    P.op("dve", lambda e: e.tensor_copy(out=Hi[:, :, 0], in_=Hi[:, :, 8]), reads=["Hi"], writes=["Hi"])
            for b_ in range(4):
                for kg in range(4):
                    for kq in range(4):
                        kbi = kg * 4 + kq
                        xi = kbi % NXB
                        kx = "xt%d" % xi
                        P.dma("sp", kx, lambda e, b_=b_, kbi=kbi, xi=xi: e.dma_start(out=xt[xi][:, 0, 0:1024], in_=ck[b_, kbi * 128:(kbi + 1) * 128, :]), writes=[kx])
                        P.op("act", lambda e, xi=xi: e.copy(out=kbf[:], in_=xt[xi][:, 0, 0:1024]), reads=[kx], writes=["kbf"])
                        pk = 4 + kq % 2
                        pv = ps[pk][:].bitcast(BF16)
                        P.op("pe", [lambda e, h=h, pv=pv: e.transpose(out=pv[:, h * 128:(h + 1) * 128], in_=kbf[:, h * 128:(h + 1) * 128], identity=ident[:]) for h in range(8)],
                             reads=["kbf", "ident"], writes=[psk[pk]])
                        P.op("dve", lambda e, kq=kq, pv=pv: e.tensor_copy(out=KTt[:, :, kq * 128:(kq + 1) * 128], in_=pv[:, 0:1024].rearrange("p (h t) -> p h t", t=128)), reads=[psk[pk]], writes=["KTt"])
                    P.dma("sp", "KTS", lambda e, b_=b_, kg=kg: e.dma_start(out=KTS[b_, :, :, kg * 512:(kg + 1) * 512].rearrange("h p t -> p h t"), in_=KTt[:]), reads=["KTt"], writes=["KTS"])
            P.op("dve", lambda e: e.tensor_copy(out=h0[:], in_=Hr[:, :, 8]), reads=["Hr", "h0"], writes=["h0"])
            P.op("dve", lambda e: e.tensor_copy(out=h1[:], in_=Hi[:, :, 8]), reads=["Hi", "h1"], writes=["h1"])
            for q_ in range(4):
                P.dma("sp", "hre", lambda e, q_=q_: e.dma_start(out=hre.rearrange("(j gl) p -> (gl p) j", gl=2)[:, 8 * q_:8 * q_ + 8], in_=h0[:, 8 * q_:8 * q_ + 8], allow_slow_non_contiguous=True), reads=["h0"], writes=["hre"])
                P.dma("sp", "him", lambda e, q_=q_: e.dma_start(out=him.rearrange("(j gl) p -> (gl p) j", gl=2)[:, 8 * q_:8 * q_ + 8], in_=h1[:, 8 * q_:8 * q_ + 8], allow_slow_non_contiguous=True), reads=["h1"], writes=["him"])
            P.emit()
        if stop_after == "bA":
            return nc

        x1s = g.x1s
        NTILES = 9
        g.b1_stop = None
        g.b1_tiles = list(range(NTILES))
        if stop_after.startswith("bB1:"):
            _, st_, tl_ = stop_after.split(":")
            g.b1_stop = int(st_)
            g.b1_tiles = [int(t_) for t_ in tl_.split(",")]

        def tile_cfg(ti):
            if ti < 8:
                return dict(T=512, nsub=4, psub=128, rows=[0, 0, 0, 0], sample=False, s=ti,
                            xsrc=lambda sub, ti=ti: xo[ti * 512 + sub * 128: ti * 512 + (sub + 1) * 128, :],
                            x1=lambda sub, ti=ti: x1s[ti * 512 + sub * 128: ti * 512 + (sub + 1) * 128, :],
                            ydst=lambda sub, ti=ti: yo[ti * 512 + sub * 128: ti * 512 + (sub + 1) * 128, :],
                            cs=lambda sub, ti=ti: (cosO[:, ti * 4 + sub, :], sinO[:, ti * 4 + sub, :]))
            return dict(T=256, nsub=4, psub=64, rows=[1, 2, 3, 4], sample=True, s=None,
                        xsrc=lambda sub: xs[sub * 64:(sub + 1) * 64, :],
                        x1=lambda sub: x1s[4096 + sub * 64: 4096 + (sub + 1) * 64, :],
                        ydst=lambda sub: ys[sub * 64:(sub + 1) * 64, :],
                        cs=lambda sub: (cosS[0:64, :], sinS[0:64, :]))

        with contextlib.ExitStack() as sc:
            def sb(name, shape, dt=F32):
                return sc.enter_context(nc.sbuf_tensor("b1_" + name, list(shape), dt))
            P = Prog(nc)
            for k_ in ("wb_in", "wb_glu", "wb_out", "KT", "VS", "HSr", "HSi", "VSS", "KTS", "mrow", "ropetab"):
                P.buf[k_] = [None, []]
            xt = [sb("xt%d" % i, [128, 1, D]) for i in range(2)]
            xb_t = sb("xb_t", [128, D], BF16); st_t = sb("st_t", [128, 4])
            hT = sb("hT", [128, 16, TT], BF16)
            catT = hT
            wblk = [sb("wblk%d" % i, [128, 16, 512], BF16) for i in range(2)]
            st32 = [sb("st32_%d" % i, [128, 512]) for i in range(2)]
            stg = [sb("stg%d" % i, [128, 4, 1024], BF16) for i in range(3)]
            QT = sb("QT", [128, 8, TT], BF16); KTo = sb("KTo", [128, 8, TT], BF16)
            uTb = [sb("uTb%d" % i, [128, TT], BF16) for i in range(2)]; uTf = [sb("uTf%d" % i, [128, TT]) for i in range(2)]
            rtmp = sb("rtmp", [128, 4, 16, 8])
            wre = [sb("wre0", [128, TT])] * 2; wim = [sb("wim0", [128, TT])] * 2
            tA = sb("tA0", [128, TT]); tB = sb("tB0", [128, TT])
            R0f = [sb("R0f0", [128, TT])] * 2
            gre = [sb("gre0", [128, TT])] * 2; gim = [sb("gim0", [128, TT])] * 2
            hbr = [sb("hbr0", [128, TT], BF16)] * 2; hbi = [sb("hbi0", [128, TT], BF16)] * 2
            Ha = sb("Ha", [128, 32, 8]); Hb = sb("Hb", [128, 32, 8]); Hnr = sb("Hnr", [128, 32, 8]); Hni = sb("Hni", [128, 32, 8])
            Lt0 = sb("Lt0", [128, 32, 4]); Lt1 = sb("Lt1", [128, 32, 4]); Ler = sb("Ler", [128, 32, 4]); Lei = sb("Lei", [128, 32, 4])
            zT = stg[0][:].rearrange("p s n -> p (s n)").rearrange("p (i t) -> p i t", t=TT)
            KTb = [sb("KTb%d" % i, [128, 512], BF16) for i in range(2)]; Vb = [sb("Vb%d" % i, [128, 4, 128], BF16) for i in range(2)]
            Pb = [[sb("P%d_%d" % (c_, i), [128, 512], BF16) for i in range(2)] for c_ in range(2)]
            n0 = sb("n0", [128, TT]); n1 = sb("n1", [128, TT]); n2b = sb("n2b", [128, TT], BF16)
            brow = sb("brow", [128, 512])
            wcount = [0]

            def load_blk(src_ap, key, view=None):
                bi = wcount[0] % 2
                wcount[0] += 1
                dst = wblk[bi][:] if view is None else view(wblk[bi])
                P.dma("sp", "wblk%d" % bi, lambda e: e.dma_start(out=dst, in_=src_ap), reads=[key], writes=["wblk%d" % bi])
                return bi

            wv_in = wb_in.rearrange("(c p) n -> p c n", p=128)
            wv_out = wb_out.rearrange("(c p) n -> p c n", p=128)
            wv_glu = wb_glu.rearrange("(c p) n -> p c n", p=128)

            def s5_pair(P, j, T, buf, u2d, ukey, B_unused, inject):
                i, j4 = j // 4, j % 4
                nch_ = T // 64
                P.op("pe", lambda e: e.matmul(out=ps[6][:, 0:T], lhsT=BTr[32 * j4:32 * j4 + 32, i, :], rhs=u2d[32 * j4:32 * j4 + 32, 0:T], start=True, stop=True, tile_position=(32 * j4, 0)),
                     reads=[ukey, "BTr"], writes=[psk[6]])
                P.op("pe", lambda e: e.matmul(out=ps[7][:, 0:T], lhsT=BTi[32 * j4:32 * j4 + 32, i, :], rhs=u2d[32 * j4:32 * j4 + 32, 0:T], start=True, stop=True, tile_position=(32 * j4, 0)),
                     reads=[ukey, "BTi"], writes=[psk[7]])
                cb = cosT[:, j, :].unsqueeze(1).to_broadcast([128, nch_, 64]); sbb = sinT[:, j, :].unsqueeze(1).to_broadcast([128, nch_, 64])
                v3 = lambda t: t[:, 0:T].rearrange("p (c t) -> p c t", t=64)
                bre = ps[6][:, 0:T].rearrange("p (c t) -> p c t", t=64); bim = ps[7][:, 0:T].rearrange("p (c t) -> p c t", t=64)
                P.op("pool", lambda e: e.tensor_copy(out=v3(R0f[0]), in_=R0[:, j, :].unsqueeze(1).to_broadcast([128, nch_, 64])), reads=["R0"], writes=["R0f0"])
                xf = xb_t[:].bitcast(F32)
                tCf = xf[:, 0:512]; tDf = xf[:, 512:1024]
                tC = tCf[:, 0:T].rearrange("p (c t) -> p c t", t=64); tD = tDf[:, 0:T].rearrange("p (c t) -> p c t", t=64)
                P.op("dve", lambda e: e.tensor_tensor(out=v3(tA), in0=bre, in1=cb, op=ALU.mult), reads=[psk[6], "cosT"], writes=["tA0"])
                P.op("dve", lambda e: e.tensor_tensor(out=tC, in0=bim, in1=sbb, op=ALU.mult), reads=[psk[7], "sinT"], writes=["xb_t"])
                P.op("dve", lambda e: e.tensor_tensor(out=v3(tB), in0=bim, in1=cb, op=ALU.mult), reads=[psk[7], "cosT"], writes=["tB0"])
                P.op("dve", lambda e: e.tensor_tensor(out=tD, in0=bre, in1=sbb, op=ALU.mult), reads=[psk[6], "sinT"], writes=["xb_t"])
                P.op("dve", lambda e: e.tensor_tensor(out=wre[0][:, 0:T], in0=tA[:, 0:T], in1=tCf[:, 0:T], op=ALU.add), reads=["tA0", "xb_t"], writes=["wre0"])
                P.op("dve", lambda e: e.tensor_tensor(out=wim[0][:, 0:T], in0=tB[:, 0:T], in1=tDf[:, 0:T], op=ALU.subtract), reads=["tB0", "xb_t"], writes=["wim0"])
                inject(j, 0, v3)
                P.op("dve", lambda e: e.tensor_tensor_scan(out=gre[0][:, 0:T], data0=R0f[0][:, 0:T], data1=wre[0][:, 0:T], initial=0.0, op0=ALU.mult, op1=ALU.add),
                     reads=["R0f0", "wre0"], writes=["gre0"])
                P.op("dve", lambda e: e.tensor_tensor_scan(out=gim[0][:, 0:T], data0=R0f[0][:, 0:T], data1=wim[0][:, 0:T], initial=0.0, op0=ALU.mult, op1=ALU.add),
                     reads=["R0f0", "wim0"], writes=["gim0"])
            g.s5_pair = s5_pair

            def _tile(ti):
                cfg = tile_cfg(ti)
                T, nsub, psub, rows, sample = cfg["T"], cfg["nsub"], cfg["psub"], cfg["rows"], cfg["sample"]
                nch = T // 64
                for sub in range(nsub):
                    xi = sub % 2
                    kx = "xt%d" % xi
                    P.dma("sp", kx, lambda e, xi=xi, sub=sub, cfg=cfg: e.dma_start(out=xt[xi][0:psub, 0, :], in_=cfg["xsrc"](sub)), writes=[kx])
                    norm_to_hT(P, xt[xi], kx, 1, psub, G1, S1, [rows[sub]], hT, "hT", xb_t, "xb_t", st_t, "st_t", (4, 5), sub0=sub)
                if g.b1_stop == 1:
                    return
                for cbk in range(6):
                    bi = load_blk(wv_in[:, :, cbk * 512:(cbk + 1) * 512], "wb_in")
                    which = cbk // 2
                    for sub in range(nsub):
                        pk = 4 + (cbk * nsub + sub) % 4
                        sbi = (cbk * nsub + sub) % 2
                        ks32 = "st32_%d" % sbi
                        P.op("pe", [lambda e, c=c, sub=sub, bi=bi, pk=pk: e.matmul(out=ps[pk][0:psub, :], lhsT=hT[:, c, sub * psub:(sub + 1) * psub], rhs=wblk[bi][:, c, :], start=(c == 0), stop=(c == 15)) for c in range(16)],
                             reads=["hT", "wblk%d" % bi], writes=[psk[pk]])
                        if not sample:
                            dcol_ = stg[which][0:psub, sub, (cbk % 2) * 512:(cbk % 2 + 1) * 512]
                            P.op("act", lambda e, pk=pk, dcol_=dcol_: e.copy(out=dcol_, in_=ps[pk][0:psub, :]), reads=[psk[pk]], writes=["stg%d" % which])
                            if which < 2:
                                cs_ap, sn_ap = cfg["cs"](sub)
                                rope(P, ps[pk][0:psub, :].rearrange("p (h d) -> p h d", d=64), psk[pk], 8, psub, cs_ap[0:psub, :], sn_ap[0:psub, :], rtmp, "rtmp",
                                     dst=dcol_.rearrange("p (h d) -> p h d", d=64), kdst="stg%d" % which)
                            continue
                        P.op("act", lambda e, pk=pk, sbi=sbi: e.copy(out=st32[sbi][0:psub, :], in_=ps[pk][0:psub, :]), reads=[psk[pk]], writes=[ks32])
                        if which < 2:
                            cs_ap, sn_ap = cfg["cs"](sub)
                            rope(P, st32[sbi][0:psub, :].rearrange("p (h d) -> p h d", d=64), ks32, 8, psub, cs_ap[0:psub, :], sn_ap[0:psub, :], rtmp, "rtmp")
                        P.op("pool", lambda e, sbi=sbi, sub=sub, which=which, cbk=cbk: e.tensor_copy(out=stg[which][0:psub, sub, (cbk % 2) * 512:(cbk % 2 + 1) * 512], in_=st32[sbi][0:psub, :]),
                             reads=[ks32], writes=["stg%d" % which])
                        if sample and which >= 1:
                            dst = nks if which == 1 else nvs
                            P.dma("sp", "nksv%d" % sbi, lambda e, dst=dst, sub=sub, cbk=cbk, sbi=sbi: e.dma_start(out=dst[sub * 64:(sub + 1) * 64, (cbk % 2) * 512:(cbk % 2 + 1) * 512], in_=st32[sbi][0:64, :]),
                                  reads=[ks32], writes=["nksv%d" % which])
                if g.b1_stop == 2:
                    return
                for (src_i, dstT, kd) in ((0, QT, "QT"), (1, KTo, "KTo")):
                    for sub in range(nsub):
                        pk = 4 + sub % 2
                        pv = ps[pk][:].bitcast(BF16)
                        P.op("pe", [lambda e, h=h, pv=pv, sub=sub, src_i=src_i: e.transpose(out=pv[:, h * psub:(h + 1) * psub], in_=stg[src_i][0:psub, sub, h * 128:(h + 1) * 128], identity=ident[0:psub, 0:psub]) for h in range(8)],
                             reads=["stg%d" % src_i, "ident"], writes=[psk[pk]])
                        P.op("dve", lambda e, sub=sub, pv=pv, dstT=dstT: e.tensor_copy(out=dstT[:, :, sub * psub:(sub + 1) * psub], in_=pv[:, 0:8 * psub].rearrange("p (h t) -> p h t", t=psub)), reads=[psk[pk]], writes=[kd])
                if g.b1_stop == 3:
                    return
                if not sample:
                    s_ = cfg["s"]
                    for (HS, Hn, kn) in ((HSr, Hnr, "Hnr"), (HSi, Hni, "Hni")):
                        P.dma("sp", "Ha", lambda e, HS=HS, s_=s_: e.dma_start(out=Ha[:], in_=HS[2 * s_]), reads=["HSr"], writes=["Ha"])
                        P.dma("sp", "Hb", lambda e, HS=HS, s_=s_: e.dma_start(out=Hb[:], in_=HS[2 * s_ + 1]), reads=["HSr"], writes=["Hb"])
                        P.op("dve", lambda e, Hn=Hn: e.tensor_scalar(out=Hn[:], in0=Ha[:], scalar1=flg[:, 2:3], scalar2=None, op0=ALU.mult), reads=["Ha", "flg"], writes=[kn])
                        P.op("dve", lambda e, Hn=Hn: e.scalar_tensor_tensor(out=Hn[:], in0=Hb[:], scalar=flg[:, 0:1], in1=Hn[:], op0=ALU.mult, op1=ALU.add), reads=["Hb", "flg", kn], writes=[kn])
                else:
                    for c_ in range(4):
                        for q_ in range(4):
                            P.dma("sp", "Hnr", lambda e, c_=c_, q_=q_: e.dma_start(out=Hnr[:, 8 * q_:8 * q_ + 8, c_], in_=sre0[c_].rearrange("(j gl) p -> (gl p) j", gl=2)[:, 8 * q_:8 * q_ + 8], allow_slow_non_contiguous=True), writes=["Hnr"])
                            P.dma("sp", "Hni", lambda e, c_=c_, q_=q_: e.dma_start(out=Hni[:, 8 * q_:8 * q_ + 8, c_], in_=sim0[c_].rearrange("(j gl) p -> (gl p) j", gl=2)[:, 8 * q_:8 * q_ + 8], allow_slow_non_contiguous=True), writes=["Hni"])

                def inject(j, buf, v3, nch=nch):
                    P.op("dve", lambda e: e.scalar_tensor_tensor(out=v3(wre[buf])[:, :, 0], in0=Hnr[:, j, 0:nch], scalar=rcol[:, j:j + 1], in1=v3(wre[buf])[:, :, 0], op0=ALU.mult, op1=ALU.add),
                         reads=["Hnr", "rcol", "wre0"], writes=["wre0"])
                    P.op("dve", lambda e: e.scalar_tensor_tensor(out=v3(wim[buf])[:, :, 0], in0=Hni[:, j, 0:nch], scalar=rcol[:, j:j + 1], in1=v3(wim[buf])[:, :, 0], op0=ALU.mult, op1=ALU.add),
                         reads=["Hni", "rcol", "wim0"], writes=["wim0"])

                if g.b1_stop == 4:
                    return
                for ub in range(2):
                    bi = load_blk(wv_in[:, :, 3072 + ub * 512: 3072 + (ub + 1) * 512], "wb_in")
                    for ii in range(4):
                        i = ub * 4 + ii
                        ib = i % 2
                        P.op("pe", [lambda e, c=c, ii=ii, bi=bi: e.matmul(out=ps[4][:, 0:T], lhsT=wblk[bi][:, c, ii * 128:(ii + 1) * 128], rhs=hT[:, c, 0:T], start=(c == 0), stop=(c == 15)) for c in range(16)],
                             reads=["hT", "wblk%d" % bi], writes=[psk[4]])
                        P.op("act", lambda e, ib=ib: e.copy(out=uTb[ib][:, 0:T], in_=ps[4][:, 0:T]), reads=[psk[4]], writes=["uTb%d" % ib])
                        P.op("dve", lambda e, ib=ib: e.tensor_copy(out=uTf[ib][:, 0:T], in_=ps[4][:, 0:T]), reads=[psk[4], "uTb%d" % ib], writes=["uTf%d" % ib])
                        for j4 in range(4):
                            j = 4 * i + j4
                            buf = j % 2
                            kb = str(buf)
                            g.s5_pair(P, j, T, buf, uTb[ib], "uTb%d" % ib, dict(wre=wre, wim=wim, tA=tA, tB=tB, R0f=R0f, gre=gre, gim=gim), inject)
                            v3 = lambda t_: t_[:, 0:T].rearrange("p (c t) -> p c t", t=64)
                            cb_ = cosT[:, j, :].unsqueeze(1).to_broadcast([128, nch, 64]); sb_ = sinT[:, j, :].unsqueeze(1).to_broadcast([128, nch, 64])
                            if sample:
                                P.op("pool", lambda e, j=j, buf=buf: e.tensor_copy(out=Lt0[:, j, :], in_=v3(gre[buf])[:, :, 63]), reads=["gre0"], writes=["Lt0"])
                                P.op("pool", lambda e, j=j, buf=buf: e.tensor_copy(out=Lt1[:, j, :], in_=v3(gim[buf])[:, :, 63]), reads=["gim0"], writes=["Lt1"])
                            xf_ = xb_t[:].bitcast(F32)
                            tC_ = xf_[:, 0:T].rearrange("p (c t) -> p c t", t=64); tD_ = xf_[:, 512:512 + T].rearrange("p (c t) -> p c t", t=64)
                            P.op("dve", lambda e, buf=buf, cb_=cb_: e.tensor_tensor(out=v3(tA), in0=v3(gre[buf]), in1=cb_, op=ALU.mult), reads=["gre0", "cosT"], writes=["tA0"])
                            P.op("dve", lambda e, buf=buf, sb_=sb_, tC_=tC_: e.tensor_tensor(out=tC_, in0=v3(gim[buf]), in1=sb_, op=ALU.mult), reads=["gim0", "sinT"], writes=["xb_t"])
                            P.op("dve", lambda e, buf=buf, cb_=cb_: e.tensor_tensor(out=v3(tB), in0=v3(gim[buf]), in1=cb_, op=ALU.mult), reads=["gim0", "cosT"], writes=["tB0"])
                            P.op("dve", lambda e, buf=buf, sb_=sb_, tD_=tD_: e.tensor_tensor(out=tD_, in0=v3(gre[buf]), in1=sb_, op=ALU.mult), reads=["gre0", "sinT"], writes=["xb_t"])
                            P.op("dve", lambda e, buf=buf: e.tensor_tensor(out=hbr[buf][:, 0:T], in0=tA[:, 0:T], in1=xf_[:, 0:T], op=ALU.subtract), reads=["tA0", "xb_t"], writes=["hbr0"])
                            P.op("dve", lambda e, buf=buf: e.tensor_tensor(out=hbi[buf][:, 0:T], in0=tB[:, 0:T], in1=xf_[:, 512:512 + T], op=ALU.add), reads=["tB0", "xb_t"], writes=["hbi0"])
                            P.op("pe", [lambda e, buf=buf, i=i, j4=j4: e.matmul(out=ps[5][32 * j4:32 * j4 + 32, 0:T], lhsT=CTr[:, i, 32 * j4:32 * j4 + 32], rhs=hbr[buf][:, 0:T], start=True, stop=False, tile_position=(0, 32 * j4)),
                                        lambda e, buf=buf, i=i, j4=j4: e.matmul(out=ps[5][32 * j4:32 * j4 + 32, 0:T], lhsT=CTi[:, i, 32 * j4:32 * j4 + 32], rhs=hbi[buf][:, 0:T], start=False, stop=True, tile_position=(0, 32 * j4))],
                                 reads=["hbr0", "hbi0", "CTr", "CTi"], writes=[psk[5]])
                        P.op("dve", lambda e, ib=ib, i=i: e.scalar_tensor_tensor(out=n0[:, 0:T], in0=uTf[ib][:, 0:T], scalar=dcol[:, i:i + 1], in1=ps[5][:, 0:T], op0=ALU.mult, op1=ALU.add),
                             reads=["uTf%d" % ib, psk[5], "dcol"], writes=["n0"])
                        P.op("dve", lambda e: e.tensor_tensor(out=n1[:, 0:T], in0=n0[:, 0:T], in1=n0[:, 0:T], op=ALU.mult), reads=["n0"], writes=["n1"])
                        P.op("dve", lambda e: e.tensor_scalar(out=n1[:, 0:T], in0=n1[:, 0:T], scalar1=0.044715, scalar2=1.0, op0=ALU.mult, op1=ALU.add), reads=["n1"], writes=["n1"])
                        P.op("dve", lambda e: e.tensor_tensor(out=n1[:, 0:T], in0=n1[:, 0:T], in1=n0[:, 0:T], op=ALU.mult), reads=["n1", "n0"], writes=["n1"])
                        P.op("act", lambda e: e.activation(out=n1[:, 0:T], in_=n1[:, 0:T], func=AF.Sigmoid, scale=2.0 * math.sqrt(2.0 / math.pi)), reads=["n1"], writes=["n1"])
                        P.op("dve", lambda e, i=i: e.tensor_tensor(out=zT[:, i, 0:T], in0=n1[:, 0:T], in1=n0[:, 0:T], op=ALU.mult), reads=["n1", "n0"], writes=["stg0"])
                if sample:
                    e64r = E64r[:].unsqueeze(2).to_broadcast([128, 32, 4]); e64i = E64i[:].unsqueeze(2).to_broadcast([128, 32, 4])
                    g.cmul(P, Ler[:], Lei[:], Lt0[:], Lt1[:], e64r, e64i, Ha[:, :, 0:4], Hb[:, :, 0:4], ["Lt0", "Lt1", "E64r", "E64i"], "Ler", "Lei", "Ha", "Hb")
                    for c_ in range(4):
                        for q_ in range(4):
                            P.dma("sp", "sre", lambda e, c_=c_, q_=q_: e.dma_start(out=sre[c_].rearrange("(j gl) p -> (gl p) j", gl=2)[:, 8 * q_:8 * q_ + 8], in_=Ler[:, 8 * q_:8 * q_ + 8, c_], allow_slow_non_contiguous=True), reads=["Ler"], writes=["sre"])
                            P.dma("sp", "sim", lambda e, c_=c_, q_=q_: e.dma_start(out=sim[c_].rearrange("(j gl) p -> (gl p) j", gl=2)[:, 8 * q_:8 * q_ + 8], in_=Lei[:, 8 * q_:8 * q_ + 8, c_], allow_slow_non_contiguous=True), reads=["Lei"], writes=["sim"])

                if g.b1_stop == 5:
                    return
                acount = [0]

                def attn_p1(h, kt_ap, kkey, v_ap, vkey, nk_, q0, q1, first, bias, zero_rect, last=False):
                    sb_i = acount[0] % 2
                    acount[0] += 1
                    for comp in range(2):
                        pk = 2 * sb_i + comp
                        P.op("pe", lambda e, comp=comp, pk=pk: e.matmul(out=ps[pk][0:nk_, q0:q1], lhsT=kt_ap[64 * comp:64 * comp + 64, :], rhs=QT[64 * comp:64 * comp + 64, h, q0:q1], start=True, stop=True),
                             reads=[kkey, "QT"], writes=[psk[pk]])
                        pkey = "P%d_%d" % (comp, sb_i)
                        P.op("act", lambda e, comp=comp, pk=pk: e.activation(out=Pb[comp][sb_i][0:nk_, q0:q1], in_=ps[pk][0:nk_, q0:q1], func=AF.Exp, bias=bias, scale=0.125),
                             reads=[psk[pk], "flg"], writes=[pkey])
                        if zero_rect is not None:
                            P.op("pool", lambda e, comp=comp: e.memset(Pb[comp][sb_i][64:128, zero_rect[0]:zero_rect[1]], 0.0), reads=[pkey], writes=[pkey])
                    return (sb_i, v_ap, vkey, nk_, q0, q1, first, last)

                def attn_p2(h, ctx):
                    sb_i, v_ap, vkey, nk_, q0, q1, first, last = ctx
                    for comp in range(2):
                        pkey = "P%d_%d" % (comp, sb_i)
                        P.op("pe", [lambda e, comp=comp: e.matmul(out=ps[4 + comp][:, q0:q1], lhsT=v_ap, rhs=Pb[comp][sb_i][0:nk_, q0:q1], start=first, stop=last),
                                    lambda e, comp=comp: e.matmul(out=ps[6 + comp][:, q0:q1], lhsT=ones_b[0:nk_, :], rhs=Pb[comp][sb_i][0:nk_, q0:q1], start=first, stop=last)],
                             reads=[pkey, vkey, "ones_b"], writes=[psk[4 + comp], psk[6 + comp]])

                def attn_run(h, blocks):
                    pending = None
                    for blk in blocks:
                        ctx = attn_p1(h, *blk())
                        if pending is not None:
                            attn_p2(h, pending)
                        pending = ctx
                    attn_p2(h, pending)

                def attn_finish(h, q0, q1):
                    w = slice(q0, q1)
                    P.op("dve", lambda e: e.reciprocal(out=n0[:, w], in_=ps[6][:, w]), reads=[psk[6]], writes=["n0"])
                    P.op("dve", lambda e: e.tensor_tensor(out=n0[:, w], in0=n0[:, w], in1=ps[4][:, w], op=ALU.mult), reads=["n0", psk[4]], writes=["n0"])
                    P.op("dve", lambda e: e.reciprocal(out=n1[:, w], in_=ps[7][:, w]), reads=[psk[7]], writes=["n1"])
                    P.op("dve", lambda e: e.tensor_tensor(out=n1[:, w], in0=n1[:, w], in1=ps[5][:, w], op=ALU.mult), reads=["n1", psk[5]], writes=["n1"])
                    P.op("dve", lambda e: e.scalar_tensor_tensor(out=n0[:, w], in0=n1[:, w], scalar=lamc[:, 0:1], in1=n0[:, w], op0=ALU.mult, op1=ALU.add), reads=["n1", "n0", "lamc"], writes=["n0"])
                    P.op("pool", lambda e: e.tensor_tensor(out=n2b[:, w], in0=n0[:, w], in1=n0[:, w], op=ALU.mult), reads=["n0"], writes=["n2b"])
                    P.op("pe", lambda e: e.matmul(out=ps[6][:, w], lhsT=ones_b[:], rhs=n2b[:, w], start=True, stop=True), reads=["n2b", "ones_b"], writes=[psk[6]])
                    P.op("act", lambda e: e.activation(out=n1[:, w], in_=ps[6][:, w], func=AF.Sqrt, bias=lamc[:, 1:2], scale=1.0 / 128.0), reads=[psk[6], "lamc"], writes=["n1"])
                    P.op("dve", lambda e: e.reciprocal(out=n1[:, w], in_=n1[:, w]), reads=["n1"], writes=["n1"])
                    P.op("dve", lambda e: e.tensor_tensor(out=n0[:, w], in0=n0[:, w], in1=n1[:, w], op=ALU.mult), reads=["n0", "n1"], writes=["n0"])
                    P.op("dve", lambda e: e.tensor_scalar(out=catT[:, h, w], in0=n0[:, w], scalar1=gsub[:, 0:1], scalar2=1.0 - LAM_INIT, op0=ALU.mult, op1=ALU.mult), reads=["n0", "gsub"], writes=["hT"])

                lcount = [0]
                loaded = {}

                def load_kv(kt_src, v_src, kkey, vkey):
                    bi = lcount[0] % 2
                    lcount[0] += 1
                    P.dma("sp", "KTb%d" % bi, lambda e: e.dma_start(out=KTb[bi][:], in_=kt_src), reads=[kkey], writes=["KTb%d" % bi])
                    P.dma("sp", "Vb%d" % bi, lambda e: e.dma_start(out=Vb[bi][:], in_=v_src.rearrange("(kb p) d -> p kb d", p=128)), reads=[vkey], writes=["Vb%d" % bi])
                    return bi

                for h in range(8):
                    if not sample:
                        s_ = cfg["s"]
                        blocks = []
                        for gt_ in range(2 * s_ + 1):
                            for kb_ in range(4):
                                def mk(gt_=gt_, kb_=kb_, h=h, s_=s_, st={}):
                                    if kb_ == 0:
                                        loaded[(h, gt_)] = load_kv(KT[h, :, gt_ * 512:(gt_ + 1) * 512], VS[gt_ * 512:(gt_ + 1) * 512, h * 128:(h + 1) * 128], "KT", "VS")
                                    bi = loaded[(h, gt_)]
                                    bias = flg[:, 1:2] if gt_ == 2 * s_ else 0.0
                                    return (KTb[bi][:, kb_ * 128:(kb_ + 1) * 128], "KTb%d" % bi, Vb[bi][:, kb_, :], "Vb%d" % bi, 128, 0, 512, (gt_ == 0 and kb_ == 0), bias, None, False)
                                blocks.append(mk)
                        for kb_ in (3, 2, 1, 0):
                            blocks.append(lambda kb_=kb_, h=h: (KTo[:, h, kb_ * 128:(kb_ + 1) * 128], "KTo", stg[2][:, kb_, h * 128:(h + 1) * 128], "stg2", 128, 128 * kb_, 512, False, 0.0, (128 * kb_, 128 * kb_ + 64), kb_ == 0))
                        attn_run(h, blocks)
                        attn_finish(h, 0, 512)
                    else:
                        for b_ in range(4):
                            q0, q1 = 64 * b_, 64 * b_ + 64
                            blocks = []
                            for gt_ in range(4):
                                for kb_ in range(4):
                                    def mk(gt_=gt_, kb_=kb_, h=h, b_=b_, q0=q0, q1=q1):
                                        if kb_ == 0:
                                            loaded[(h, b_, gt_)] = load_kv(KTS[b_, h, :, gt_ * 512:(gt_ + 1) * 512], VSS[b_, gt_ * 512:(gt_ + 1) * 512, h * 128:(h + 1) * 128], "KTS", "VSS")
                                        bi = loaded[(h, b_, gt_)]
                                        return (KTb[bi][:, kb_ * 128:(kb_ + 1) * 128], "KTb%d" % bi, Vb[bi][:, kb_, :], "Vb%d" % bi, 128, q0, q1, (gt_ == 0 and kb_ == 0), 0.0, None, False)
                                    blocks.append(mk)
                            blocks.append(lambda h=h, b_=b_, q0=q0, q1=q1: (KTo[:, h, q0:q1], "KTo", stg[2][0:64, b_, h * 128:(h + 1) * 128], "stg2", 64, q0, q1, False, 0.0, None, True))
                            attn_run(h, blocks)
                        attn_finish(h, 0, 256)

                if g.b1_stop == 6:
                    return
                for half in range(2):
                    b1 = load_blk(wv_glu[:, :, half * 512:(half + 1) * 512], "wb_glu", view=lambda w_: w_[:, 0:8, :])
                    b2 = load_blk(wv_glu[:, :, 1024 + half * 512:1024 + (half + 1) * 512], "wb_glu", view=lambda w_: w_[:, 0:8, :])
                    for ii in range(4):
                        i = half * 4 + ii
                        P.op("pe", [lambda e, c=c, ii=ii, b1=b1: e.matmul(out=ps[0][:, 0:T], lhsT=wblk[b1][:, c, ii * 128:(ii + 1) * 128], rhs=zT[:, c, 0:T], start=(c == 0), stop=(c == 7)) for c in range(8)],
                             reads=["stg0", "wblk%d" % b1], writes=[psk[0]])
                        P.op("pe", [lambda e, c=c, ii=ii, b2=b2: e.matmul(out=ps[1][:, 0:T], lhsT=wblk[b2][:, c, ii * 128:(ii + 1) * 128], rhs=zT[:, c, 0:T], start=(c == 0), stop=(c == 7)) for c in range(8)],
                             reads=["stg0", "wblk%d" % b2], writes=[psk[1]])
                        P.op("act", lambda e, i=i: e.activation(out=n1[:, 0:T], in_=ps[1][:, 0:T], func=AF.Sigmoid, bias=bglu[:, 8 + i:9 + i], scale=1.0), reads=[psk[1], "bglu"], writes=["n1"])
                        P.op("dve", lambda e, i=i: e.scalar_tensor_tensor(out=catT[:, 8 + i, 0:T], in0=ps[0][:, 0:T], scalar=bglu[:, i:i + 1], in1=n1[:, 0:T], op0=ALU.add, op1=ALU.mult),
                             reads=[psk[0], "n1", "bglu"], writes=["hT"])

                if g.b1_stop == 7:
                    return
                cur_row = [None]
                for sub in range(nsub):
                    xi = sub % 2
                    kx = "xt%d" % xi
                    P.dma("sp", kx, lambda e, xi=xi, sub=sub, cfg=cfg: e.dma_start(out=xt[xi][0:psub, 0, :], in_=cfg["xsrc"](sub)), writes=[kx])
                    for cb in range(4):
                        P.dma("sp", "brow", lambda e, r=rows[sub], cb=cb: e.dma_start(out=brow[:], in_=mrow[r:r + 1, 2 * D + cb * 512:2 * D + (cb + 1) * 512].partition_broadcast(128)), reads=["mrow"], writes=["brow"])
                        bi = load_blk(wv_out[:, :, cb * 512:(cb + 1) * 512], "wb_out")
                        pk = cb % 2
                        P.op("pe", [lambda e, c=c, sub=sub, bi=bi, pk=pk: e.matmul(out=ps[pk][0:psub, :], lhsT=catT[:, c, sub * psub:(sub + 1) * psub], rhs=wblk[bi][:, c, :], start=(c == 0), stop=(c == 15)) for c in range(16)],
                             reads=["hT", "wblk%d" % bi], writes=[psk[pk]])
                        P.op("dve", lambda e, pk=pk, cb=cb: e.tensor_tensor(out=n0[0:psub, :], in0=ps[pk][0:psub, :], in1=brow[0:psub, :], op=ALU.mult), reads=[psk[pk], "brow"], writes=["n0"])
                        P.op("pool", lambda e, xi=xi, cb=cb: e.tensor_tensor(out=xt[xi][0:psub, 0, cb * 512:(cb + 1) * 512], in0=xt[xi][0:psub, 0, cb * 512:(cb + 1) * 512], in1=n0[0:psub, :], op=ALU.add), reads=["n0", kx], writes=[kx])
                    P.dma("sp", "x1s%d" % xi, lambda e, xi=xi, sub=sub, cfg=cfg: e.dma_start(out=cfg["x1"](sub), in_=xt[xi][0:psub, 0, :]), reads=[kx], writes=["x1s"])
            for ti_ in g.b1_tiles:
                _tile(ti_)
            P.emit()
        if stop_after.startswith("bB1"):
            return nc

        with contextlib.ExitStack() as sc:
            def sb(name, shape, dt=F32):
                return sc.enter_context(nc.sbuf_tensor("b2_" + name, list(shape), dt))
            P = Prog(nc)
            for k_ in ("wb_ff1", "wb_ff2", "x1s", "mrow"):
                P.buf[k_] = [None, []]
            xt = sb("xt", [128, 4, D]); xb_t = sb("xb_t", [128, D], BF16); st_t = sb("st_t", [128, 4])
            hT = sb("hT", [128, 16, TT], BF16)
            wblk = [sb("wblk%d" % i, [128, 16, 512], BF16) for i in range(2)]
            aT = sb("aT", [128, 64, TT], BF16)
            n0 = sb("n0", [128, TT]); brow = sb("brow", [128, 1024]); st_f = sb("st_f", [128, 12])
            wv1 = wb_ff1.rearrange("(c p) n -> p c n", p=128)
            wv2 = wb_ff2.rearrange("(g k p) n -> g p k n", p=128, k=8)
            wcount = [0]

            def load_blk2(src_ap, key, view=None):
                bi = wcount[0] % 2
                wcount[0] += 1
                dst = wblk[bi][:] if view is None else view(wblk[bi])
                P.dma("sp", "wblk%d" % bi, lambda e: e.dma_start(out=dst, in_=src_ap), reads=[key], writes=["wblk%d" % bi])
                return bi

            def _tile(ti):
                cfg = tile_cfg(ti)
                T, nsub, psub, rows, sample = cfg["T"], cfg["nsub"], cfg["psub"], cfg["rows"], cfg["sample"]
                for sub in range(nsub):
                    P.dma("sp", "xt", lambda e, sub=sub, cfg=cfg: e.dma_start(out=xt[0:psub, sub, :], in_=cfg["x1"](sub)), reads=["x1s"], writes=["xt"])
                norm_to_hT(P, xt, "xt", nsub, psub, G2, S2, rows, hT, "hT", xb_t, "xb_t", st_t, "st_t", (4, 5))
                for blk in range(16):
                    bi = load_blk2(wv1[:, :, blk * 512:(blk + 1) * 512], "wb_ff1")
                    for ii in range(4):
                        j = blk * 4 + ii
                        pk = ii
                        P.op("pe", [lambda e, c=c, ii=ii, bi=bi, pk=pk: e.matmul(out=ps[pk][:, 0:T], lhsT=wblk[bi][:, c, ii * 128:(ii + 1) * 128], rhs=hT[:, c, 0:T], start=(c == 0), stop=(c == 15)) for c in range(16)],
                             reads=["hT", "wblk%d" % bi], writes=[psk[pk]])
                        P.op("act", lambda e, pk=pk: e.activation(out=n0[:, 0:T], in_=ps[pk][:, 0:T], func=AF.Relu), reads=[psk[pk]], writes=["n0"])
                        P.op("dve", lambda e, j=j: e.tensor_tensor(out=aT[:, j, 0:T], in0=n0[:, 0:T], in1=n0[:, 0:T], op=ALU.mult), reads=["n0"], writes=["aT"])
                cur_row = [None]
                for rnd in range(2):
                    for kg in range(8):
                        bi = load_blk2(wv2[kg][:, :, rnd * 1024:(rnd + 1) * 1024], "wb_ff2", view=lambda w_: w_[:].rearrange("p c n -> p (c n)").rearrange("p (k n) -> p k n", k=8))
                        wvw = wblk[bi][:].rearrange("p c n -> p (c n)").rearrange("p (k n) -> p k n", k=8)
                        fns = []
                        for kk in range(8):
                            k = kg * 8 + kk
                            for sub in range(nsub):
                                for cb in range(2):
                                    fns.append(lambda e, k=k, kk=kk, sub=sub, cb=cb, wvw=wvw: e.matmul(out=ps[sub * 2 + cb][0:psub, :], lhsT=aT[:, k, sub * psub:(sub + 1) * psub], rhs=wvw[:, kk, cb * 512:(cb + 1) * 512], start=(k == 0), stop=(k == 63)))
                        P.op("pe", fns, reads=["aT", "wblk%d" % bi], writes=[psk[i_] for i_ in range(8)])
                    for sub in range(nsub):
                        if cur_row[0] != (rows[sub], rnd):
                            cur_row[0] = (rows[sub], rnd)
                            P.dma("sp", "brow", lambda e, r=rows[sub], rnd=rnd: e.dma_start(out=brow[:], in_=mrow[r:r + 1, 5 * D + rnd * 1024:5 * D + (rnd + 1) * 1024].partition_broadcast(128)), reads=["mrow"], writes=["brow"])
                        for cb in range(2):
                            col = rnd * 1024 + cb * 512
                            P.op("dve", lambda e, sub=sub, cb=cb, col=col: e.tensor_tensor(out=n0[0:psub, :], in0=ps[sub * 2 + cb][0:psub, :], in1=brow[0:psub, cb * 512:(cb + 1) * 512], op=ALU.mult), reads=[psk[sub * 2 + cb], "brow"], writes=["n0"])
                            P.op("pool", lambda e, sub=sub, col=col: e.tensor_tensor(out=xt[0:psub, sub, col:col + 512], in0=xt[0:psub, sub, col:col + 512], in1=n0[0:psub, :], op=ALU.add), reads=["n0", "xt"], writes=["xt"])
                for sub in range(nsub):
                    xv = xt[0:psub, sub, :]
                    o3 = 3 * sub
                    P.op("dve", lambda e, o3=o3: e.memset(st_f[0:psub, o3:o3 + 1], 0.0), writes=["st_f"])
                    P.op("act", lambda e, xv=xv, o3=o3: e.activation(out=xb_t[0:psub, :], in_=xv, func=AF.Square, accum_out=st_f[0:psub, o3:o3 + 1]), reads=["xt", "st_f"], writes=["xb_t", "st_f"])
                    P.op("act", lambda e, o3=o3: e.activation(out=st_f[0:psub, o3 + 1:o3 + 2], in_=st_f[0:psub, o3:o3 + 1], func=AF.Sqrt, bias=lamc[0:psub, 1:2], scale=1.0 / D), reads=["st_f", "lamc"], writes=["st_f"])
                    P.op("dve", lambda e, o3=o3: e.reciprocal(out=st_f[0:psub, o3 + 2:o3 + 3], in_=st_f[0:psub, o3 + 1:o3 + 2]), reads=["st_f"], writes=["st_f"])
                for half in range(2):
                    P.dma("sp", "brow", lambda e, half=half: e.dma_start(out=brow[:], in_=g_final[:, half * 1024:(half + 1) * 1024].partition_broadcast(128)), writes=["brow"])
                    for sub in range(nsub):
                        xh = xt[0:psub, sub, half * 1024:(half + 1) * 1024]
                        P.op("dve", lambda e, xh=xh, sub=sub: e.scalar_tensor_tensor(out=xh, in0=xh, scalar=st_f[0:psub, 3 * sub + 2:3 * sub + 3], in1=brow[0:psub, :], op0=ALU.mult, op1=ALU.mult), reads=["xt", "st_f", "brow"], writes=["xt"])
                for sub in range(nsub):
                    P.dma("sp", "yout", lambda e, sub=sub, cfg=cfg: e.dma_start(out=cfg["ydst"](sub), in_=xt[0:psub, sub, :]), reads=["xt"], writes=["yout"])
            for ti_ in range(NTILES):
                _tile(ti_)
            P.emit()
    return nc


def _prep_inputs(inp):
    f32 = lambda a: np.ascontiguousarray(np.asarray(a, dtype=np.float32))
    xp = f32(inp["x_prompt"]); xsm = f32(inp["x_sample"])
    cp = f32(inp["c_prompt"]); csm = f32(inp["c_sample"])
    ckk = f32(inp["cache_k"])[0].reshape(32, 2048, 1024); cvv = f32(inp["cache_v"])[0].reshape(32, 2048, 1024)
    sr = f32(inp["state_ssm_re"])[0]; si = f32(inp["state_ssm_im"])[0]
    shared = {
        "w_ada": f32(inp["w_ada"])[0], "b_ada": f32(inp["b_ada"]).reshape(1, -1), "g_mix": f32(inp["g_mix"]).reshape(1, -1),
        "w_in": f32(inp["w_in"])[0],
        "lq1": f32(inp["lam_q1"]).reshape(1, 64), "lk1": f32(inp["lam_k1"]).reshape(1, 64),
        "lq2": f32(inp["lam_q2"]).reshape(1, 64), "lk2": f32(inp["lam_k2"]).reshape(1, 64),
        "g_subln": f32(inp["g_subln"]).reshape(1, 128),
        "lam_re": f32(inp["ssm_lam_re"])[0], "lam_im": f32(inp["ssm_lam_im"])[0], "log_dt": f32(inp["ssm_log_dt"]).reshape(1, 64),
        "b_re": f32(inp["ssm_b_re"])[0], "b_im": f32(inp["ssm_b_im"])[0], "c_re": f32(inp["ssm_c_re"])[0], "c_im": f32(inp["ssm_c_im"])[0],
        "ssm_d": f32(inp["ssm_d"]).reshape(1, 1024), "w_glu": f32(inp["w_glu"])[0], "b_glu": f32(inp["b_glu"]).reshape(1, 2048),
        "w_out": f32(inp["w_out"])[0], "g_ffn": f32(inp["g_ffn"]).reshape(1, -1), "w_ff1": f32(inp["w_ff1"])[0], "w_ff2": f32(inp["w_ff2"])[0],
        "g_final": f32(inp["g_final"]).reshape(1, -1),
    }
    r = np.arange(128)
    bmask = ((r[:, None] % 32) // 16 == (r[None, :] // 64)).astype(np.float32)
    shared["bmask"] = bmask
    maps = []
    for c in range(8):
        b, par = c // 2, c % 2
        xa = xp[b]
        xo = np.ascontiguousarray(xa.reshape(16, 512, D)[par::2].reshape(4096, D))
        fl = np.zeros((128, 4), np.float32)
        fl[:, 0] = par; fl[:, 1] = 0.0 if par == 1 else -30000.0; fl[:, 2] = 1 - par
        m = dict(shared)
        m.update({
            "xa": xa, "xo": xo, "xs": np.ascontiguousarray(xsm[4 * c:4 * c + 4].reshape(256, D)),
            "cvec": np.ascontiguousarray(np.concatenate([cp[b:b + 1], csm[4 * c:4 * c + 4]], 0)),
            "ck": np.ascontiguousarray(ckk[4 * c:4 * c + 4]), "cv": np.ascontiguousarray(cvv[4 * c:4 * c + 4]),
            "sre0": np.ascontiguousarray(sr[4 * c:4 * c + 4]), "sim0": np.ascontiguousarray(si[4 * c:4 * c + 4]),
            "flags": fl,
        })
        maps.append(m)
    return maps


def _assemble(results):
    y_prompt = np.zeros((4, SEQ, D), np.float32); y_sample = np.zeros((32, 64, D), np.float32)
    nkp = np.zeros((1, 4, SEQ, 8, 2, 64), np.float32); nvp = np.zeros((1, 4, SEQ, 8, 128), np.float32)
    srp = np.zeros((1, 4, 64, 64), np.float32); sip = np.zeros((1, 4, 64, 64), np.float32)
    nks = np.zeros((1, 32, 64, 8, 2, 64), np.float32); nvs = np.zeros((1, 32, 64, 8, 128), np.float32)
    srs = np.zeros((1, 32, 64, 64), np.float32); sis = np.zeros((1, 32, 64, 64), np.float32)
    for c in range(8):
        r = results[c]
        b, par = c // 2, c % 2
        y_prompt[b].reshape(16, 512, D)[par::2] = np.asarray(r["yo"]).reshape(8, 512, D)
        y_sample[4 * c:4 * c + 4] = np.asarray(r["ys"]).reshape(4, 64, D)
        if par == 0:
            nkp[0, b] = np.asarray(r["nk"]).reshape(SEQ, 8, 2, 64); nvp[0, b] = np.asarray(r["nv"]).reshape(SEQ, 8, 128)
            srp[0, b] = np.asarray(r["hre"]); sip[0, b] = np.asarray(r["him"])
        nks[0, 4 * c:4 * c + 4] = np.asarray(r["nks"]).reshape(4, 64, 8, 2, 64); nvs[0, 4 * c:4 * c + 4] = np.asarray(r["nvs"]).reshape(4, 64, 8, 128)
        srs[0, 4 * c:4 * c + 4] = np.asarray(r["sre"]); sis[0, 4 * c:4 * c + 4] = np.asarray(r["sim"])
    return (y_prompt, y_sample, nkp, nvp, srp, sip, nks, nvs, srs, sis)


def kernel(**inputs):
    maps = _prep_inputs(inputs)
    nc = build_nc()
    res = run_bass_kernel_spmd(nc, maps, core_ids=list(range(8)))
    return _assemble(res.results)
```

```python
import math
import bisect
import contextlib
import numpy as np
import concourse.bass as bass
import concourse.mybir as mybir
from concourse.bass_utils import run_bass_kernel_spmd

F32 = mybir.dt.float32
BF16 = mybir.dt.bfloat16
I32 = mybir.dt.int32
AF = mybir.ActivationFunctionType
ALU = mybir.AluOpType
AX = mybir.AxisListType

D = 2048
SEQ = 8192
NT = 16
TT = 512
EPS = 1e-6
LAM_INIT = 0.2
TWO_PI = 2.0 * math.pi


class Prog:
    def __init__(self, nc):
        self.nc = nc
        self.streams = {e: [] for e in ("pe", "act", "dve", "pool", "sp")}
        self.buf = {}
        self.vidx = {}
        self.dval = {}
        self.waited = {e: {} for e in self.streams}

    def _need(self, eng, deps, sk, val):
        if eng == "pe" and sk == "Epe":
            return
        if sk[0] == "D":
            val = self.dval[sk]
        if self.waited[eng].get(sk, 0) >= val:
            return
        deps[sk] = max(deps.get(sk, 0), val)

    def _collect(self, eng, reads, writes):
        deps = {}
        for k in reads:
            st = self.buf.get(k)
            if st and st[0]:
                self._need(eng, deps, *st[0])
            if st and k.startswith("ps") and k[2:].isdigit():
                for r in st[1]:
                    if r[0] != "E" + eng:
                        self._need(eng, deps, *r)
        for k in writes:
            st = self.buf.get(k)
            if st:
                if st[0]:
                    self._need(eng, deps, *st[0])
                for r in st[1]:
                    self._need(eng, deps, *r)
        for sk, v in deps.items():
            self.waited[eng][sk] = v
        return list(deps.items())

    def _commit(self, reads, writes, tok):
        for k in reads:
            st = self.buf.setdefault(k, [None, []])
            st[1].append(tok)
            if len(st[1]) > 10:
                best = {}
                for sk, v in st[1]:
                    best[sk] = max(best.get(sk, 0), v)
                st[1] = list(best.items())
        for k in writes:
            self.buf[k] = [tok, []]

    def op(self, eng, fn, reads=(), writes=()):
        fns = fn if isinstance(fn, (list, tuple)) else [fn]
        waits = self._collect(eng, reads, writes)
        sk = "E" + eng
        self.vidx[sk] = self.vidx.get(sk, 0) + 1
        tok = (sk, self.vidx[sk])
        for i, f in enumerate(fns):
            self.streams[eng].append((f, waits if i == 0 else [], tok if i == len(fns) - 1 else None))
        self._commit(reads, writes, tok)

    def dma(self, q, slot, fn, reads=(), writes=()):
        if slot == "misc":
            slot = "m_" + writes[0]
        waits = self._collect(q, reads, writes)
        sk = "D" + slot
        self.dval[sk] = self.dval.get(sk, 0) + 16
        tok = (sk, self.dval[sk])
        self.streams[q].append((fn, waits, tok))
        self._commit(reads, writes, tok)

    def emit(self):
        nc = self.nc
        fin = [(sk, v) for sk, v in self.vidx.items()] + [(sk, v) for sk, v in self.dval.items()]
        self.streams["sp"].append((None, fin, None))
        needed = {}
        for st in self.streams.values():
            for fn, waits, tok in st:
                for sk, v in waits:
                    if sk[0] == "E":
                        needed.setdefault(sk, set()).add(v)
        needed = {sk: sorted(s) for sk, s in needed.items()}

        def real(sk, v):
            if sk[0] == "D":
                return v
            return bisect.bisect_right(needed[sk], v)

        with contextlib.ExitStack() as es:
            sems = {}
            for sk in list(self.vidx) + list(self.dval):
                sems[sk] = es.enter_context(nc.semaphore(sk))
            block = es.enter_context(nc.Block())
            streams = self.streams

            def run(engine, name):
                for fn, waits, tok in streams[name]:
                    for sk, v in waits:
                        engine.wait_ge(sems[sk], real(sk, v))
                    if fn is None:
                        continue
                    ins = fn(engine)
                    if tok is not None:
                        sk, v = tok
                        if sk[0] == "D":
                            ins.then_inc(sems[sk], 16)
                        else:
                            lst = needed.get(sk, [])
                            i = bisect.bisect_left(lst, v)
                            if i < len(lst) and lst[i] == v:
                                ins.then_inc(sems[sk], 1)

            @block.tensor
            def _(e):
                run(e, "pe")

            @block.scalar
            def _(e):
                run(e, "act")

            @block.vector
            def _(e):
                run(e, "dve")

            @block.gpsimd
            def _(e):
                run(e, "pool")

            @block.sync
            def _(e):
                run(e, "sp")


class Ctx:
    pass


def build_nc(stop_after="all", debug=()):
    nc = bass.Bass("TRN2", target_bir_lowering=False)
    g = Ctx()

    def din(name, shape, dt=F32):
        return nc.dram_tensor(name, list(shape), dt, kind="ExternalInput").ap()

    def dout(name, shape, dt=F32):
        return nc.dram_tensor(name, list(shape), dt, kind="ExternalOutput").ap()

    def dscr(name, shape, dt):
        return nc.dram_tensor(name, list(shape), dt, kind="ExternalOutput" if name in debug else "Internal").ap()

    xa = din("xa", [SEQ, D]); xo = din("xo", [4096, D]); xs = din("xs", [256, D])
    cvec = din("cvec", [5, D])
    ck = din("ck", [4, 2048, 1024]); cv = din("cv", [4, 2048, 1024])
    sre0 = din("sre0", [4, 64, 64]); sim0 = din("sim0", [4, 64, 64])
    w_ada = din("w_ada", [D, 6 * D]); b_ada = din("b_ada", [1, 6 * D])
    g_mix = din("g_mix", [1, D]); w_in = din("w_in", [D, 4096])
    lq1 = din("lq1", [1, 64]); lk1 = din("lk1", [1, 64]); lq2 = din("lq2", [1, 64]); lk2 = din("lk2", [1, 64])
    g_subln = din("g_subln", [1, 128])
    lam_re = din("lam_re", [64, 64]); lam_im = din("lam_im", [64, 64]); log_dt = din("log_dt", [1, 64])
    b_re = din("b_re", [64, 64, 16]); b_im = din("b_im", [64, 64, 16])
    c_re = din("c_re", [64, 16, 64]); c_im = din("c_im", [64, 16, 64])
    ssm_d = din("ssm_d", [1, 1024])
    w_glu = din("w_glu", [1024, 2048]); b_glu = din("b_glu", [1, 2048])
    w_out = din("w_out", [D, D]); g_ffn = din("g_ffn", [1, D])
    w_ff1 = din("w_ff1", [D, 4 * D]); w_ff2 = din("w_ff2", [4 * D, D]); g_final = din("g_final", [1, D])
    flags = din("flags", [128, 4]); bmask = din("bmask", [128, 128])

    yo = dout("yo", [4096, D]); ys = dout("ys", [256, D])
    nk = dout("nk", [SEQ, 1024]); nv = dout("nv", [SEQ, 1024])
    hre = dout("hre", [64, 64]); him = dout("him", [64, 64])
    nks = dout("nks", [256, 1024]); nvs = dout("nvs", [256, 1024])
    sre = dout("sre", [4, 64, 64]); sim = dout("sim", [4, 64, 64])

    wb_in = dscr("wb_in", [D, 4096], BF16); wb_glu = dscr("wb_glu", [1024, 2048], BF16)
    wb_out = dscr("wb_out", [D, D], BF16); wb_ff1 = dscr("wb_ff1", [D, 4 * D], BF16)
    wb_ff2 = dscr("wb_ff2", [4 * D, D], BF16)
    mrow = dscr("mrow", [5, 6 * D], F32)
    KT = dscr("KT", [8, 128, SEQ], BF16); VS = dscr("VS", [SEQ, 1024], BF16)
    HSr = dscr("HSr", [NT, 128, 32, 8], F32); HSi = dscr("HSi", [NT, 128, 32, 8], F32)
    KTS = dscr("KTS", [4, 8, 128, 2048], BF16); VSS = dscr("VSS", [4, 2048, 1024], BF16)
    g.x1s = dscr("x1s", [4096 + 256, D], F32)

    outer = contextlib.ExitStack()
    with outer:
        def sbp(name, shape, dt=F32):
            return outer.enter_context(nc.sbuf_tensor(name, list(shape), dt))

        ps = [outer.enter_context(nc.psum_tensor("ps%d" % i, [128, 512], F32)) for i in range(8)]
        psk = ["ps%d" % i for i in range(8)]
        ident = sbp("ident", [128, 128], BF16); identf = sbp("identf", [128, 128], F32)
        ones_b = sbp("ones_b", [128, 128], BF16)
        cosT = sbp("cosT", [128, 32, 64]); sinT = sbp("sinT", [128, 32, 64])
        R0 = sbp("R0", [128, 32, 64]); rpow = sbp("rpow", [128, 32, 64])
        BTr = sbp("BTr", [128, 8, 128], BF16); BTi = sbp("BTi", [128, 8, 128], BF16)
        CTr = sbp("CTr", [128, 8, 128], BF16); CTi = sbp("CTi", [128, 8, 128], BF16)
        A64r = sbp("A64r", [128, 32]); A64i = sbp("A64i", [128, 32])
        E64r = sbp("E64r", [128, 32]); E64i = sbp("E64i", [128, 32])
        rcol = sbp("rcol", [128, 32])
        cosA = sbp("cosA", [128, 64, 8]); sinA = sbp("sinA", [128, 64, 8])
        cosO = sbp("cosO", [128, 32, 8]); sinO = sbp("sinO", [128, 32, 8])
        cosS = sbp("cosS", [128, 8]); sinS = sbp("sinS", [128, 8])
        flg = sbp("flg", [128, 4])
        G1 = sbp("G1", [128, 16, 5]); S1 = sbp("S1", [128, 16, 5])
        G2 = sbp("G2", [128, 16, 5]); S2 = sbp("S2", [128, 16, 5])
        lamc = sbp("lamc", [128, 2])
        dcol = sbp("dcol", [128, 8]); gsub = sbp("gsub", [128, 1])
        bglu = sbp("bglu", [128, 16])

        with contextlib.ExitStack() as sc:
            def sb(name, shape, dt=F32):
                return sc.enter_context(nc.sbuf_tensor(name, list(shape), dt))
            P = Prog(nc)

            def cast(src, dst, rows, piece, slot, key):
                for r0 in range(0, rows, piece):
                    P.dma("pool", slot, lambda e, r0=r0: e.dma_start(out=dst[r0:r0 + piece, :], in_=src[r0:r0 + piece, :]),
                          writes=[key])
            cast(w_in, wb_in, D, 256, "c_in", "wb_in")

            P.op("pool", lambda e: e.memset(identf[:], 0.0), writes=["identf"])
            P.op("pool", lambda e: e.affine_select(out=identf[:], in_=identf[:], pattern=[[-1, 128]], compare_op=ALU.not_equal,
                                                   fill=1.0, base=0, channel_multiplier=1), reads=["identf"], writes=["identf"])
            P.op("dve", lambda e: e.tensor_copy(out=ident[:], in_=identf[:]), reads=["identf"], writes=["ident"])
            P.op("dve", lambda e: e.memset(ones_b[:], 1.0), writes=["ones_b"])
            P.dma("sp", "misc", lambda e: e.dma_start(out=flg[:], in_=flags), writes=["flg"])

            ti = sb("ti", [128, 2048], I32); tf = sb("tf", [128, 2048]); ta = sb("ta", [128, 2048])

            def sin_of(out2d, ang2d, n, extra, key_out):
                a = ta[:, 0:n]; f = tf[:, 0:n]; i = ti[:, 0:n]
                P.op("dve", lambda e: e.tensor_scalar(out=a, in0=ang2d, scalar1=extra + 8 * math.pi, scalar2=None, op0=ALU.add),
                     reads=[key_out + "_ang"], writes=["ta"])
                P.op("dve", lambda e: e.tensor_scalar(out=f, in0=a, scalar1=1.0 / TWO_PI, scalar2=None, op0=ALU.mult), reads=["ta"], writes=["tf"])
                P.op("dve", lambda e: e.tensor_copy(out=i, in_=f), reads=["tf"], writes=["ti"])
                P.op("dve", lambda e: e.tensor_copy(out=f, in_=i), reads=["ti"], writes=["tf"])
                P.op("dve", lambda e: e.scalar_tensor_tensor(out=a, in0=f, scalar=-TWO_PI, in1=a, op0=ALU.mult, op1=ALU.add), reads=["tf", "ta"], writes=["ta"])
                P.op("dve", lambda e: e.tensor_scalar(out=f, in0=a, scalar1=math.pi, scalar2=-TWO_PI, op0=ALU.is_gt, op1=ALU.mult), reads=["ta"], writes=["tf"])
                P.op("dve", lambda e: e.tensor_tensor(out=a, in0=a, in1=f, op=ALU.add), reads=["ta", "tf"], writes=["ta"])
                P.op("dve", lambda e: e.tensor_scalar(out=f, in0=a, scalar1=-math.pi, scalar2=TWO_PI, op0=ALU.is_lt, op1=ALU.mult), reads=["ta"], writes=["tf"])
                P.op("dve", lambda e: e.tensor_tensor(out=a, in0=a, in1=f, op=ALU.add), reads=["ta", "tf"], writes=["ta"])
                P.op("act", lambda e: e.activation(out=out2d, in_=a, func=AF.Sin), reads=["ta"], writes=[key_out])

            inv = sb("inv", [128, 8]); posi = sb("posi", [128, 64], I32); posf = sb("posf", [128, 64])
            angR = sb("angR", [128, 512])
            for i in range(8):
                P.op("pool", lambda e, i=i: e.memset(inv[:, i:i + 1], float(np.float32(500000.0) ** np.float32(-i / 8.0))), writes=["inv"])
            P.op("pool", lambda e: e.iota(out=posi[:], pattern=[[128, 64]], base=0, channel_multiplier=1), writes=["posi"])
            P.op("dve", lambda e: e.tensor_copy(out=posf[:], in_=posi[:]), reads=["posi"], writes=["posf"])
            P.op("dve", lambda e: e.tensor_tensor(out=angR[:].rearrange("p (n i) -> p n i", i=8), in0=posf[:].unsqueeze(2).to_broadcast([128, 64, 8]),
                                                  in1=inv[:].unsqueeze(1).to_broadcast([128, 64, 8]), op=ALU.mult), reads=["posf", "inv"], writes=["cosA_ang", "sinA_ang"])
            sin_of(sinA[:].rearrange("p n i -> p (n i)"), angR[:], 512, 0.0, "sinA")
            sin_of(cosA[:].rearrange("p n i -> p (n i)"), angR[:], 512, math.pi / 2, "cosA")
            P.op("pool", lambda e: e.iota(out=posi[:, 0:32].rearrange("p (s u) -> p s u", u=4), pattern=[[1024, 8], [128, 4]], base=0, channel_multiplier=1),
                 reads=["posf"], writes=["posi"])
            P.op("dve", lambda e: e.tensor_copy(out=posf[:, 0:32], in_=posi[:, 0:32]), reads=["posi", "cosA_ang", "sinA_ang"], writes=["posf"])
            P.op("dve", lambda e: e.scalar_tensor_tensor(out=posf[:, 0:32], in0=flg[:, 0:1].to_broadcast([128, 32]), scalar=512.0, in1=posf[:, 0:32],
                                                         op0=ALU.mult, op1=ALU.add), reads=["flg", "posf"], writes=["posf"])
            P.op("dve", lambda e: e.tensor_tensor(out=angR[:, 0:256].rearrange("p (n i) -> p n i", i=8), in0=posf[:, 0:32].unsqueeze(2).to_broadcast([128, 32, 8]),
                                                  in1=inv[:].unsqueeze(1).to_broadcast([128, 32, 8]), op=ALU.mult), reads=["posf", "inv", "sinA", "cosA"], writes=["cosO_ang", "sinO_ang"])
            sin_of(sinO[:].rearrange("p n i -> p (n i)"), angR[:, 0:256], 256, 0.0, "sinO")
            sin_of(cosO[:].rearrange("p n i -> p (n i)"), angR[:, 0:256], 256, math.pi / 2, "cosO")
            P.op("pool", lambda e: e.iota(out=posi[:, 0:1], pattern=[[0, 1]], base=2048, channel_multiplier=1), reads=["posf"], writes=["posi"])
            P.op("dve", lambda e: e.tensor_copy(out=posf[:, 0:1], in_=posi[:, 0:1]), reads=["posi", "cosO_ang", "sinO_ang"], writes=["posf"])
            P.op("dve", lambda e: e.tensor_tensor(out=angR[:, 0:8], in0=posf[:, 0:1].to_broadcast([128, 8]), in1=inv[:], op=ALU.mult),
                 reads=["posf", "inv", "sinO", "cosO"], writes=["cosS_ang", "sinS_ang"])
            sin_of(sinS[:], angR[:, 0:8], 8, 0.0, "sinS")
            sin_of(cosS[:], angR[:, 0:8], 8, math.pi / 2, "cosS")

            lre = sb("lre", [128, 32]); lim = sb("lim", [128, 32]); dtb = sb("dtb", [128, 32])
            th = sb("th", [128, 32]); lr = sb("lr", [128, 32])
            T1i = sb("T1i", [128, 64], I32); T1 = sb("T1", [128, 64])
            ang3 = sb("ang3", [128, 2048])
            for q_ in range(4):
                P.dma("sp", "misc", lambda e, q_=q_: e.dma_start(out=lre[:, 8 * q_:8 * q_ + 8], in_=lam_re.rearrange("(j gl) p -> (gl p) j", gl=2)[:, 8 * q_:8 * q_ + 8], allow_slow_non_contiguous=True), writes=["lre"])
                P.dma("sp", "misc", lambda e, q_=q_: e.dma_start(out=lim[:, 8 * q_:8 * q_ + 8], in_=lam_im.rearrange("(j gl) p -> (gl p) j", gl=2)[:, 8 * q_:8 * q_ + 8], allow_slow_non_contiguous=True), writes=["lim"])
            ldt = log_dt.rearrange("o (j gl) -> o gl j", gl=2)
            for gl in range(2):
                P.dma("sp", "misc", lambda e, gl=gl: e.dma_start(out=dtb[64 * gl:64 * gl + 64, :], in_=ldt[:, gl, :].partition_broadcast(64), allow_slow_non_contiguous=True), writes=["dtb"])
            P.op("act", lambda e: e.activation(out=dtb[:], in_=dtb[:], func=AF.Exp), reads=["dtb"], writes=["dtb"])
            P.op("dve", lambda e: e.tensor_tensor(out=th[:], in0=lim[:], in1=dtb[:], op=ALU.mult), reads=["lim", "dtb"], writes=["th"])
            P.op("dve", lambda e: e.tensor_tensor(out=lr[:], in0=lre[:], in1=dtb[:], op=ALU.mult), reads=["lre", "dtb"], writes=["lr"])
            P.op("pool", lambda e: e.iota(out=T1i[:], pattern=[[1, 64]], base=1, channel_multiplier=0), writes=["T1i"])
            P.op("dve", lambda e: e.tensor_copy(out=T1[:], in_=T1i[:]), reads=["T1i"], writes=["T1"])
            a3 = ang3[:].rearrange("p (j t) -> p j t", t=64)
            P.op("dve", lambda e: e.tensor_tensor(out=a3, in0=th[:].unsqueeze(2).to_broadcast([128, 32, 64]), in1=T1[:].unsqueeze(1).to_broadcast([128, 32, 64]), op=ALU.mult),
                 reads=["th", "T1", "sinS", "cosS"], writes=["sinT_ang", "cosT_ang"])
            sin_of(sinT[:].rearrange("p j t -> p (j t)"), ang3[:], 2048, 0.0, "sinT")
            sin_of(cosT[:].rearrange("p j t -> p (j t)"), ang3[:], 2048, math.pi / 2, "cosT")
            P.op("dve", lambda e: e.tensor_tensor(out=a3, in0=lr[:].unsqueeze(2).to_broadcast([128, 32, 64]), in1=T1[:].unsqueeze(1).to_broadcast([128, 32, 64]), op=ALU.mult),
                 reads=["lr", "T1", "sinT", "cosT"], writes=["ang3"])
            P.op("act", lambda e: e.activation(out=rpow[:].rearrange("p j t -> p (j t)"), in_=ang3[:], func=AF.Exp), reads=["ang3"], writes=["rpow"])
            P.op("dve", lambda e: e.tensor_copy(out=rcol[:], in_=rpow[:, :, 0]), reads=["rpow"], writes=["rcol"])
            P.op("dve", lambda e: e.tensor_copy(out=R0[:], in_=rcol[:].unsqueeze(2).to_broadcast([128, 32, 64])), reads=["rcol"], writes=["R0"])
            P.op("dve", lambda e: e.memset(R0[:, :, 0:1], 0.0), reads=["R0"], writes=["R0"])
            P.op("dve", lambda e: e.tensor_copy(out=E64r[:], in_=cosT[:, :, 63]), reads=["cosT"], writes=["E64r"])
            P.op("dve", lambda e: e.tensor_copy(out=E64i[:], in_=sinT[:, :, 63]), reads=["sinT"], writes=["E64i"])
            P.op("dve", lambda e: e.tensor_tensor(out=A64r[:], in0=E64r[:], in1=rpow[:, :, 63], op=ALU.mult), reads=["E64r", "rpow"], writes=["A64r"])
            P.op("dve", lambda e: e.tensor_tensor(out=A64i[:], in0=E64i[:], in1=rpow[:, :, 63], op=ALU.mult), reads=["E64i", "rpow"], writes=["A64i"])
            ar = sb("ar", [128, 32]); ai = sb("ai", [128, 32]); den = sb("den", [128, 32]); t0 = sb("t0", [128, 32]); t1 = sb("t1", [128, 32])
            fr = sb("fr", [128, 32]); fi = sb("fi", [128, 32])
            P.op("dve", lambda e: e.tensor_tensor(out=ar[:], in0=cosT[:, :, 0], in1=rcol[:], op=ALU.mult), reads=["cosT", "rcol"], writes=["ar"])
            P.op("dve", lambda e: e.tensor_scalar(out=ar[:], in0=ar[:], scalar1=-1.0, scalar2=None, op0=ALU.add), reads=["ar"], writes=["ar"])
            P.op("dve", lambda e: e.tensor_tensor(out=ai[:], in0=sinT[:, :, 0], in1=rcol[:], op=ALU.mult), reads=["sinT", "rcol"], writes=["ai"])
            P.op("dve", lambda e: e.tensor_tensor(out=den[:], in0=lre[:], in1=lre[:], op=ALU.mult), reads=["lre"], writes=["den"])
            P.op("dve", lambda e: e.tensor_tensor(out=t0[:], in0=lim[:], in1=lim[:], op=ALU.mult), reads=["lim"], writes=["t0"])
            P.op("dve", lambda e: e.tensor_tensor(out=den[:], in0=den[:], in1=t0[:], op=ALU.add), reads=["den", "t0"], writes=["den"])
            P.op("dve", lambda e: e.reciprocal(out=den[:], in_=den[:]), reads=["den"], writes=["den"])
            P.op("dve", lambda e: e.tensor_tensor(out=t0[:], in0=ar[:], in1=lre[:], op=ALU.mult), reads=["ar", "lre", "den"], writes=["t0"])
            P.op("dve", lambda e: e.tensor_tensor(out=t1[:], in0=ai[:], in1=lim[:], op=ALU.mult), reads=["ai", "lim"], writes=["t1"])
            P.op("dve", lambda e: e.tensor_tensor(out=t0[:], in0=t0[:], in1=t1[:], op=ALU.add), reads=["t0", "t1"], writes=["t0"])
            P.op("dve", lambda e: e.tensor_tensor(out=fr[:], in0=t0[:], in1=den[:], op=ALU.mult), reads=["t0", "den"], writes=["fr"])
            P.op("dve", lambda e: e.tensor_tensor(out=t0[:], in0=ai[:], in1=lre[:], op=ALU.mult), reads=["ai", "lre", "fr"], writes=["t0"])
            P.op("dve", lambda e: e.tensor_tensor(out=t1[:], in0=ar[:], in1=lim[:], op=ALU.mult), reads=["ar", "lim", "t0"], writes=["t1"])
            P.op("dve", lambda e: e.tensor_tensor(out=t0[:], in0=t0[:], in1=t1[:], op=ALU.subtract), reads=["t0", "t1"], writes=["t0"])
            P.op("dve", lambda e: e.tensor_tensor(out=fi[:], in0=t0[:], in1=den[:], op=ALU.mult), reads=["t0", "den"], writes=["fi"])
            Br = sb("Br", [128, 32, 16]); Bi = sb("Bi", [128, 32, 16]); bbr = sb("bbr", [128, 32, 16]); bbi = sb("bbi", [128, 32, 16]); tb = sb("tb", [128, 32, 16])
            Xr = sb("Xr", [128, 32, 2, 16], BF16); Xi = sb("Xi", [128, 32, 2, 16], BF16)
            m01 = sb("m01", [128, 2])
            for q_ in range(4):
                P.dma("sp", "misc", lambda e, q_=q_: e.dma_start(out=Br[:, 8 * q_:8 * q_ + 8, :], in_=b_re.rearrange("(j gl) p h -> (gl p) j h", gl=2)[:, 8 * q_:8 * q_ + 8, :], allow_slow_non_contiguous=True), writes=["Br"])
                P.dma("sp", "misc", lambda e, q_=q_: e.dma_start(out=Bi[:, 8 * q_:8 * q_ + 8, :], in_=b_im.rearrange("(j gl) p h -> (gl p) j h", gl=2)[:, 8 * q_:8 * q_ + 8, :], allow_slow_non_contiguous=True), writes=["Bi"])
            P.op("pool", lambda e: e.memset(m01[:], 0.0), writes=["m01"])
            P.op("pool", lambda e: e.memset(m01[0:64, 0:1], 1.0), reads=["m01"], writes=["m01"])
            P.op("pool", lambda e: e.memset(m01[64:128, 1:2], 1.0), reads=["m01"], writes=["m01"])
            frb = fr[:].unsqueeze(2).to_broadcast([128, 32, 16]); fib = fi[:].unsqueeze(2).to_broadcast([128, 32, 16])
            P.op("dve", lambda e: e.tensor_tensor(out=bbr[:], in0=Br[:], in1=frb, op=ALU.mult), reads=["Br", "fr"], writes=["bbr"])
            P.op("dve", lambda e: e.tensor_tensor(out=tb[:], in0=Bi[:], in1=fib, op=ALU.mult), reads=["Bi", "fi"], writes=["tb"])
            P.op("dve", lambda e: e.tensor_tensor(out=bbr[:], in0=bbr[:], in1=tb[:], op=ALU.subtract), reads=["bbr", "tb"], writes=["bbr"])
            P.op("dve", lambda e: e.tensor_tensor(out=bbi[:], in0=Bi[:], in1=frb, op=ALU.mult), reads=["Bi", "fr"], writes=["bbi"])
            P.op("dve", lambda e: e.tensor_tensor(out=tb[:], in0=Br[:], in1=fib, op=ALU.mult), reads=["Br", "fi", "bbr"], writes=["tb"])
            P.op("dve", lambda e: e.tensor_tensor(out=bbi[:], in0=bbi[:], in1=tb[:], op=ALU.add), reads=["bbi", "tb"], writes=["bbi"])
            for gl in range(2):
                P.op("dve", lambda e, gl=gl: e.tensor_scalar(out=Xr[:, :, gl, :], in0=bbr[:], scalar1=m01[:, gl:gl + 1], scalar2=None, op0=ALU.mult), reads=["bbr", "m01"], writes=["Xr"])
                P.op("dve", lambda e, gl=gl: e.tensor_scalar(out=Xi[:, :, gl, :], in0=bbi[:], scalar1=m01[:, gl:gl + 1], scalar2=None, op0=ALU.mult), reads=["bbi", "m01"], writes=["Xi"])
            pb = [p_[:].bitcast(BF16) for p_ in ps]
            for (X, BT, kx, kb, pi) in ((Xr, BTr, "Xr", "BTr", 0), (Xi, BTi, "Xi", "BTi", 1)):
                Xf = X[:].rearrange("p j g h -> p (j g h)")
                P.op("pe", [lambda e, i=i, Xf=Xf, pi=pi: e.transpose(out=pb[pi][:, i * 128:(i + 1) * 128], in_=Xf[:, i * 128:(i + 1) * 128], identity=ident[:]) for i in range(8)],
                     reads=[kx, "ident"], writes=[psk[pi]])
                P.op("dve", lambda e, BT=BT, pi=pi: e.tensor_copy(out=BT[:].rearrange("p i m -> p (i m)"), in_=pb[pi][:, 0:1024]), reads=[psk[pi]], writes=[kb])
            Cr = sb("Cr", [128, 8, 64]); Ci = sb("Ci", [128, 8, 64]); bm = sb("bm", [128, 128])
            Yr = sb("Yr", [128, 8, 2, 64], BF16); Yi = sb("Yi", [128, 8, 2, 64], BF16)
            P.dma("sp", "misc", lambda e: e.dma_start(out=Cr[:], in_=c_re.rearrange("(i r) h p -> (r h) i p", r=8), allow_slow_non_contiguous=True), writes=["Cr"])
            P.dma("sp", "misc", lambda e: e.dma_start(out=Ci[:], in_=c_im.rearrange("(i r) h p -> (r h) i p", r=8), allow_slow_non_contiguous=True), writes=["Ci"])
            P.dma("sp", "misc", lambda e: e.dma_start(out=bm[:], in_=bmask), writes=["bm"])
            bmb = bm[:].rearrange("p (g q) -> p g q", g=2).unsqueeze(1).to_broadcast([128, 8, 2, 64])
            P.op("dve", lambda e: e.tensor_tensor(out=Yr[:], in0=Cr[:].unsqueeze(2).to_broadcast([128, 8, 2, 64]), in1=bmb, op=ALU.mult), reads=["Cr", "bm"], writes=["Yr"])
            P.op("dve", lambda e: e.tensor_tensor(out=Yi[:], in0=Ci[:].unsqueeze(2).to_broadcast([128, 8, 2, 64]), in1=bmb, op=ALU.mult), reads=["Ci", "bm"], writes=["Yi"])
            for (Y, CT, ky, kc, pi, sgn) in ((Yr, CTr, "Yr", "CTr", 2, 1.0), (Yi, CTi, "Yi", "CTi", 3, -1.0)):
                Yf = Y[:].rearrange("p i g q -> p (i g q)")
                P.op("pe", [lambda e, i=i, Yf=Yf, pi=pi: e.transpose(out=pb[pi][:, i * 128:(i + 1) * 128], in_=Yf[:, i * 128:(i + 1) * 128], identity=ident[:]) for i in range(8)],
                     reads=[ky, "ident"], writes=[psk[pi]])
                P.op("dve", lambda e, CT=CT, pi=pi, sgn=sgn: e.tensor_scalar(out=CT[:].rearrange("p i m -> p (i m)"), in0=pb[pi][:, 0:1024], scalar1=sgn, scalar2=None, op0=ALU.mult),
                     reads=[psk[pi]], writes=[kc])

            P.dma("sp", "misc", lambda e: e.dma_start(out=dcol[:], in_=ssm_d.rearrange("o (i p) -> p (o i)", p=128), allow_slow_non_contiguous=True), writes=["dcol"])
            P.dma("sp", "misc", lambda e: e.dma_start(out=gsub[:], in_=g_subln.rearrange("o p -> p o"), allow_slow_non_contiguous=True), writes=["gsub"])
            P.dma("sp", "misc", lambda e: e.dma_start(out=bglu[:], in_=b_glu.rearrange("o (i p) -> p (o i)", p=128), allow_slow_non_contiguous=True), writes=["bglu"])
            lt = sb("lt", [128, 4, 64]); ls = sb("ls", [128, 2])
            for n, src in enumerate((lq1, lk1, lq2, lk2)):
                P.dma("sp", "misc", lambda e, n=n, src=src: e.dma_start(out=lt[:, n, :], in_=src.partition_broadcast(128)), writes=["lt"])
            P.op("dve", lambda e: e.tensor_tensor(out=lt[:, 0, :], in0=lt[:, 0, :], in1=lt[:, 1, :], op=ALU.mult), reads=["lt"], writes=["lt"])
            P.op("dve", lambda e: e.tensor_tensor(out=lt[:, 2, :], in0=lt[:, 2, :], in1=lt[:, 3, :], op=ALU.mult), reads=["lt"], writes=["lt"])
            P.op("dve", lambda e: e.reduce_sum(out=ls[:, 0:1], in_=lt[:, 0, :], axis=AX.X), reads=["lt"], writes=["ls"])
            P.op("dve", lambda e: e.reduce_sum(out=ls[:, 1:2], in_=lt[:, 2, :], axis=AX.X), reads=["lt", "ls"], writes=["ls"])
            P.op("act", lambda e: e.activation(out=ls[:], in_=ls[:], func=AF.Exp), reads=["ls"], writes=["ls"])
            P.op("dve", lambda e: e.tensor_tensor(out=lamc[:, 0:1], in0=ls[:, 1:2], in1=ls[:, 0:1], op=ALU.subtract), reads=["ls"], writes=["lamc"])
            P.op("dve", lambda e: e.tensor_scalar(out=lamc[:, 0:1], in0=lamc[:, 0:1], scalar1=-LAM_INIT, scalar2=None, op0=ALU.add), reads=["lamc"], writes=["lamc"])
            P.op("dve", lambda e: e.memset(lamc[:, 1:2], EPS), reads=["lamc"], writes=["lamc"])

            cs = sb("cs", [5, D]); csT = sb("csT", [128, 16, 5])
            wa = [sb("wa%d" % i, [128, 16, 256]) for i in range(2)]
            ba = [sb("ba%d" % i, [5, 256]) for i in range(2)]
            mst = [sb("mst%d" % i, [5, 256]) for i in range(2)]
            P.dma("sp", "misc", lambda e: e.dma_start(out=cs[:], in_=cvec), writes=["cs"])
            sg = sb("sg", [5, D])
            P.op("act", lambda e: e.activation(out=sg[:], in_=cs[:], func=AF.Sigmoid), reads=["cs"], writes=["sg"])
            P.op("dve", lambda e: e.tensor_tensor(out=cs[:], in0=cs[:], in1=sg[:], op=ALU.mult), reads=["cs", "sg"], writes=["cs"])
            P.op("pe", [lambda e, c=c: e.transpose(out=ps[4][:, c * 5:(c + 1) * 5], in_=cs[:, c * 128:(c + 1) * 128], identity=identf[0:5, 0:5]) for c in range(16)],
                 reads=["cs", "identf"], writes=[psk[4]])
            P.op("dve", lambda e: e.tensor_copy(out=csT[:].rearrange("p c r -> p (c r)"), in_=ps[4][:, 0:80]), reads=[psk[4]], writes=["csT"])
            wav = w_ada.rearrange("(c p) n -> p c n", p=128)
            for cb in range(48):
                bi = cb % 2
                P.dma("sp", "wa%d" % bi, lambda e, cb=cb, bi=bi: e.dma_start(out=wa[bi][:], in_=wav[:, :, cb * 256:(cb + 1) * 256]), writes=["wa%d" % bi])
                P.dma("sp", "ba%d" % bi, lambda e, cb=cb, bi=bi: e.dma_start(out=ba[bi][:], in_=b_ada[:, cb * 256:(cb + 1) * 256].partition_broadcast(5)), writes=["ba%d" % bi])
                pi = 5 + bi
                P.op("pe", [lambda e, c=c, bi=bi, pi=pi: e.matmul(out=ps[pi][0:5, 0:256], lhsT=csT[:, c, :], rhs=wa[bi][:, c, :], start=(c == 0), stop=(c == 15)) for c in range(16)],
                     reads=["csT", "wa%d" % bi], writes=[psk[pi]])
                P.op("dve", lambda e, bi=bi, pi=pi: e.tensor_tensor(out=mst[bi][:], in0=ps[pi][0:5, 0:256], in1=ba[bi][:], op=ALU.add), reads=[psk[pi], "ba%d" % bi], writes=["mst%d" % bi])
                P.dma("sp", "mo%d" % bi, lambda e, cb=cb, bi=bi: e.dma_start(out=mrow[:, cb * 256:(cb + 1) * 256], in_=mst[bi][:]), reads=["mst%d" % bi], writes=["mrow"])
            gm = sb("gm", [128, 16]); gf = sb("gf", [128, 16])
            P.dma("sp", "misc", lambda e: e.dma_start(out=gm[:], in_=g_mix.rearrange("o (c p) -> p (o c)", p=128), allow_slow_non_contiguous=True), writes=["gm"])
            P.dma("sp", "misc", lambda e: e.dma_start(out=gf[:], in_=g_ffn.rearrange("o (c p) -> p (o c)", p=128), allow_slow_non_contiguous=True), writes=["gf"])
            for (Gt, St, gvec, o_shift, o_scale, kg, ks, kv) in ((G1, S1, gm, 0, D, "G1", "S1", "gm"), (G2, S2, gf, 3 * D, 4 * D, "G2", "S2", "gf")):
                for r in range(5):
                    for q_ in range(2):
                        P.dma("sp", "misc", lambda e, St=St, o=o_shift, r=r, q_=q_: e.dma_start(out=St[:, 8 * q_:8 * q_ + 8, r], in_=mrow[r:r + 1, o + 1024 * q_:o + 1024 * q_ + 1024].rearrange("r (c p) -> p (r c)", p=128), allow_slow_non_contiguous=True), reads=["mrow"], writes=[ks])
                        P.dma("sp", "misc", lambda e, Gt=Gt, o=o_scale, r=r, q_=q_: e.dma_start(out=Gt[:, 8 * q_:8 * q_ + 8, r], in_=mrow[r:r + 1, o + 1024 * q_:o + 1024 * q_ + 1024].rearrange("r (c p) -> p (r c)", p=128), allow_slow_non_contiguous=True), reads=["mrow"], writes=[kg])
                P.op("dve", lambda e, Gt=Gt: e.tensor_scalar(out=Gt[:], in0=Gt[:], scalar1=1.0, scalar2=None, op0=ALU.add), reads=[kg], writes=[kg])
                P.op("dve", lambda e, Gt=Gt, gvec=gvec: e.tensor_tensor(out=Gt[:], in0=Gt[:], in1=gvec[:].unsqueeze(2).to_broadcast([128, 16, 5]), op=ALU.mult), reads=[kg, kv], writes=[kg])
            P.emit()
        g.stop = stop_after
        if stop_after == "b0":
            return nc

        def norm_to_hT(P, sb_x, kx, nsub, psub, Gt, St, rows, hT, khT, xb_t, kxb, st_t, kst, pbank, sub0=0):
            for sub in range(nsub):
                xv = sb_x[0:psub, sub, :]
                xbv = xb_t[0:psub, :]
                P.op("dve", lambda e: e.memset(st_t[0:psub, 0:1], 0.0), writes=[kst])
                P.op("act", lambda e, xv=xv, xbv=xbv: e.activation(out=xbv, in_=xv, func=AF.Square, accum_out=st_t[0:psub, 0:1]), reads=[kx], writes=[kxb, kst])
                P.op("act", lambda e: e.activation(out=st_t[0:psub, 1:2], in_=st_t[0:psub, 0:1], func=AF.Sqrt, bias=lamc[0:psub, 1:2], scale=1.0 / D), reads=[kst, "lamc"], writes=[kst])
                P.op("dve", lambda e: e.reciprocal(out=st_t[0:psub, 2:3], in_=st_t[0:psub, 1:2]), reads=[kst], writes=[kst])
                P.op("act", lambda e, xv=xv, xbv=xbv: e.activation(out=xbv, in_=xv, func=AF.Copy, scale=st_t[0:psub, 2:3]), reads=[kx, kst], writes=[kxb])
                for half in range(2):
                    pk = pbank[half]
                    pv = ps[pk][:].bitcast(BF16)
                    P.op("pe", [lambda e, c=c, half=half, pv=pv, xbv=xbv: e.transpose(out=pv[:, (c % 8) * psub:(c % 8 + 1) * psub], in_=xbv[:, c * 128:(c + 1) * 128], identity=ident[0:psub, 0:psub])
                                for c in range(half * 8, half * 8 + 8)], reads=[kxb, "ident"], writes=[psk[pk]])
                    r = rows[sub]
                    for c in range(half * 8, half * 8 + 8):
                        P.op("dve", lambda e, c=c, pv=pv, sub=sub, r=r: e.tensor_scalar(out=hT[:, c, (sub0 + sub) * psub:(sub0 + sub + 1) * psub], in0=pv[:, (c % 8) * psub:(c % 8 + 1) * psub],
                                                                                       scalar1=Gt[:, c, r:r + 1], scalar2=St[:, c, r:r + 1], op0=ALU.mult, op1=ALU.add),
                             reads=[psk[pk], "G1", "S1", "G2", "S2"], writes=[khT])

        def rope(P, kt, kk, nh2, psub, cs_ap, sn_ap, tmp, ktmp, dst=None, kdst=None):
            t1 = kt[:, :, 0:8]; t2 = kt[:, :, 8:16]
            o1, o2, ko = (t1, t2, kk) if dst is None else (dst[:, :, 0:8], dst[:, :, 8:16], kdst)
            cb = cs_ap.unsqueeze(1).to_broadcast([psub, nh2, 8]); sbb = sn_ap.unsqueeze(1).to_broadcast([psub, nh2, 8])
            a = tmp[0:psub, 0, 0:nh2, :]; b = tmp[0:psub, 1, 0:nh2, :]; c = tmp[0:psub, 2, 0:nh2, :]; d = tmp[0:psub, 3, 0:nh2, :]
            P.op("dve", lambda e: e.tensor_tensor(out=a, in0=t1, in1=cb, op=ALU.mult), reads=[kk, "ropetab"], writes=[ktmp])
            P.op("dve", lambda e: e.tensor_tensor(out=b, in0=t2, in1=sbb, op=ALU.mult), reads=[kk], writes=[ktmp])
            P.op("dve", lambda e: e.tensor_tensor(out=c, in0=t2, in1=cb, op=ALU.mult), reads=[kk], writes=[ktmp])
            P.op("dve", lambda e: e.tensor_tensor(out=d, in0=t1, in1=sbb, op=ALU.mult), reads=[kk], writes=[ktmp])
            P.op("dve", lambda e: e.tensor_tensor(out=o1, in0=a, in1=b, op=ALU.subtract), reads=[ktmp], writes=[ko])
            P.op("dve", lambda e: e.tensor_tensor(out=o2, in0=c, in1=d, op=ALU.add), reads=[ktmp], writes=[ko])

        with contextlib.ExitStack() as sc:
            def sb(name, shape, dt=F32):
                return sc.enter_context(nc.sbuf_tensor(name, list(shape), dt))
            P = Prog(nc)
            P.buf["wb_in"] = [None, []]
            def cast(src, dst, rows, piece, slot, key):
                for r0 in range(0, rows, piece):
                    P.dma("pool", slot, lambda e, r0=r0: e.dma_start(out=dst[r0:r0 + piece, :], in_=src[r0:r0 + piece, :]), writes=[key])
            cast(w_glu, wb_glu, 1024, 256, "c_glu", "wb_glu")
            cast(w_out, wb_out, D, 256, "c_out", "wb_out")
            cast(w_ff1, wb_ff1, D, 128, "c_ff1", "wb_ff1")
            cast(w_ff2, wb_ff2, 4 * D, 512, "c_ff2", "wb_ff2")
            for b_ in range(4):
                for r0 in range(0, 2048, 1024):
                    P.dma("pool", "c_cv", lambda e, b_=b_, r0=r0: e.dma_start(out=VSS[b_, r0:r0 + 1024, :], in_=cv[b_, r0:r0 + 1024, :]), writes=["VSS"])

            NXB = 2
            xt = [sb("xt%d" % i, [128, 1, D]) for i in range(NXB)]
            xb_t = sb("xb_t", [128, D], BF16); st_t = sb("st_t", [128, 4])
            hT = sb("hT", [128, 16, TT], BF16)
            wblk = [sb("wblk%d" % i, [128, 16, 512], BF16) for i in range(2)]
            kst = sb("kst", [128, 4, 1024]); vst = sb("vst", [128, 4, 1024])
            kbf = sb("kbf", [128, 1024], BF16); vbf = sb("vbf", [128, 4, 1024], BF16)
            KTt = sb("KTt", [128, 8, TT], BF16)
            uT = sb("uT", [128, 8, TT], BF16)
            rtmp = sb("rtmp", [128, 4, 16, 8])
            wre = [sb("wre%d" % i, [128, TT]) for i in range(2)]; wim = [sb("wim%d" % i, [128, TT]) for i in range(2)]
            tA = [sb("tA0", [128, TT])] * 2; tB = [sb("tB0", [128, TT])] * 2
            R0f = [sb("R0f%d" % i, [128, TT]) for i in range(2)]
            TK = ["tA0", "tA0"]; TKB = ["tB0", "tB0"]
            gre = [sb("gre%d" % i, [128, TT]) for i in range(2)]; gim = [sb("gim%d" % i, [128, TT]) for i in range(2)]
            Lr = sb("Lr", [128, 32, 8]); Li = sb("Li", [128, 32, 8]); Lt0 = sb("Lt0", [128, 32, 8]); Lt1 = sb("Lt1", [128, 32, 8])
            Hr = sb("Hr", [128, 32, 9]); Hi = sb("Hi", [128, 32, 9]); h0 = sb("h0", [128, 32]); h1 = sb("h1", [128, 32])
            P.op("dve", lambda e: e.memset(Hr[:], 0.0), writes=["Hr"])
            P.op("dve", lambda e: e.memset(Hi[:], 0.0), writes=["Hi"])
            wv = wb_in.rearrange("(c p) n -> p c n", p=128)
            wcount = [0]

            def load_w(colblk):
                bi = wcount[0] % 2
                wcount[0] += 1
                P.dma("sp", "wblk%d" % bi, lambda e, bi=bi, colblk=colblk: e.dma_start(out=wblk[bi][:], in_=wv[:, :, colblk * 512:(colblk + 1) * 512]),
                      reads=["wb_in"], writes=["wblk%d" % bi])
                return bi

            def s5_scan_pair(P, j, T, buf, inject=None):
                i, j4 = j // 4, j % 4
                nch = T // 64
                P.op("pe", lambda e: e.matmul(out=ps[0 + 2 * buf][:, 0:T], lhsT=BTr[32 * j4:32 * j4 + 32, i, :], rhs=uT[32 * j4:32 * j4 + 32, i, 0:T], start=True, stop=True, tile_position=(32 * j4, 0)),
                     reads=["uT", "BTr"], writes=[psk[0 + 2 * buf]])
                P.op("pe", lambda e: e.matmul(out=ps[1 + 2 * buf][:, 0:T], lhsT=BTi[32 * j4:32 * j4 + 32, i, :], rhs=uT[32 * j4:32 * j4 + 32, i, 0:T], start=True, stop=True, tile_position=(32 * j4, 0)),
                     reads=["uT", "BTi"], writes=[psk[1 + 2 * buf]])
                cb = cosT[:, j, :].unsqueeze(1).to_broadcast([128, nch, 64]); sbb = sinT[:, j, :].unsqueeze(1).to_broadcast([128, nch, 64])
                v3 = lambda t: t[:, 0:T].rearrange("p (c t) -> p c t", t=64)
                bre = ps[0 + 2 * buf][:, 0:T].rearrange("p (c t) -> p c t", t=64); bim = ps[1 + 2 * buf][:, 0:T].rearrange("p (c t) -> p c t", t=64)
                kb = str(buf)
                P.op("pool", lambda e: e.tensor_copy(out=v3(R0f[buf]), in_=R0[:, j, :].unsqueeze(1).to_broadcast([128, nch, 64])), reads=["R0"], writes=["R0f" + kb])
                xf = xb_t[:].bitcast(F32)
                tCf = xf[:, 0:512]; tDf = xf[:, 512:1024]
                tC = tCf[:, 0:T].rearrange("p (c t) -> p c t", t=64); tD = tDf[:, 0:T].rearrange("p (c t) -> p c t", t=64)
                P.op("dve", lambda e: e.tensor_tensor(out=v3(tA[buf]), in0=bre, in1=cb, op=ALU.mult), reads=[psk[0 + 2 * buf], "cosT"], writes=[TK[buf]])
                P.op("dve", lambda e: e.tensor_tensor(out=tC, in0=bim, in1=sbb, op=ALU.mult), reads=[psk[1 + 2 * buf], "sinT"], writes=["xb_t"])
                P.op("dve", lambda e: e.tensor_tensor(out=v3(tB[buf]), in0=bim, in1=cb, op=ALU.mult), reads=[psk[1 + 2 * buf], "cosT"], writes=[TKB[buf]])
                P.op("dve", lambda e: e.tensor_tensor(out=tD, in0=bre, in1=sbb, op=ALU.mult), reads=[psk[0 + 2 * buf], "sinT"], writes=["xb_t"])
                P.op("dve", lambda e: e.tensor_tensor(out=wre[buf][:, 0:T], in0=tA[buf][:, 0:T], in1=tCf[:, 0:T], op=ALU.add), reads=[TK[buf], "xb_t"], writes=["wre" + kb])
                P.op("dve", lambda e: e.tensor_tensor(out=wim[buf][:, 0:T], in0=tB[buf][:, 0:T], in1=tDf[:, 0:T], op=ALU.subtract), reads=[TKB[buf], "xb_t"], writes=["wim" + kb])
                if inject is not None:
                    inject(j, buf, v3)
                P.op("dve", lambda e: e.tensor_tensor_scan(out=gre[buf][:, 0:T], data0=R0f[buf][:, 0:T], data1=wre[buf][:, 0:T], initial=0.0, op0=ALU.mult, op1=ALU.add),
                     reads=["R0f" + kb, "wre" + kb], writes=["gre" + kb])
                P.op("dve", lambda e: e.tensor_tensor_scan(out=gim[buf][:, 0:T], data0=R0f[buf][:, 0:T], data1=wim[buf][:, 0:T], initial=0.0, op0=ALU.mult, op1=ALU.add),
                     reads=["R0f" + kb, "wim" + kb], writes=["gim" + kb])
            g.s5_scan_pair = s5_scan_pair

            def cmul(P, outr, outi, ar_, ai_, br_, bi_, tmp0, tmp1, keys_in, kor, koi, kt0, kt1, eng="dve"):
                P.op(eng, lambda e: e.tensor_tensor(out=tmp0, in0=ar_, in1=br_, op=ALU.mult), reads=keys_in, writes=[kt0])
                P.op(eng, lambda e: e.tensor_tensor(out=tmp1, in0=ai_, in1=bi_, op=ALU.mult), reads=keys_in, writes=[kt1])
                P.op(eng, lambda e: e.tensor_tensor(out=outr, in0=tmp0, in1=tmp1, op=ALU.subtract), reads=[kt0, kt1], writes=[kor])
                P.op(eng, lambda e: e.tensor_tensor(out=tmp0, in0=ar_, in1=bi_, op=ALU.mult), reads=keys_in + [kor], writes=[kt0])
                P.op(eng, lambda e: e.tensor_tensor(out=tmp1, in0=ai_, in1=br_, op=ALU.mult), reads=keys_in + [kor], writes=[kt1])
                P.op(eng, lambda e: e.tensor_tensor(out=outi, in0=tmp0, in1=tmp1, op=ALU.add), reads=[kt0, kt1], writes=[koi])

            g.cmul = cmul
            for gt in range(NT):
                for sub in range(4):
                    xi = (gt * 4 + sub) % NXB
                    kx = "xt%d" % xi
                    P.dma("sp", kx, lambda e, gt=gt, xi=xi, sub=sub: e.dma_start(out=xt[xi][:, 0, :], in_=xa[gt * TT + sub * 128:gt * TT + (sub + 1) * 128, :]), writes=[kx])
                    norm_to_hT(P, xt[xi], kx, 1, 128, G1, S1, [0], hT, "hT", xb_t, "xb_t", st_t, "st_t", (4, 5), sub0=sub)
                for cbk, (dst, kd) in enumerate(((kst, "kst"), (kst, "kst"), (vst, "vst"), (vst, "vst"))):
                    bi = load_w(2 + cbk)
                    for sub in range(4):
                        pk = 4 + (cbk * 4 + sub) % 4
                        P.op("pe", [lambda e, c=c, sub=sub, bi=bi, pk=pk: e.matmul(out=ps[pk][:], lhsT=hT[:, c, sub * 128:(sub + 1) * 128], rhs=wblk[bi][:, c, :], start=(c == 0), stop=(c == 15)) for c in range(16)],
                             reads=["hT", "wblk%d" % bi], writes=[psk[pk]])
                        P.op("act", lambda e, sub=sub, pk=pk, dst=dst, cbk=cbk: e.copy(out=dst[:, sub, (cbk % 2) * 512:(cbk % 2 + 1) * 512], in_=ps[pk][:]), reads=[psk[pk]], writes=[kd])
                for ub in range(2):
                    bi = load_w(6 + ub)
                    for ii in range(4):
                        i = ub * 4 + ii
                        pk = 4 + ii
                        P.op("pe", [lambda e, c=c, ii=ii, bi=bi, pk=pk: e.matmul(out=ps[pk][:], lhsT=wblk[bi][:, c, ii * 128:(ii + 1) * 128], rhs=hT[:, c, :], start=(c == 0), stop=(c == 15)) for c in range(16)],
                             reads=["hT", "wblk%d" % bi], writes=[psk[pk]])
                        P.op("act", lambda e, i=i, pk=pk: e.copy(out=uT[:, i, :], in_=ps[pk][:]), reads=[psk[pk]], writes=["uT"])
                P.buf.setdefault("ropetab", [None, []])
                for sub in range(4):
                    n = gt * 4 + sub
                    rope(P, kst[:, sub, :].rearrange("p (h d) -> p h d", d=64), "kst", 16, 128, cosA[:, n, :], sinA[:, n, :], rtmp, "rtmp")
                P.dma("sp", "nk", lambda e, gt=gt: e.dma_start(out=nk[gt * TT:(gt + 1) * TT, :].rearrange("(s p) d -> p s d", p=128), in_=kst[:]), reads=["kst"], writes=["nk"])
                P.dma("sp", "nv", lambda e, gt=gt: e.dma_start(out=nv[gt * TT:(gt + 1) * TT, :].rearrange("(s p) d -> p s d", p=128), in_=vst[:]), reads=["vst"], writes=["nv"])
                P.op("pool", lambda e: e.tensor_copy(out=vbf[:], in_=vst[:]), reads=["vst"], writes=["vbf"])
                P.dma("sp", "VS", lambda e, gt=gt: e.dma_start(out=VS[gt * TT:(gt + 1) * TT, :].rearrange("(s p) d -> p s d", p=128), in_=vbf[:]), reads=["vbf"], writes=["VS"])
                for sub in range(4):
                    P.op("act", lambda e, sub=sub: e.copy(out=kbf[:], in_=kst[:, sub, :]), reads=["kst"], writes=["kbf"])
                    pk = 4 + sub % 2
                    pv = ps[pk][:].bitcast(BF16)
                    P.op("pe", [lambda e, h=h, pv=pv: e.transpose(out=pv[:, h * 128:(h + 1) * 128], in_=kbf[:, h * 128:(h + 1) * 128], identity=ident[:]) for h in range(8)],
                         reads=["kbf", "ident"], writes=[psk[pk]])
                    P.op("dve", lambda e, sub=sub, pv=pv: e.tensor_copy(out=KTt[:, :, sub * 128:(sub + 1) * 128], in_=pv[:, 0:1024].rearrange("p (h t) -> p h t", t=128)), reads=[psk[pk]], writes=["KTt"])
                P.dma("sp", "KT", lambda e, gt=gt: e.dma_start(out=KT[:, :, gt * TT:(gt + 1) * TT].rearrange("h p t -> p h t"), in_=KTt[:]), reads=["KTt"], writes=["KT"])
                for j in range(32):
                    buf = j % 2
                    s5_scan_pair(P, j, TT, buf)
                    kb = str(buf)
                    P.op("pool", lambda e, j=j, buf=buf: e.tensor_copy(out=Lt0[:, j, :], in_=gre[buf][:].rearrange("p (c t) -> p c t", t=64)[:, :, 63]), reads=["gre" + kb], writes=["Lt0"])
                    P.op("pool", lambda e, j=j, buf=buf: e.tensor_copy(out=Lt1[:, j, :], in_=gim[buf][:].rearrange("p (c t) -> p c t", t=64)[:, :, 63]), reads=["gim" + kb], writes=["Lt1"])
                e64r = E64r[:].unsqueeze(2).to_broadcast([128, 32, 8]); e64i = E64i[:].unsqueeze(2).to_broadcast([128, 32, 8])
                t_a = wre[0][:, 0:256].rearrange("p (j c) -> p j c", c=8); t_b = wim[0][:, 0:256].rearrange("p (j c) -> p j c", c=8)
                cmul(P, Lr[:], Li[:], Lt0[:], Lt1[:], e64r, e64i, t_a, t_b, ["Lt0", "Lt1", "E64r", "E64i"], "Lr", "Li", "wre0", "wim0", eng="pool")
                for c in range(8):
                    cmul(P, Hr[:, :, c + 1], Hi[:, :, c + 1], Hr[:, :, c], Hi[:, :, c], A64r[:], A64i[:], h0[:], h1[:], ["Hr", "Hi", "A64r", "A64i"], "Hr", "Hi", "h0", "h1", eng="pool")
                    P.op("pool", lambda e, c=c: e.tensor_tensor(out=Hr[:, :, c + 1], in0=Hr[:, :, c + 1], in1=Lr[:, :, c], op=ALU.add), reads=["Hr", "Lr"], writes=["Hr"])
                    P.op("pool", lambda e, c=c: e.tensor_tensor(out=Hi[:, :, c + 1], in0=Hi[:, :, c + 1], in1=Li[:, :, c], op=ALU.add), reads=["Hi", "Li"], writes=["Hi"])
                P.dma("sp", "HSr", lambda e, gt=gt: e.dma_start(out=HSr[gt], in_=Hr[:, :, 0:8]), reads=["Hr"], writes=["HSr"])
                P.dma("sp", "HSi", lambda e, gt=gt: e.dma_start(out=HSi[gt], in_=Hi[:, :, 0:8]), reads=["Hi"], writes=["HSi"])
                if gt < NT - 1:
                    P.op("pool", lambda e: e.tensor_copy(out=Hr[:, :, 0], in_=Hr[:, :, 8]), reads=["Hr"], writes=["Hr"])
                    P.op("pool", lambda e: e.tensor_copy(out=Hi[:, :, 0], in_=Hi[:, :, 8]), reads=["Hi"], writes=["Hi"])
            for b_ in range(4):
                for kg in range(4):
                    for kq in range(4):
                        kbi = kg * 4 + kq
                        xi = kbi % NXB
                        kx = "xt%d" % xi
                        P.dma("sp", kx, lambda e, b_=b_, kbi=kbi, xi=xi: e.dma_start(out=xt[xi][:, 0, 0:1024], in_=ck[b_, kbi * 128:(kbi + 1) * 128, :]), writes=[kx])
                        P.op("act", lambda e, xi=xi: e.copy(out=kbf[:], in_=xt[xi][:, 0, 0:1024]), reads=[kx], writes=["kbf"])
                        pk = 4 + kq % 2
                        pv = ps[pk][:].bitcast(BF16)
                        P.op("pe", [lambda e, h=h, pv=pv: e.transpose(out=pv[:, h * 128:(h + 1) * 128], in_=kbf[:, h * 128:(h + 1) * 128], identity=ident[:]) for h in range(8)],
                             reads=["kbf", "ident"], writes=[psk[pk]])
                        P.op("dve", lambda e, kq=kq, pv=pv: e.tensor_copy(out=KTt[:, :, kq * 128:(kq + 1) * 128], in_=pv[:, 0:1024].rearrange("p (h t) -> p h t", t=128)), reads=[psk[pk]], writes=["KTt"])
                    P.dma("sp", "KTS", lambda e, b_=b_, kg=kg: e.dma_start(out=KTS[b_, :, :, kg * 512:(kg + 1) * 512].rearrange("h p t -> p h t"), in_=KTt[:]), reads=["KTt"], writes=["KTS"])
            P.op("dve", lambda e: e.tensor_copy(out=h0[:], in_=Hr[:, :, 8]), reads=["Hr", "h0"], writes=["h0"])
            P.op("dve", lambda e: e.tensor_copy(out=h1[:], in_=Hi[:, :, 8]), reads=["Hi", "h1"], writes=["h1"])
            for q_ in range(4):
                P.dma("sp", "hre", lambda e, q_=q_: e.dma_start(out=hre.rearrange("(j gl) p -> (gl p) j", gl=2)[:, 8 * q_:8 * q_ + 8], in_=h0[:, 8 * q_:8 * q_ + 8], allow_slow_non_contiguous=True), reads=["h0"], writes=["hre"])
                P.dma("sp", "him", lambda e, q_=q_: e.dma_start(out=him.rearrange("(j gl) p -> (gl p) j", gl=2)[:, 8 * q_:8 * q_ + 8], in_=h1[:, 8 * q_:8 * q_ + 8], allow_slow_non_contiguous=True), reads=["h1"], writes=["him"])
            P.emit()
        if stop_after == "bA":
            return nc

        x1s = g.x1s
        NTILES = 9
        g.b1_stop = None
        g.b1_tiles = list(range(NTILES))
        if stop_after.startswith("bB1:"):
            _, st_, tl_ = stop_after.split(":")
            g.b1_stop = int(st_)
            g.b1_tiles = [int(t_) for t_ in tl_.split(",")]

        def tile_cfg(ti):
            if ti < 8:
                return dict(T=512, nsub=4, psub=128, rows=[0, 0, 0, 0], sample=False, s=ti,
                            xsrc=lambda sub, ti=ti: xo[ti * 512 + sub * 128: ti * 512 + (sub + 1) * 128, :],
                            x1=lambda sub, ti=ti: x1s[ti * 512 + sub * 128: ti * 512 + (sub + 1) * 128, :],
                            ydst=lambda sub, ti=ti: yo[ti * 512 + sub * 128: ti * 512 + (sub + 1) * 128, :],
                            cs=lambda sub, ti=ti: (cosO[:, ti * 4 + sub, :], sinO[:, ti * 4 + sub, :]))
            return dict(T=256, nsub=4, psub=64, rows=[1, 2, 3, 4], sample=True, s=None,
                        xsrc=lambda sub: xs[sub * 64:(sub + 1) * 64, :],
                        x1=lambda sub: x1s[4096 + sub * 64: 4096 + (sub + 1) * 64, :],
                        ydst=lambda sub: ys[sub * 64:(sub + 1) * 64, :],
                        cs=lambda sub: (cosS[0:64, :], sinS[0:64, :]))

        with contextlib.ExitStack() as sc:
            def sb(name, shape, dt=F32):
                return sc.enter_context(nc.sbuf_tensor("b1_" + name, list(shape), dt))
            P = Prog(nc)
            for k_ in ("wb_in", "wb_glu", "wb_out", "KT", "VS", "HSr", "HSi", "VSS", "KTS", "mrow", "ropetab"):
                P.buf[k_] = [None, []]
            xt = [sb("xt%d" % i, [128, 1, D]) for i in range(2)]
            xb_t = sb("xb_t", [128, D], BF16); st_t = sb("st_t", [128, 4])
            hT = sb("hT", [128, 16, TT], BF16)
            catT = hT
            wblk = [sb("wblk%d" % i, [128, 16, 512], BF16) for i in range(2)]
            st32 = [sb("st32_%d" % i, [128, 512]) for i in range(2)]
            stg = [sb("stg%d" % i, [128, 4, 1024], BF16) for i in range(3)]
            QT = sb("QT", [128, 8, TT], BF16); KTo = sb("KTo", [128, 8, TT], BF16)
            uTb = [sb("uTb%d" % i, [128, TT], BF16) for i in range(2)]; uTf = [sb("uTf%d" % i, [128, TT]) for i in range(2)]
            rtmp = sb("rtmp", [128, 4, 16, 8])
            wre = [sb("wre0", [128, TT])] * 2; wim = [sb("wim0", [128, TT])] * 2
            tA = sb("tA0", [128, TT]); tB = sb("tB0", [128, TT])
            R0f = [sb("R0f0", [128, TT])] * 2
            gre = [sb("gre0", [128, TT])] * 2; gim = [sb("gim0", [128, TT])] * 2
            hbr = [sb("hbr0", [128, TT], BF16)] * 2; hbi = [sb("hbi0", [128, TT], BF16)] * 2
            Ha = sb("Ha", [128, 32, 8]); Hb = sb("Hb", [128, 32, 8]); Hnr = sb("Hnr", [128, 32, 8]); Hni = sb("Hni", [128, 32, 8])
            Lt0 = sb("Lt0", [128, 32, 4]); Lt1 = sb("Lt1", [128, 32, 4]); Ler = sb("Ler", [128, 32, 4]); Lei = sb("Lei", [128, 32, 4])
            zT = stg[0][:].rearrange("p s n -> p (s n)").rearrange("p (i t) -> p i t", t=TT)
            KTb = [sb("KTb%d" % i, [128, 512], BF16) for i in range(2)]; Vb = [sb("Vb%d" % i, [128, 4, 128], BF16) for i in range(2)]
            Pb = [[sb("P%d_%d" % (c_, i), [128, 512], BF16) for i in range(2)] for c_ in range(2)]
            n0 = sb("n0", [128, TT]); n1 = sb("n1", [128, TT]); n2b = sb("n2b", [128, TT], BF16)
            brow = sb("brow", [128, 512])
            wcount = [0]

            def load_blk(src_ap, key, view=None):
                bi = wcount[0] % 2
                wcount[0] += 1
                dst = wblk[bi][:] if view is None else view(wblk[bi])
                P.dma("sp", "wblk%d" % bi, lambda e: e.dma_start(out=dst, in_=src_ap), reads=[key], writes=["wblk%d" % bi])
                return bi

            wv_in = wb_in.rearrange("(c p) n -> p c n", p=128)
            wv_out = wb_out.rearrange("(c p) n -> p c n", p=128)
            wv_glu = wb_glu.rearrange("(c p) n -> p c n", p=128)

            def s5_pair(P, j, T, buf, u2d, ukey, B_unused, inject):
                i, j4 = j // 4, j % 4
                nch_ = T // 64
                P.op("pe", lambda e: e.matmul(out=ps[6][:, 0:T], lhsT=BTr[32 * j4:32 * j4 + 32, i, :], rhs=u2d[32 * j4:32 * j4 + 32, 0:T], start=True, stop=True, tile_position=(32 * j4, 0)),
                     reads=[ukey, "BTr"], writes=[psk[6]])
                P.op("pe", lambda e: e.matmul(out=ps[7][:, 0:T], lhsT=BTi[32 * j4:32 * j4 + 32, i, :], rhs=u2d[32 * j4:32 * j4 + 32, 0:T], start=True, stop=True, tile_position=(32 * j4, 0)),
                     reads=[ukey, "BTi"], writes=[psk[7]])
                cb = cosT[:, j, :].unsqueeze(1).to_broadcast([128, nch_, 64]); sbb = sinT[:, j, :].unsqueeze(1).to_broadcast([128, nch_, 64])
                v3 = lambda t: t[:, 0:T].rearrange("p (c t) -> p c t", t=64)
                bre = ps[6][:, 0:T].rearrange("p (c t) -> p c t", t=64); bim = ps[7][:, 0:T].rearrange("p (c t) -> p c t", t=64)
                P.op("pool", lambda e: e.tensor_copy(out=v3(R0f[0]), in_=R0[:, j, :].unsqueeze(1).to_broadcast([128, nch_, 64])), reads=["R0"], writes=["R0f0"])
                xf = xb_t[:].bitcast(F32)
                tCf = xf[:, 0:512]; tDf = xf[:, 512:1024]
                tC = tCf[:, 0:T].rearrange("p (c t) -> p c t", t=64); tD = tDf[:, 0:T].rearrange("p (c t) -> p c t", t=64)
                P.op("dve", lambda e: e.tensor_tensor(out=v3(tA), in0=bre, in1=cb, op=ALU.mult), reads=[psk[6], "cosT"], writes=["tA0"])
                P.op("dve", lambda e: e.tensor_tensor(out=tC, in0=bim, in1=sbb, op=ALU.mult), reads=[psk[7], "sinT"], writes=["xb_t"])
                P.op("dve", lambda e: e.tensor_tensor(out=v3(tB), in0=bim, in1=cb, op=ALU.mult), reads=[psk[7], "cosT"], writes=["tB0"])
                P.op("dve", lambda e: e.tensor_tensor(out=tD, in0=bre, in1=sbb, op=ALU.mult), reads=[psk[6], "sinT"], writes=["xb_t"])
                P.op("dve", lambda e: e.tensor_tensor(out=wre[0][:, 0:T], in0=tA[:, 0:T], in1=tCf[:, 0:T], op=ALU.add), reads=["tA0", "xb_t"], writes=["wre0"])
                P.op("dve", lambda e: e.tensor_tensor(out=wim[0][:, 0:T], in0=tB[:, 0:T], in1=tDf[:, 0:T], op=ALU.subtract), reads=["tB0", "xb_t"], writes=["wim0"])
                inject(j, 0, v3)
                P.op("dve", lambda e: e.tensor_tensor_scan(out=gre[0][:, 0:T], data0=R0f[0][:, 0:T], data1=wre[0][:, 0:T], initial=0.0, op0=ALU.mult, op1=ALU.add),
                     reads=["R0f0", "wre0"], writes=["gre0"])
                P.op("dve", lambda e: e.tensor_tensor_scan(out=gim[0][:, 0:T], data0=R0f[0][:, 0:T], data1=wim[0][:, 0:T], initial=0.0, op0=ALU.mult, op1=ALU.add),
                     reads=["R0f0", "wim0"], writes=["gim0"])
            g.s5_pair = s5_pair

            def _tile(ti):
                cfg = tile_cfg(ti)
                T, nsub, psub, rows, sample = cfg["T"], cfg["nsub"], cfg["psub"], cfg["rows"], cfg["sample"]
                nch = T // 64
                for sub in range(nsub):
                    xi = sub % 2
                    kx = "xt%d" % xi
                    P.dma("sp", kx, lambda e, xi=xi, sub=sub, cfg=cfg: e.dma_start(out=xt[xi][0:psub, 0, :], in_=cfg["xsrc"](sub)), writes=[kx])
                    norm_to_hT(P, xt[xi], kx, 1, psub, G1, S1, [rows[sub]], hT, "hT", xb_t, "xb_t", st_t, "st_t", (4, 5), sub0=sub)
                if g.b1_stop == 1:
                    return
                for cbk in range(6):
                    bi = load_blk(wv_in[:, :, cbk * 512:(cbk + 1) * 512], "wb_in")
                    which = cbk // 2
                    for sub in range(nsub):
                        pk = 4 + (cbk * nsub + sub) % 4
                        sbi = (cbk * nsub + sub) % 2
                        ks32 = "st32_%d" % sbi
                        P.op("pe", [lambda e, c=c, sub=sub, bi=bi, pk=pk: e.matmul(out=ps[pk][0:psub, :], lhsT=hT[:, c, sub * psub:(sub + 1) * psub], rhs=wblk[bi][:, c, :], start=(c == 0), stop=(c == 15)) for c in range(16)],
                             reads=["hT", "wblk%d" % bi], writes=[psk[pk]])
                        if not sample:
                            dcol_ = stg[which][0:psub, sub, (cbk % 2) * 512:(cbk % 2 + 1) * 512]
                            P.op("act", lambda e, pk=pk, dcol_=dcol_: e.copy(out=dcol_, in_=ps[pk][0:psub, :]), reads=[psk[pk]], writes=["stg%d" % which])
                            if which < 2:
                                cs_ap, sn_ap = cfg["cs"](sub)
                                rope(P, ps[pk][0:psub, :].rearrange("p (h d) -> p h d", d=64), psk[pk], 8, psub, cs_ap[0:psub, :], sn_ap[0:psub, :], rtmp, "rtmp",
                                     dst=dcol_.rearrange("p (h d) -> p h d", d=64), kdst="stg%d" % which)
                            continue
                        P.op("act", lambda e, pk=pk, sbi=sbi: e.copy(out=st32[sbi][0:psub, :], in_=ps[pk][0:psub, :]), reads=[psk[pk]], writes=[ks32])
                        if which < 2:
                            cs_ap, sn_ap = cfg["cs"](sub)
                            rope(P, st32[sbi][0:psub, :].rearrange("p (h d) -> p h d", d=64), ks32, 8, psub, cs_ap[0:psub, :], sn_ap[0:psub, :], rtmp, "rtmp")
                        P.op("pool", lambda e, sbi=sbi, sub=sub, which=which, cbk=cbk: e.tensor_copy(out=stg[which][0:psub, sub, (cbk % 2) * 512:(cbk % 2 + 1) * 512], in_=st32[sbi][0:psub, :]),
                             reads=[ks32], writes=["stg%d" % which])
                        if sample and which >= 1:
                            dst = nks if which == 1 else nvs
                            P.dma("sp", "nksv%d" % sbi, lambda e, dst=dst, sub=sub, cbk=cbk, sbi=sbi: e.dma_start(out=dst[sub * 64:(sub + 1) * 64, (cbk % 2) * 512:(cbk % 2 + 1) * 512], in_=st32[sbi][0:64, :]),
                                  reads=[ks32], writes=["nksv%d" % which])
                if g.b1_stop == 2:
                    return
                for (src_i, dstT, kd) in ((0, QT, "QT"), (1, KTo, "KTo")):
                    for sub in range(nsub):
                        pk = 4 + sub % 2
                        pv = ps[pk][:].bitcast(BF16)
                        P.op("pe", [lambda e, h=h, pv=pv, sub=sub, src_i=src_i: e.transpose(out=pv[:, h * psub:(h + 1) * psub], in_=stg[src_i][0:psub, sub, h * 128:(h + 1) * 128], identity=ident[0:psub, 0:psub]) for h in range(8)],
                             reads=["stg%d" % src_i, "ident"], writes=[psk[pk]])
                        P.op("dve", lambda e, sub=sub, pv=pv, dstT=dstT: e.tensor_copy(out=dstT[:, :, sub * psub:(sub + 1) * psub], in_=pv[:, 0:8 * psub].rearrange("p (h t) -> p h t", t=psub)), reads=[psk[pk]], writes=[kd])
                if g.b1_stop == 3:
                    return
                if not sample:
                    s_ = cfg["s"]
                    for (HS, Hn, kn) in ((HSr, Hnr, "Hnr"), (HSi, Hni, "Hni")):
                        P.dma("sp", "Ha", lambda e, HS=HS, s_=s_: e.dma_start(out=Ha[:], in_=HS[2 * s_]), reads=["HSr"], writes=["Ha"])
                        P.dma("sp", "Hb", lambda e, HS=HS, s_=s_: e.dma_start(out=Hb[:], in_=HS[2 * s_ + 1]), reads=["HSr"], writes=["Hb"])
                        P.op("dve", lambda e, Hn=Hn: e.tensor_scalar(out=Hn[:], in0=Ha[:], scalar1=flg[:, 2:3], scalar2=None, op0=ALU.mult), reads=["Ha", "flg"], writes=[kn])
                        P.op("dve", lambda e, Hn=Hn: e.scalar_tensor_tensor(out=Hn[:], in0=Hb[:], scalar=flg[:, 0:1], in1=Hn[:], op0=ALU.mult, op1=ALU.add), reads=["Hb", "flg", kn], writes=[kn])
                else:
                    for c_ in range(4):
                        for q_ in range(4):
                            P.dma("sp", "Hnr", lambda e, c_=c_, q_=q_: e.dma_start(out=Hnr[:, 8 * q_:8 * q_ + 8, c_], in_=sre0[c_].rearrange("(j gl) p -> (gl p) j", gl=2)[:, 8 * q_:8 * q_ + 8], allow_slow_non_contiguous=True), writes=["Hnr"])
                            P.dma("sp", "Hni", lambda e, c_=c_, q_=q_: e.dma_start(out=Hni[:, 8 * q_:8 * q_ + 8, c_], in_=sim0[c_].rearrange("(j gl) p -> (gl p) j", gl=2)[:, 8 * q_:8 * q_ + 8], allow_slow_non_contiguous=True), writes=["Hni"])

                def inject(j, buf, v3, nch=nch):
                    P.op("dve", lambda e: e.scalar_tensor_tensor(out=v3(wre[buf])[:, :, 0], in0=Hnr[:, j, 0:nch], scalar=rcol[:, j:j + 1], in1=v3(wre[buf])[:, :, 0], op0=ALU.mult, op1=ALU.add),
                         reads=["Hnr", "rcol", "wre0"], writes=["wre0"])
                    P.op("dve", lambda e: e.scalar_tensor_tensor(out=v3(wim[buf])[:, :, 0], in0=Hni[:, j, 0:nch], scalar=rcol[:, j:j + 1], in1=v3(wim[buf])[:, :, 0], op0=ALU.mult, op1=ALU.add),
                         reads=["Hni", "rcol", "wim0"], writes=["wim0"])

                if g.b1_stop == 4:
                    return
                for ub in range(2):
                    bi = load_blk(wv_in[:, :, 3072 + ub * 512: 3072 + (ub + 1) * 512], "wb_in")
                    for ii in range(4):
                        i = ub * 4 + ii
                        ib = i % 2
                        P.op("pe", [lambda e, c=c, ii=ii, bi=bi: e.matmul(out=ps[4][:, 0:T], lhsT=wblk[bi][:, c, ii * 128:(ii + 1) * 128], rhs=hT[:, c, 0:T], start=(c == 0), stop=(c == 15)) for c in range(16)],
                             reads=["hT", "wblk%d" % bi], writes=[psk[4]])
                        P.op("act", lambda e, ib=ib: e.copy(out=uTb[ib][:, 0:T], in_=ps[4][:, 0:T]), reads=[psk[4]], writes=["uTb%d" % ib])
                        P.op("dve", lambda e, ib=ib: e.tensor_copy(out=uTf[ib][:, 0:T], in_=ps[4][:, 0:T]), reads=[psk[4], "uTb%d" % ib], writes=["uTf%d" % ib])
                        for j4 in range(4):
                            j = 4 * i + j4
                            buf = j % 2
                            kb = str(buf)
                            g.s5_pair(P, j, T, buf, uTb[ib], "uTb%d" % ib, dict(wre=wre, wim=wim, tA=tA, tB=tB, R0f=R0f, gre=gre, gim=gim), inject)
                            v3 = lambda t_: t_[:, 0:T].rearrange("p (c t) -> p c t", t=64)
                            cb_ = cosT[:, j, :].unsqueeze(1).to_broadcast([128, nch, 64]); sb_ = sinT[:, j, :].unsqueeze(1).to_broadcast([128, nch, 64])
                            if sample:
                                P.op("pool", lambda e, j=j, buf=buf: e.tensor_copy(out=Lt0[:, j, :], in_=v3(gre[buf])[:, :, 63]), reads=["gre0"], writes=["Lt0"])
                                P.op("pool", lambda e, j=j, buf=buf: e.tensor_copy(out=Lt1[:, j, :], in_=v3(gim[buf])[:, :, 63]), reads=["gim0"], writes=["Lt1"])
                            xf_ = xb_t[:].bitcast(F32)
                            tC_ = xf_[:, 0:T].rearrange("p (c t) -> p c t", t=64); tD_ = xf_[:, 512:512 + T].rearrange("p (c t) -> p c t", t=64)
                            P.op("dve", lambda e, buf=buf, cb_=cb_: e.tensor_tensor(out=v3(tA), in0=v3(gre[buf]), in1=cb_, op=ALU.mult), reads=["gre0", "cosT"], writes=["tA0"])
                            P.op("dve", lambda e, buf=buf, sb_=sb_, tC_=tC_: e.tensor_tensor(out=tC_, in0=v3(gim[buf]), in1=sb_, op=ALU.mult), reads=["gim0", "sinT"], writes=["xb_t"])
                            P.op("dve", lambda e, buf=buf, cb_=cb_: e.tensor_tensor(out=v3(tB), in0=v3(gim[buf]), in1=cb_, op=ALU.mult), reads=["gim0", "cosT"], writes=["tB0"])
                            P.op("dve", lambda e, buf=buf, sb_=sb_, tD_=tD_: e.tensor_tensor(out=tD_, in0=v3(gre[buf]), in1=sb_, op=ALU.mult), reads=["gre0", "sinT"], writes=["xb_t"])
                            P.op("dve", lambda e, buf=buf: e.tensor_tensor(out=hbr[buf][:, 0:T], in0=tA[:, 0:T], in1=xf_[:, 0:T], op=ALU.subtract), reads=["tA0", "xb_t"], writes=["hbr0"])
                            P.op("dve", lambda e, buf=buf: e.tensor_tensor(out=hbi[buf][:, 0:T], in0=tB[:, 0:T], in1=xf_[:, 512:512 + T], op=ALU.add), reads=["tB0", "xb_t"], writes=["hbi0"])
                            P.op("pe", [lambda e, buf=buf, i=i, j4=j4: e.matmul(out=ps[5][32 * j4:32 * j4 + 32, 0:T], lhsT=CTr[:, i, 32 * j4:32 * j4 + 32], rhs=hbr[buf][:, 0:T], start=True, stop=False, tile_position=(0, 32 * j4)),
                                        lambda e, buf=buf, i=i, j4=j4: e.matmul(out=ps[5][32 * j4:32 * j4 + 32, 0:T], lhsT=CTi[:, i, 32 * j4:32 * j4 + 32], rhs=hbi[buf][:, 0:T], start=False, stop=True, tile_position=(0, 32 * j4))],
                                 reads=["hbr0", "hbi0", "CTr", "CTi"], writes=[psk[5]])
                        P.op("dve", lambda e, ib=ib, i=i: e.scalar_tensor_tensor(out=n0[:, 0:T], in0=uTf[ib][:, 0:T], scalar=dcol[:, i:i + 1], in1=ps[5][:, 0:T], op0=ALU.mult, op1=ALU.add),
                             reads=["uTf%d" % ib, psk[5], "dcol"], writes=["n0"])
                        P.op("dve", lambda e: e.tensor_tensor(out=n1[:, 0:T], in0=n0[:, 0:T], in1=n0[:, 0:T], op=ALU.mult), reads=["n0"], writes=["n1"])
                        P.op("dve", lambda e: e.tensor_scalar(out=n1[:, 0:T], in0=n1[:, 0:T], scalar1=0.044715, scalar2=1.0, op0=ALU.mult, op1=ALU.add), reads=["n1"], writes=["n1"])
                        P.op("dve", lambda e: e.tensor_tensor(out=n1[:, 0:T], in0=n1[:, 0:T], in1=n0[:, 0:T], op=ALU.mult), reads=["n1", "n0"], writes=["n1"])
                        P.op("act", lambda e: e.activation(out=n1[:, 0:T], in_=n1[:, 0:T], func=AF.Sigmoid, scale=2.0 * math.sqrt(2.0 / math.pi)), reads=["n1"], writes=["n1"])
                        P.op("dve", lambda e, i=i: e.tensor_tensor(out=zT[:, i, 0:T], in0=n1[:, 0:T], in1=n0[:, 0:T], op=ALU.mult), reads=["n1", "n0"], writes=["stg0"])
                if sample:
                    e64r = E64r[:].unsqueeze(2).to_broadcast([128, 32, 4]); e64i = E64i[:].unsqueeze(2).to_broadcast([128, 32, 4])
                    g.cmul(P, Ler[:], Lei[:], Lt0[:], Lt1[:], e64r, e64i, Ha[:, :, 0:4], Hb[:, :, 0:4], ["Lt0", "Lt1", "E64r", "E64i"], "Ler", "Lei", "Ha", "Hb")
                    for c_ in range(4):
                        for q_ in range(4):
                            P.dma("sp", "sre", lambda e, c_=c_, q_=q_: e.dma_start(out=sre[c_].rearrange("(j gl) p -> (gl p) j", gl=2)[:, 8 * q_:8 * q_ + 8], in_=Ler[:, 8 * q_:8 * q_ + 8, c_], allow_slow_non_contiguous=True), reads=["Ler"], writes=["sre"])
                            P.dma("sp", "sim", lambda e, c_=c_, q_=q_: e.dma_start(out=sim[c_].rearrange("(j gl) p -> (gl p) j", gl=2)[:, 8 * q_:8 * q_ + 8], in_=Lei[:, 8 * q_:8 * q_ + 8, c_], allow_slow_non_contiguous=True), reads=["Lei"], writes=["sim"])

                if g.b1_stop == 5:
                    return
                acount = [0]

                def attn_p1(h, kt_ap, kkey, v_ap, vkey, nk_, q0, q1, first, bias, zero_rect, last=False):
                    sb_i = acount[0] % 2
                    acount[0] += 1
                    for comp in range(2):
                        pk = 2 * sb_i + comp
                        P.op("pe", lambda e, comp=comp, pk=pk: e.matmul(out=ps[pk][0:nk_, q0:q1], lhsT=kt_ap[64 * comp:64 * comp + 64, :], rhs=QT[64 * comp:64 * comp + 64, h, q0:q1], start=True, stop=True),
                             reads=[kkey, "QT"], writes=[psk[pk]])
                        pkey = "P%d_%d" % (comp, sb_i)
                        P.op("act", lambda e, comp=comp, pk=pk: e.activation(out=Pb[comp][sb_i][0:nk_, q0:q1], in_=ps[pk][0:nk_, q0:q1], func=AF.Exp, bias=bias, scale=0.125),
                             reads=[psk[pk], "flg"], writes=[pkey])
                        if zero_rect is not None:
                            P.op("pool", lambda e, comp=comp: e.memset(Pb[comp][sb_i][64:128, zero_rect[0]:zero_rect[1]], 0.0), reads=[pkey], writes=[pkey])
                    return (sb_i, v_ap, vkey, nk_, q0, q1, first, last)

                def attn_p2(h, ctx):
                    sb_i, v_ap, vkey, nk_, q0, q1, first, last = ctx
                    for comp in range(2):
                        pkey = "P%d_%d" % (comp, sb_i)
                        P.op("pe", [lambda e, comp=comp: e.matmul(out=ps[4 + comp][:, q0:q1], lhsT=v_ap, rhs=Pb[comp][sb_i][0:nk_, q0:q1], start=first, stop=last),
                                    lambda e, comp=comp: e.matmul(out=ps[6 + comp][:, q0:q1], lhsT=ones_b[0:nk_, :], rhs=Pb[comp][sb_i][0:nk_, q0:q1], start=first, stop=last)],
                             reads=[pkey, vkey, "ones_b"], writes=[psk[4 + comp], psk[6 + comp]])

                def attn_run(h, blocks):
                    pending = None
                    for blk in blocks:
                        ctx = attn_p1(h, *blk())
                        if pending is not None:
                            attn_p2(h, pending)
                        pending = ctx
                    attn_p2(h, pending)

                def attn_finish(h, q0, q1):
                    w = slice(q0, q1)
                    P.op("dve", lambda e: e.reciprocal(out=n0[:, w], in_=ps[6][:, w]), reads=[psk[6]], writes=["n0"])
                    P.op("dve", lambda e: e.tensor_tensor(out=n0[:, w], in0=n0[:, w], in1=ps[4][:, w], op=ALU.mult), reads=["n0", psk[4]], writes=["n0"])
                    P.op("dve", lambda e: e.reciprocal(out=n1[:, w], in_=ps[7][:, w]), reads=[psk[7]], writes=["n1"])
                    P.op("dve", lambda e: e.tensor_tensor(out=n1[:, w], in0=n1[:, w], in1=ps[5][:, w], op=ALU.mult), reads=["n1", psk[5]], writes=["n1"])
                    P.op("dve", lambda e: e.scalar_tensor_tensor(out=n0[:, w], in0=n1[:, w], scalar=lamc[:, 0:1], in1=n0[:, w], op0=ALU.mult, op1=ALU.add), reads=["n1", "n0", "lamc"], writes=["n0"])
                    P.op("pool", lambda e: e.tensor_tensor(out=n2b[:, w], in0=n0[:, w], in1=n0[:, w], op=ALU.mult), reads=["n0"], writes=["n2b"])
                    P.op("pe", lambda e: e.matmul(out=ps[6][:, w], lhsT=ones_b[:], rhs=n2b[:, w], start=True, stop=True), reads=["n2b", "ones_b"], writes=[psk[6]])
                    P.op("act", lambda e: e.activation(out=n1[:, w], in_=ps[6][:, w], func=AF.Sqrt, bias=lamc[:, 1:2], scale=1.0 / 128.0), reads=[psk[6], "lamc"], writes=["n1"])
                    P.op("dve", lambda e: e.reciprocal(out=n1[:, w], in_=n1[:, w]), reads=["n1"], writes=["n1"])
                    P.op("dve", lambda e: e.tensor_tensor(out=n0[:, w], in0=n0[:, w], in1=n1[:, w], op=ALU.mult), reads=["n0", "n1"], writes=["n0"])
                    P.op("dve", lambda e: e.tensor_scalar(out=catT[:, h, w], in0=n0[:, w], scalar1=gsub[:, 0:1], scalar2=1.0 - LAM_INIT, op0=ALU.mult, op1=ALU.mult), reads=["n0", "gsub"], writes=["hT"])

                lcount = [0]
                loaded = {}

                def load_kv(kt_src, v_src, kkey, vkey):
                    bi = lcount[0] % 2
                    lcount[0] += 1
                    P.dma("sp", "KTb%d" % bi, lambda e: e.dma_start(out=KTb[bi][:], in_=kt_src), reads=[kkey], writes=["KTb%d" % bi])
                    P.dma("sp", "Vb%d" % bi, lambda e: e.dma_start(out=Vb[bi][:], in_=v_src.rearrange("(kb p) d -> p kb d", p=128)), reads=[vkey], writes=["Vb%d" % bi])
                    return bi

                for h in range(8):
                    if not sample:
                        s_ = cfg["s"]
                        blocks = []
                        for gt_ in range(2 * s_ + 1):
                            for kb_ in range(4):
                                def mk(gt_=gt_, kb_=kb_, h=h, s_=s_, st={}):
                                    if kb_ == 0:
                                        loaded[(h, gt_)] = load_kv(KT[h, :, gt_ * 512:(gt_ + 1) * 512], VS[gt_ * 512:(gt_ + 1) * 512, h * 128:(h + 1) * 128], "KT", "VS")
                                    bi = loaded[(h, gt_)]
                                    bias = flg[:, 1:2] if gt_ == 2 * s_ else 0.0
                                    return (KTb[bi][:, kb_ * 128:(kb_ + 1) * 128], "KTb%d" % bi, Vb[bi][:, kb_, :], "Vb%d" % bi, 128, 0, 512, (gt_ == 0 and kb_ == 0), bias, None, False)
                                blocks.append(mk)
                        for kb_ in (3, 2, 1, 0):
                            blocks.append(lambda kb_=kb_, h=h: (KTo[:, h, kb_ * 128:(kb_ + 1) * 128], "KTo", stg[2][:, kb_, h * 128:(h + 1) * 128], "stg2", 128, 128 * kb_, 512, False, 0.0, (128 * kb_, 128 * kb_ + 64), kb_ == 0))
                        attn_run(h, blocks)
                        attn_finish(h, 0, 512)
                    else:
                        for b_ in range(4):
                            q0, q1 = 64 * b_, 64 * b_ + 64
                            blocks = []
                            for gt_ in range(4):
                                for kb_ in range(4):
                                    def mk(gt_=gt_, kb_=kb_, h=h, b_=b_, q0=q0, q1=q1):
                                        if kb_ == 0:
                                            loaded[(h, b_, gt_)] = load_kv(KTS[b_, h, :, gt_ * 512:(gt_ + 1) * 512], VSS[b_, gt_ * 512:(gt_ + 1) * 512, h * 128:(h + 1) * 128], "KTS", "VSS")
                                        bi = loaded[(h, b_, gt_)]
                                        return (KTb[bi][:, kb_ * 128:(kb_ + 1) * 128], "KTb%d" % bi, Vb[bi][:, kb_, :], "Vb%d" % bi, 128, q0, q1, (gt_ == 0 and kb_ == 0), 0.0, None, False)
                                    blocks.append(mk)
                            blocks.append(lambda h=h, b_=b_, q0=q0, q1=q1: (KTo[:, h, q0:q1], "KTo", stg[2][0:64, b_, h * 128:(h + 1) * 128], "stg2", 64, q0, q1, False, 0.0, None, True))
                            attn_run(h, blocks)
                        attn_finish(h, 0, 256)

                if g.b1_stop == 6:
                    return
                for half in range(2):
                    b1 = load_blk(wv_glu[:, :, half * 512:(half + 1) * 512], "wb_glu", view=lambda w_: w_[:, 0:8, :])
                    b2 = load_blk(wv_glu[:, :, 1024 + half * 512:1024 + (half + 1) * 512], "wb_glu", view=lambda w_: w_[:, 0:8, :])
                    for ii in range(4):
                        i = half * 4 + ii
                        P.op("pe", [lambda e, c=c, ii=ii, b1=b1: e.matmul(out=ps[0][:, 0:T], lhsT=wblk[b1][:, c, ii * 128:(ii + 1) * 128], rhs=zT[:, c, 0:T], start=(c == 0), stop=(c == 7)) for c in range(8)],
                             reads=["stg0", "wblk%d" % b1], writes=[psk[0]])
                        P.op("pe", [lambda e, c=c, ii=ii, b2=b2: e.matmul(out=ps[1][:, 0:T], lhsT=wblk[b2][:, c, ii * 128:(ii + 1) * 128], rhs=zT[:, c, 0:T], start=(c == 0), stop=(c == 7)) for c in range(8)],
                             reads=["stg0", "wblk%d" % b2], writes=[psk[1]])
                        P.op("act", lambda e, i=i: e.activation(out=n1[:, 0:T], in_=ps[1][:, 0:T], func=AF.Sigmoid, bias=bglu[:, 8 + i:9 + i], scale=1.0), reads=[psk[1], "bglu"], writes=["n1"])
                        P.op("dve", lambda e, i=i: e.scalar_tensor_tensor(out=catT[:, 8 + i, 0:T], in0=ps[0][:, 0:T], scalar=bglu[:, i:i + 1], in1=n1[:, 0:T], op0=ALU.add, op1=ALU.mult),
                             reads=[psk[0], "n1", "bglu"], writes=["hT"])

                if g.b1_stop == 7:
                    return
                cur_brow = [None]
                for pr in range(nsub // 2):
                    subs = (2 * pr, 2 * pr + 1)
                    for sub in subs:
                        xi = sub % 2
                        P.dma("sp", "xt%d" % xi, lambda e, xi=xi, sub=sub, cfg=cfg: e.dma_start(out=xt[xi][0:psub, 0, :], in_=cfg["xsrc"](sub)), writes=["xt%d" % xi])
                    for cb in range(4):
                        bi = load_blk(wv_out[:, :, cb * 512:(cb + 1) * 512], "wb_out")
                        for sub in subs:
                            xi = sub % 2
                            kx = "xt%d" % xi
                            if cur_brow[0] != (rows[sub], cb):
                                cur_brow[0] = (rows[sub], cb)
                                P.dma("sp", "brow", lambda e, r=rows[sub], cb=cb: e.dma_start(out=brow[:], in_=mrow[r:r + 1, 2 * D + cb * 512:2 * D + (cb + 1) * 512].partition_broadcast(128)), reads=["mrow"], writes=["brow"])
                            pk = sub % 2
                            P.op("pe", [lambda e, c=c, sub=sub, bi=bi, pk=pk: e.matmul(out=ps[pk][0:psub, :], lhsT=catT[:, c, sub * psub:(sub + 1) * psub], rhs=wblk[bi][:, c, :], start=(c == 0), stop=(c == 15)) for c in range(16)],
                                 reads=["hT", "wblk%d" % bi], writes=[psk[pk]])
                            P.op("dve", lambda e, pk=pk: e.tensor_tensor(out=n0[0:psub, :], in0=ps[pk][0:psub, :], in1=brow[0:psub, :], op=ALU.mult), reads=[psk[pk], "brow"], writes=["n0"])
                            P.op("pool", lambda e, xi=xi, cb=cb: e.tensor_tensor(out=xt[xi][0:psub, 0, cb * 512:(cb + 1) * 512], in0=xt[xi][0:psub, 0, cb * 512:(cb + 1) * 512], in1=n0[0:psub, :], op=ALU.add), reads=["n0", kx], writes=[kx])
                    for sub in subs:
                        xi = sub % 2
                        P.dma("sp", "x1s%d" % xi, lambda e, xi=xi, sub=sub, cfg=cfg: e.dma_start(out=cfg["x1"](sub), in_=xt[xi][0:psub, 0, :]), reads=["xt%d" % xi], writes=["x1s"])
            for ti_ in g.b1_tiles:
                _tile(ti_)
            P.emit()
        if stop_after.startswith("bB1"):
            return nc

        with contextlib.ExitStack() as sc:
            def sb(name, shape, dt=F32):
                return sc.enter_context(nc.sbuf_tensor("b2_" + name, list(shape), dt))
            P = Prog(nc)
            for k_ in ("wb_ff1", "wb_ff2", "x1s", "mrow"):
                P.buf[k_] = [None, []]
            xt = sb("xt", [128, 4, D]); xb_t = sb("xb_t", [128, D], BF16); st_t = sb("st_t", [128, 4])
            hT = sb("hT", [128, 16, TT], BF16)
            wblk = [sb("wblk%d" % i, [128, 16, 512], BF16) for i in range(2)]
            aT = sb("aT", [128, 64, TT], BF16)
            n0 = sb("n0", [128, TT]); brow = sb("brow", [128, 1024]); st_f = sb("st_f", [128, 12])
            wv1 = wb_ff1.rearrange("(c p) n -> p c n", p=128)
            wv2 = wb_ff2.rearrange("(g k p) n -> g p k n", p=128, k=8)
            wcount = [0]

            def load_blk2(src_ap, key, view=None):
                bi = wcount[0] % 2
                wcount[0] += 1
                dst = wblk[bi][:] if view is None else view(wblk[bi])
                P.dma("sp", "wblk%d" % bi, lambda e: e.dma_start(out=dst, in_=src_ap), reads=[key], writes=["wblk%d" % bi])
                return bi

            def _tile(ti):
                cfg = tile_cfg(ti)
                T, nsub, psub, rows, sample = cfg["T"], cfg["nsub"], cfg["psub"], cfg["rows"], cfg["sample"]
                for sub in range(nsub):
                    P.dma("sp", "xt", lambda e, sub=sub, cfg=cfg: e.dma_start(out=xt[0:psub, sub, :], in_=cfg["x1"](sub)), reads=["x1s"], writes=["xt"])
                norm_to_hT(P, xt, "xt", nsub, psub, G2, S2, rows, hT, "hT", xb_t, "xb_t", st_t, "st_t", (4, 5))
                for blk in range(16):
                    bi = load_blk2(wv1[:, :, blk * 512:(blk + 1) * 512], "wb_ff1")
                    for ii in range(4):
                        j = blk * 4 + ii
                        pk = ii
                        P.op("pe", [lambda e, c=c, ii=ii, bi=bi, pk=pk: e.matmul(out=ps[pk][:, 0:T], lhsT=wblk[bi][:, c, ii * 128:(ii + 1) * 128], rhs=hT[:, c, 0:T], start=(c == 0), stop=(c == 15)) for c in range(16)],
                             reads=["hT", "wblk%d" % bi], writes=[psk[pk]])
                        P.op("act", lambda e, pk=pk: e.activation(out=n0[:, 0:T], in_=ps[pk][:, 0:T], func=AF.Relu), reads=[psk[pk]], writes=["n0"])
                        P.op("dve", lambda e, j=j: e.tensor_tensor(out=aT[:, j, 0:T], in0=n0[:, 0:T], in1=n0[:, 0:T], op=ALU.mult), reads=["n0"], writes=["aT"])
                cur_row = [None]
                for rnd in range(2):
                    for kg in range(8):
                        bi = load_blk2(wv2[kg][:, :, rnd * 1024:(rnd + 1) * 1024], "wb_ff2", view=lambda w_: w_[:].rearrange("p c n -> p (c n)").rearrange("p (k n) -> p k n", k=8))
                        wvw = wblk[bi][:].rearrange("p c n -> p (c n)").rearrange("p (k n) -> p k n", k=8)
                        fns = []
                        for kk in range(8):
                            k = kg * 8 + kk
                            for sub in range(nsub):
                                for cb in range(2):
                                    fns.append(lambda e, k=k, kk=kk, sub=sub, cb=cb, wvw=wvw: e.matmul(out=ps[sub * 2 + cb][0:psub, :], lhsT=aT[:, k, sub * psub:(sub + 1) * psub], rhs=wvw[:, kk, cb * 512:(cb + 1) * 512], start=(k == 0), stop=(k == 63)))
                        P.op("pe", fns, reads=["aT", "wblk%d" % bi], writes=[psk[i_] for i_ in range(8)])
                    for sub in range(nsub):
                        if cur_row[0] != (rows[sub], rnd):
                            cur_row[0] = (rows[sub], rnd)
                            P.dma("sp", "brow", lambda e, r=rows[sub], rnd=rnd: e.dma_start(out=brow[:], in_=mrow[r:r + 1, 5 * D + rnd * 1024:5 * D + (rnd + 1) * 1024].partition_broadcast(128)), reads=["mrow"], writes=["brow"])
                        for cb in range(2):
                            col = rnd * 1024 + cb * 512
                            P.op("dve", lambda e, sub=sub, cb=cb, col=col: e.tensor_tensor(out=n0[0:psub, :], in0=ps[sub * 2 + cb][0:psub, :], in1=brow[0:psub, cb * 512:(cb + 1) * 512], op=ALU.mult), reads=[psk[sub * 2 + cb], "brow"], writes=["n0"])
                            P.op("pool", lambda e, sub=sub, col=col: e.tensor_tensor(out=xt[0:psub, sub, col:col + 512], in0=xt[0:psub, sub, col:col + 512], in1=n0[0:psub, :], op=ALU.add), reads=["n0", "xt"], writes=["xt"])
                for sub in range(nsub):
                    xv = xt[0:psub, sub, :]
                    o3 = 3 * sub
                    P.op("dve", lambda e, o3=o3: e.memset(st_f[0:psub, o3:o3 + 1], 0.0), writes=["st_f"])
                    P.op("act", lambda e, xv=xv, o3=o3: e.activation(out=xb_t[0:psub, :], in_=xv, func=AF.Square, accum_out=st_f[0:psub, o3:o3 + 1]), reads=["xt", "st_f"], writes=["xb_t", "st_f"])
                    P.op("act", lambda e, o3=o3: e.activation(out=st_f[0:psub, o3 + 1:o3 + 2], in_=st_f[0:psub, o3:o3 + 1], func=AF.Sqrt, bias=lamc[0:psub, 1:2], scale=1.0 / D), reads=["st_f", "lamc"], writes=["st_f"])
                    P.op("dve", lambda e, o3=o3: e.reciprocal(out=st_f[0:psub, o3 + 2:o3 + 3], in_=st_f[0:psub, o3 + 1:o3 + 2]), reads=["st_f"], writes=["st_f"])
                for half in range(2):
                    P.dma("sp", "brow", lambda e, half=half: e.dma_start(out=brow[:], in_=g_final[:, half * 1024:(half + 1) * 1024].partition_broadcast(128)), writes=["brow"])
                    for sub in range(nsub):
                        xh = xt[0:psub, sub, half * 1024:(half + 1) * 1024]
                        P.op("dve", lambda e, xh=xh, sub=sub: e.scalar_tensor_tensor(out=xh, in0=xh, scalar=st_f[0:psub, 3 * sub + 2:3 * sub + 3], in1=brow[0:psub, :], op0=ALU.mult, op1=ALU.mult), reads=["xt", "st_f", "brow"], writes=["xt"])
                for sub in range(nsub):
                    P.dma("sp", "yout", lambda e, sub=sub, cfg=cfg: e.dma_start(out=cfg["ydst"](sub), in_=xt[0:psub, sub, :]), reads=["xt"], writes=["yout"])
            for ti_ in range(NTILES):
                _tile(ti_)
            P.emit()
    return nc


def _prep_inputs(inp):
    f32 = lambda a: np.ascontiguousarray(np.asarray(a, dtype=np.float32))
    xp = f32(inp["x_prompt"]); xsm = f32(inp["x_sample"])
    cp = f32(inp["c_prompt"]); csm = f32(inp["c_sample"])
    ckk = f32(inp["cache_k"])[0].reshape(32, 2048, 1024); cvv = f32(inp["cache_v"])[0].reshape(32, 2048, 1024)
    sr = f32(inp["state_ssm_re"])[0]; si = f32(inp["state_ssm_im"])[0]
    shared = {
        "w_ada": f32(inp["w_ada"])[0], "b_ada": f32(inp["b_ada"]).reshape(1, -1), "g_mix": f32(inp["g_mix"]).reshape(1, -1),
        "w_in": f32(inp["w_in"])[0],
        "lq1": f32(inp["lam_q1"]).reshape(1, 64), "lk1": f32(inp["lam_k1"]).reshape(1, 64),
        "lq2": f32(inp["lam_q2"]).reshape(1, 64), "lk2": f32(inp["lam_k2"]).reshape(1, 64),
        "g_subln": f32(inp["g_subln"]).reshape(1, 128),
        "lam_re": f32(inp["ssm_lam_re"])[0], "lam_im": f32(inp["ssm_lam_im"])[0], "log_dt": f32(inp["ssm_log_dt"]).reshape(1, 64),
        "b_re": f32(inp["ssm_b_re"])[0], "b_im": f32(inp["ssm_b_im"])[0], "c_re": f32(inp["ssm_c_re"])[0], "c_im": f32(inp["ssm_c_im"])[0],
        "ssm_d": f32(inp["ssm_d"]).reshape(1, 1024), "w_glu": f32(inp["w_glu"])[0], "b_glu": f32(inp["b_glu"]).reshape(1, 2048),
        "w_out": f32(inp["w_out"])[0], "g_ffn": f32(inp["g_ffn"]).reshape(1, -1), "w_ff1": f32(inp["w_ff1"])[0], "w_ff2": f32(inp["w_ff2"])[0],
        "g_final": f32(inp["g_final"]).reshape(1, -1),
    }
    r = np.arange(128)
    bmask = ((r[:, None] % 32) // 16 == (r[None, :] // 64)).astype(np.float32)
    shared["bmask"] = bmask
    maps = []
    for c in range(8):
        b, par = c // 2, c % 2
        xa = xp[b]
        xo = np.ascontiguousarray(xa.reshape(16, 512, D)[par::2].reshape(4096, D))
        fl = np.zeros((128, 4), np.float32)
        fl[:, 0] = par; fl[:, 1] = 0.0 if par == 1 else -30000.0; fl[:, 2] = 1 - par
        m = dict(shared)
        m.update({
            "xa": xa, "xo": xo, "xs": np.ascontiguousarray(xsm[4 * c:4 * c + 4].reshape(256, D)),
            "cvec": np.ascontiguousarray(np.concatenate([cp[b:b + 1], csm[4 * c:4 * c + 4]], 0)),
            "ck": np.ascontiguousarray(ckk[4 * c:4 * c + 4]), "cv": np.ascontiguousarray(cvv[4 * c:4 * c + 4]),
            "sre0": np.ascontiguousarray(sr[4 * c:4 * c + 4]), "sim0": np.ascontiguousarray(si[4 * c:4 * c + 4]),
            "flags": fl,
        })
        maps.append(m)
    return maps


def _assemble(results):
    y_prompt = np.zeros((4, SEQ, D), np.float32); y_sample = np.zeros((32, 64, D), np.float32)
    nkp = np.zeros((1, 4, SEQ, 8, 2, 64), np.float32); nvp = np.zeros((1, 4, SEQ, 8, 128), np.float32)
    srp = np.zeros((1, 4, 64, 64), np.float32); sip = np.zeros((1, 4, 64, 64), np.float32)
    nks = np.zeros((1, 32, 64, 8, 2, 64), np.float32); nvs = np.zeros((1, 32, 64, 8, 128), np.float32)
    srs = np.zeros((1, 32, 64, 64), np.float32); sis = np.zeros((1, 32, 64, 64), np.float32)
    for c in range(8):
        r = results[c]
        b, par = c // 2, c % 2
        y_prompt[b].reshape(16, 512, D)[par::2] = np.asarray(r["yo"]).reshape(8, 512, D)
        y_sample[4 * c:4 * c + 4] = np.asarray(r["ys"]).reshape(4, 64, D)
        if par == 0:
            nkp[0, b] = np.asarray(r["nk"]).reshape(SEQ, 8, 2, 64); nvp[0, b] = np.asarray(r["nv"]).reshape(SEQ, 8, 128)
            srp[0, b] = np.asarray(r["hre"]); sip[0, b] = np.asarray(r["him"])
        nks[0, 4 * c:4 * c + 4] = np.asarray(r["nks"]).reshape(4, 64, 8, 2, 64); nvs[0, 4 * c:4 * c + 4] = np.asarray(r["nvs"]).reshape(4, 64, 8, 128)
        srs[0, 4 * c:4 * c + 4] = np.asarray(r["sre"]); sis[0, 4 * c:4 * c + 4] = np.asarray(r["sim"])
    return (y_prompt, y_sample, nkp, nvp, srp, sip, nks, nvs, srs, sis)


def kernel(**inputs):
    maps = _prep_inputs(inputs)
    nc = build_nc()
    res = run_bass_kernel_spmd(nc, maps, core_ids=list(range(8)))
    return _assemble(res.results)
```
